# Optimizing a Trainium2 kernel written in Bass

```python
import jax, jax.numpy as jnp
from jax import lax
import numpy as np

D_MODEL = 1024
BATCH = 8
SEQ = 2048
DEPTH = 1
DEC_BATCH = 128
DEC_SEQ = 1
PAST_LEN = 16384
PAGE_SIZE = 128

D_CONV = D_MODEL
CONV_WIDTH = 31
D_POOL = D_MODEL
POOL_WINDOWS = (2, 4, 8, 16)
POOL_GROUPS = len(POOL_WINDOWS)
POOL_GW = D_POOL // POOL_GROUPS
POOL_MAX = 16
D_FF = ((8 * D_MODEL // 3 + 127) // 128) * 128
N_MOD = 9
D_IN = 2 * D_CONV + D_POOL + 2 * D_MODEL
DN_ALPHA = (2.0 * DEPTH) ** 0.25
DN_BETA = (8.0 * DEPTH) ** -0.25
FFN_RES = 0.5
LN_EPS = 1e-5

kernel_name = "gated_conv_pool_macaron_decoder_step"


def layer_norm(x, g, b):
    xf = x.astype(jnp.float32)
    mu = jnp.mean(xf, axis=-1, keepdims=True)
    var = jnp.mean(jnp.square(xf - mu), axis=-1, keepdims=True)
    y = (xf - mu) * lax.rsqrt(var + LN_EPS)
    return (y * g.astype(jnp.float32) + b.astype(jnp.float32)).astype(x.dtype)


def swiglu(h, w_in, w_out):
    gt, up = jnp.split(h @ w_in, 2, axis=-1)
    return (jax.nn.silu(gt) * up) @ w_out


def causal_depthwise_conv(u, prev, w, b):
    ext = jnp.concatenate([prev.astype(u.dtype), u], axis=1)
    y = lax.conv_general_dilated(ext, w[:, None, :].astype(u.dtype), window_strides=(1,),
                                 padding='VALID', dimension_numbers=('NWC', 'WIO', 'NWC'),
                                 feature_group_count=u.shape[-1])
    return y + b, ext[:, -(CONV_WIDTH - 1):]


def multiscale_pool(u, prev, pos0):
    T = u.shape[1]
    P = POOL_MAX - 1
    ext = jnp.concatenate([prev.astype(u.dtype), u], axis=1)
    cs = jnp.cumsum(ext.astype(jnp.float32), axis=1)
    cs = jnp.pad(cs, ((0, 0), (1, 0), (0, 0)))
    pos = pos0 + jnp.arange(T, dtype=jnp.int32)
    means = []
    for g, w in enumerate(POOL_WINDOWS):
        sl = slice(g * POOL_GW, (g + 1) * POOL_GW)
        s = cs[:, P + 1:P + 1 + T, sl] - cs[:, P + 1 - w:P + 1 - w + T, sl]
        cnt = jnp.minimum(w, pos + 1).astype(jnp.float32)[None, :, None]
        means.append(s / cnt)
    mean = jnp.concatenate(means, axis=-1).astype(u.dtype)
    return mean - u, ext[:, -P:]


def hybrid_mixer(h, conv_prev, pool_prev, pos0, w_in, conv_w, conv_b, conv_ln_g, conv_ln_b,
                 w_conv_out, pool_w, pool_scale, w_pool_out, w_out):
    Bn, T = h.shape[0], h.shape[1]
    proj = h @ w_in
    a, bg, u_pool, gate_a, gate_b = jnp.split(
        proj, [D_CONV, 2 * D_CONV, 2 * D_CONV + D_POOL, 2 * D_CONV + D_POOL + D_MODEL], axis=-1)
    glu = a * jax.nn.sigmoid(bg)
    conv, new_conv = causal_depthwise_conv(glu, conv_prev, conv_w, conv_b)
    y_a = jax.nn.silu(layer_norm(conv, conv_ln_g, conv_ln_b)) @ w_conv_out
    pooled, new_pool = multiscale_pool(u_pool, pool_prev, pos0)
    mixed = jnp.einsum('btgi,gij->btgj', pooled.reshape(Bn, T, POOL_GROUPS, POOL_GW),
                       pool_w).reshape(Bn, T, D_POOL)
    y_b = (mixed * pool_scale) @ w_pool_out
    merged = jax.nn.sigmoid(gate_a) * y_a + jax.nn.sigmoid(gate_b) * y_b
    return merged @ w_out, new_conv, new_pool


def decoder_layer(x, c, conv_prev, pool_prev, pos0, w_ada, b_ada,
                  ffn1_w_in, ffn1_w_out, ln1_g, ln1_b,
                  w_in, conv_w, conv_b, conv_ln_g, conv_ln_b, w_conv_out,
                  pool_w, pool_scale, w_pool_out, w_out, ln2_g, ln2_b,
                  ffn2_w_in, ffn2_w_out, ln3_g, ln3_b):
    mod = (jax.nn.silu(c) @ w_ada + b_ada)[:, None, :]
    sh1, sc1, gt1, sh2, sc2, gt2, sh3, sc3, gt3 = jnp.split(mod, N_MOD, axis=-1)
    h = x * (1 + sc1) + sh1
    x = layer_norm(DN_ALPHA * x + FFN_RES * gt1 * swiglu(h, ffn1_w_in, ffn1_w_out), ln1_g, ln1_b)
    h = x * (1 + sc2) + sh2
    m, new_conv, new_pool = hybrid_mixer(h, conv_prev, pool_prev, pos0, w_in, conv_w, conv_b,
                                         conv_ln_g, conv_ln_b, w_conv_out, pool_w, pool_scale,
                                         w_pool_out, w_out)
    x = layer_norm(DN_ALPHA * x + gt2 * m, ln2_g, ln2_b)
    h = x * (1 + sc3) + sh3
    x = layer_norm(DN_ALPHA * x + FFN_RES * gt3 * swiglu(h, ffn2_w_in, ffn2_w_out), ln3_g, ln3_b)
    return x, new_conv, new_pool


def setup_inputs(seed: int = 0) -> dict:
    key = jax.random.key(seed)
    ks = jax.random.split(key, 32)
    f32 = jnp.float32

    def nrm(k, shape, scale):
        return jax.random.normal(k, shape, f32) * scale

    def gain(k, shape):
        return 1.0 + 0.05 * jax.random.normal(k, shape, f32)

    L = DEPTH
    return {
        "x_prompt": nrm(ks[0], (BATCH, SEQ, D_MODEL), 1.0),
        "x_sample": nrm(ks[1], (DEC_BATCH, DEC_SEQ, D_MODEL), 1.0),
        "state_conv": nrm(ks[2], (L, DEC_BATCH, CONV_WIDTH - 1, D_CONV), 0.5),
        "state_pool": nrm(ks[3], (L, DEC_BATCH, POOL_MAX - 1, D_POOL), 1.0),
        "c_prompt": nrm(ks[4], (BATCH, D_MODEL), 1.0),
        "c_sample": nrm(ks[5], (DEC_BATCH, D_MODEL), 1.0),
        "w_ada": nrm(ks[6], (L, D_MODEL, N_MOD * D_MODEL), 0.5 * D_MODEL ** -0.5),
        "b_ada": nrm(ks[7], (L, N_MOD * D_MODEL), 0.01),
        "ffn1_w_in": nrm(ks[8], (L, D_MODEL, 2 * D_FF), D_MODEL ** -0.5),
        "ffn1_w_out": nrm(ks[9], (L, D_FF, D_MODEL), DN_BETA * D_FF ** -0.5),
        "ln1_g": gain(ks[10], (L, D_MODEL)),
        "ln1_b": nrm(ks[11], (L, D_MODEL), 0.02),
        "w_in": nrm(ks[12], (L, D_MODEL, D_IN), D_MODEL ** -0.5),
        "conv_w": nrm(ks[13], (L, CONV_WIDTH, D_CONV), CONV_WIDTH ** -0.5),
        "conv_b": nrm(ks[14], (L, D_CONV), 0.02),
        "conv_ln_g": gain(ks[15], (L, D_CONV)),
        "conv_ln_b": nrm(ks[16], (L, D_CONV), 0.02),
        "w_conv_out": nrm(ks[17], (L, D_CONV, D_MODEL), D_CONV ** -0.5),
        "pool_w": nrm(ks[18], (L, POOL_GROUPS, POOL_GW, POOL_GW), POOL_GW ** -0.5),
        "pool_scale": 1.0 + 0.1 * jax.random.normal(ks[19], (L, D_POOL), f32),
        "w_pool_out": nrm(ks[20], (L, D_POOL, D_MODEL), D_POOL ** -0.5),
        "w_out": nrm(ks[21], (L, D_MODEL, D_MODEL), DN_BETA * D_MODEL ** -0.5),
        "ln2_g": gain(ks[22], (L, D_MODEL)),
        "ln2_b": nrm(ks[23], (L, D_MODEL), 0.02),
        "ffn2_w_in": nrm(ks[24], (L, D_MODEL, 2 * D_FF), D_MODEL ** -0.5),
        "ffn2_w_out": nrm(ks[25], (L, D_FF, D_MODEL), DN_BETA * D_FF ** -0.5),
        "ln3_g": gain(ks[26], (L, D_MODEL)),
        "ln3_b": nrm(ks[27], (L, D_MODEL), 0.02),
    }


def reference(x_prompt, x_sample, state_conv, state_pool, c_prompt, c_sample,
              w_ada, b_ada, ffn1_w_in, ffn1_w_out, ln1_g, ln1_b,
              w_in, conv_w, conv_b, conv_ln_g, conv_ln_b, w_conv_out,
              pool_w, pool_scale, w_pool_out, w_out, ln2_g, ln2_b,
              ffn2_w_in, ffn2_w_out, ln3_g, ln3_b):
    xp, xs = x_prompt, x_sample
    Bp = x_prompt.shape[0]
    conv_p_list, pool_p_list, conv_s_list, pool_s_list = [], [], [], []
    for l in range(DEPTH):
        lw = (w_ada[l], b_ada[l], ffn1_w_in[l], ffn1_w_out[l], ln1_g[l], ln1_b[l],
              w_in[l], conv_w[l], conv_b[l], conv_ln_g[l], conv_ln_b[l], w_conv_out[l],
              pool_w[l], pool_scale[l], w_pool_out[l], w_out[l], ln2_g[l], ln2_b[l],
              ffn2_w_in[l], ffn2_w_out[l], ln3_g[l], ln3_b[l])
        zc = jnp.zeros((Bp, CONV_WIDTH - 1, D_CONV), xp.dtype)
        zp = jnp.zeros((Bp, POOL_MAX - 1, D_POOL), xp.dtype)
        xp, nc_p, np_p = decoder_layer(xp, c_prompt, zc, zp, 0, *lw)
        xs, nc_s, np_s = decoder_layer(xs, c_sample, state_conv[l], state_pool[l], PAST_LEN, *lw)
        conv_p_list.append(nc_p)
        pool_p_list.append(np_p)
        conv_s_list.append(nc_s)
        pool_s_list.append(np_s)
    new_conv_prompt = jnp.stack(conv_p_list, axis=0)
    new_pool_prompt = jnp.stack(pool_p_list, axis=0)
    new_conv_sample = jnp.stack(conv_s_list, axis=0)
    new_pool_sample = jnp.stack(pool_s_list, axis=0)
    return (xp, xs, new_conv_prompt, new_pool_prompt, new_conv_sample, new_pool_sample)
```

```python
import numpy as np
from contextlib import ExitStack
import concourse.bass as bass
import concourse.mybir as mybir
from concourse.bass_utils import run_bass_kernel_spmd

F32 = mybir.dt.float32
BF16 = mybir.dt.bfloat16
AF = mybir.ActivationFunctionType
ALU = mybir.AluOpType
AX = mybir.AxisListType

D = 1024
KC = 8
DFF = 2816
FC = 22
PWID = 1024
SWID = 16
NTC = PWID + SWID
EPS = 1e-5
ALPHA = 2.0 ** 0.25
EPS_DN = EPS / (ALPHA * ALPHA)
POOL_W = (2, 4, 8, 16)
NSLOT = 4
WARM_C = 5
WARM_N = 20
SLOT_ELEMS = 4096


class Op:
    __slots__ = ("eng", "fn", "deps", "marked", "count", "dma_sem", "dma_val", "idx")

    def __init__(self, eng, fn, deps):
        self.eng = eng
        self.fn = fn
        self.deps = deps
        self.marked = False
        self.count = 0
        self.dma_sem = None
        self.dma_val = 0


class Sched:
    ENGS = ("pe", "act", "dve", "pool", "sp")

    def __init__(self):
        self.ops = {e: [] for e in self.ENGS}
        self.last_w = {}
        self.readers = {}
        self.dma_counts = {}
        self.all_ops = []
        self.known = set()
        self.pending = {}

    def add(self, eng, fn, reads=(), writes=(), dma_sem=None, n_dma=0, extra=()):
        deps = []
        seen = set()

        def push(o):
            if o is not None and id(o) not in seen:
                seen.add(id(o))
                deps.append(o)

        for k in list(reads) + list(writes):
            if k not in self.known:
                self.known.add(k)
                if k[0] in self.pending:
                    self.readers.setdefault(k, []).extend(self.pending[k[0]])
        for k in reads:
            push(self.last_w.get(k))
        for k in writes:
            push(self.last_w.get(k))
            for r in self.readers.get(k, ()):
                push(r)
        for o in extra:
            push(o)
        op = Op(eng, fn, deps)
        if dma_sem is not None:
            c = self.dma_counts.get(dma_sem, 0) + 16 * n_dma
            self.dma_counts[dma_sem] = c
            op.dma_sem = dma_sem
            op.dma_val = c
        for k in reads:
            self.readers.setdefault(k, []).append(op)
        for k in writes:
            self.last_w[k] = op
            self.readers[k] = []
        self.ops[eng].append(op)
        self.all_ops.append(op)
        return op

    def fence(self, new_names, old_names):
        olds = []
        seen = set()
        for k in list(self.known):
            if k[0] in old_names:
                for o in [self.last_w.get(k)] + list(self.readers.get(k, ())):
                    if o is not None and id(o) not in seen:
                        seen.add(id(o))
                        olds.append(o)
        for n in new_names:
            self.pending[n] = list(olds)
        for k in list(self.known):
            if k[0] in new_names:
                self.readers.setdefault(k, []).extend(olds)

    def finalize(self):
        for op in self.all_ops:
            for d in op.deps:
                if d.dma_sem is None:
                    if d.eng == "pe" and op.eng == "pe":
                        continue
                    d.marked = True
        for e in self.ENGS:
            c = 0
            for op in self.ops[e]:
                if op.marked:
                    c += 1
                    op.count = c

    def emit(self, eng, handle, prog_sems, dma_sems):
        waited = {}
        for op in self.ops[eng]:
            for d in op.deps:
                if d.dma_sem is not None:
                    key, val, sem = ("dma", d.dma_sem), d.dma_val, dma_sems[d.dma_sem]
                else:
                    if d.eng == "pe" and eng == "pe":
                        continue
                    key, val, sem = ("eng", d.eng), d.count, prog_sems[d.eng]
                if waited.get(key, 0) < val:
                    handle.wait_ge(sem, val)
                    waited[key] = val
            ins = op.fn(handle)
            if op.marked:
                ins.then_inc(prog_sems[eng], 1)


def build_program(groups_cfg=None, debug=None):
    nc = bass.Bass("TRN2", target_bir_lowering=False)
    S = Sched()

    def din(name, shape):
        return nc.dram_tensor(name, list(shape), F32, kind="ExternalInput").ap()

    def dout(name, shape):
        return nc.dram_tensor(name, list(shape), F32, kind="ExternalOutput").ap()

    x_p = din("x_p", [2048, D])
    x_s = din("x_s", [SWID, D])
    sconv = din("sconv", [SWID, 30, D])
    spool = din("spool", [SWID, 15, D])
    c_all = din("c_all", [17, D])
    w_ada = din("w_ada", [D, 9 * D])
    b_ada = din("b_ada", [72, 128])
    ffn_w_in = [din("ffn1_w_in", [D, 2 * DFF]), din("ffn2_w_in", [D, 2 * DFF])]
    ffn_w_out = [din("ffn1_w_out", [DFF, D]), din("ffn2_w_out", [DFF, D])]
    w_in = din("w_in", [D, 5 * D])
    conv_w = din("conv_w", [31, D])
    w_conv_out = din("w_conv_out", [D, D])
    pool_w = din("pool_w", [4, 256, 256])
    w_pool_out = din("w_pool_out", [D, D])
    w_out = din("w_out", [D, D])
    pvec = din("pvec", [80, 128])
    cb_row = din("cb_row", [2, D])
    ident_d = din("ident", [128, 128])
    selc_d = din("selc", [120, 4, 16])
    selp_d = din("selp", [120, 2, 4, 16])
    invc_d = din("invc", [128, 4, 16])

    y_p = dout("y_p", [2048, D])
    y_s = dout("y_s", [SWID, D])
    ncp = dout("ncp", [30, D])
    npp = dout("npp", [15, D])
    ncs = dout("ncs", [SWID, 30, D])
    nps = dout("nps", [SWID, 15, D])
    dbg = dout("dbg", [128, KC * NTC]) if debug else None

    PV = {n: i for i, n in enumerate(
        ["ln1_g", "ln1_b", "ln2_g", "ln2_b", "ln3_g", "ln3_b", "conv_b", "cln_g", "cln_b", "pscale"])}
    R1N = ["hid", "xin", "xs_in", "glu", "gluhalo", "siluln", "mixs", "pooled"]
    R2N = ["conv", "merged", "m1h", "upb", "tmb", "sstage", "scprod", "setupR2", "yst", "vsq"]

    es = ExitStack()
    with es:
        def sb(name, shape, dt=F32):
            return es.enter_context(nc.sbuf_tensor(name, list(shape), dt))

        ident = sb("ident_sb", [128, 128])
        ones_bf = sb("ones_bf", [128, 128], BF16)
        pT = sb("pT", [128, 80])
        baT = sb("baT", [128, 72])
        cwT = sb("cwT", [128, 248])
        modT = sb("modT", [128, 72, 17])
        cT = sb("cT", [128, KC, 17], BF16)
        invc = sb("invc_sb", [128, 4, 16])
        selc = sb("selc_sb", [120, 4, 16])
        selp = sb("selp_sb", [120, 2, 4, 16])
        cbrow = sb("cbrow", [16, 2, D])
        xT = sb("xT", [128, KC, NTC])
        h = sb("h", [128, KC, NTC], BF16)
        R1 = sb("R1", [128, 11440])
        R2 = sb("R2", [128, KC * NTC])
        scr = [sb("scr0", [128, NTC]), sb("scr1", [128, NTC])]
        ring = sb("ring", [128, NSLOT, SLOT_ELEMS], BF16)
        poolw = sb("poolw", [128, 4, 2, 256], BF16)
        gl_halo = sb("gl_halo", [128, KC, 30])
        up_halo = sb("up_halo", [128, KC, 15])
        stmp = sb("stmp", [128, 8, 16])
        rstd_sb = sb("rstd_sb", [128, NTC])
        glus = sb("glus", [128, KC, SWID])
        lns = sb("lns", [128, 2, KC, SWID], BF16)
        epsT = sb("epsT", [128, 2])
        sscr = sb("sscr", [128, 2, SWID])
        jnk = sb("jnk", [128, 2])
        diag = sb("diag", [128, 8, 128], BF16)
        tmA = sb("tmA", [16, D])
        mprev = sb("mprev_sb", [16, D])
        ps = es.enter_context(nc.psum_tensor("ps", [128, 8, 512], F32))

        hid = R1[:, :].bitcast(BF16).rearrange("p (j n) -> p j n", n=NTC)
        xin = R1[:, 0:8192].rearrange("p (s r d) -> p s r d", s=2, r=4)
        xs_in = R1[:, 8192:9216]
        yst = R2[:, 0:4096].rearrange("p (s d) -> p s d", s=4)
        glu = R1[:, 0:KC * 1070].rearrange("p (c n) -> p c n", n=1070)
        glub = R1[:, 0:KC * 535].bitcast(BF16).rearrange("p (c n) -> p c n", n=1070)
        siluln = R1[:, 0:4160].bitcast(BF16).rearrange("p (c n) -> p c n", n=NTC)
        mixs = R1[:, 4160:8320].bitcast(BF16).rearrange("p (c n) -> p c n", n=NTC)
        pooled = R1[:, 8320:11440].bitcast(BF16).rearrange("p (c n) -> p c n", n=NTC)
        conv = R2[:, :].rearrange("p (c n) -> p c n", n=NTC)
        merged = R2[:, 0:4160].bitcast(BF16).rearrange("p (c n) -> p c n", n=NTC)
        m1h = R2[:, 4160:8320].rearrange("p (c n) -> p c n", n=NTC)
        vsqf = R2[:, 4160:8320].bitcast(BF16).rearrange("p (c n) -> p c n", n=NTC)
        upb = R2[:, 0:6 * 1056].rearrange("p (b n) -> p b n", n=1056)
        tmbuf = R2[:, 6400:8320].rearrange("p (b n) -> p b n", n=384)
        sc_st = R2[:, 0:4096].rearrange("p (q d) -> p q d", q=4)
        wrep = R2[:, 4096:5120]
        sp_st = R2[:, 5120:7168].rearrange("p (q d) -> p q d", q=2)

        def pb_p(b):
            return ps[:, 3 * b:3 * b + 2, :]

        def pb_s(b):
            return ps[:, 3 * b + 2, 0:SWID]

        sem_names = []
        SEMS = {}
        OUT_OPS = []

        def semname(n):
            if n not in sem_names:
                sem_names.append(n)
            return n

        if groups_cfg is None:
            groups_cfg = [("A", ["p"]), ("B", ["p", "s"])]

        def P3(ap2):
            return ap2.rearrange("p (t n) -> p t n", n=512)

        def SCR(i):
            return [("scr", i, "a"), ("scr", i, "b")]

        def SK(i, pk):
            return SCR(i) if pk == "p" else [("sscr", i)]

        def dma_op(eng, sem, pairs, reads=(), writes=(), out=False):
            sem = semname(sem)

            def fn(e):
                ins = None
                for (dst, src) in pairs:
                    ins = e.dma_start(out=dst, in_=src)
                    ins.then_inc(SEMS[sem], 16)
                return ins
            op = S.add(eng, fn, reads=reads, writes=writes, dma_sem=sem, n_dma=len(pairs))
            if out:
                OUT_OPS.append(op)
            return op

        def dbg_dump(name, ap, keys, is_bf16=False):
            if debug != name:
                return
            n = ap.shape[-1] if len(ap.shape) == 2 else None
            if len(ap.shape) == 3:
                dst = dbg[:, 0:ap.shape[1] * ap.shape[2]].rearrange("p (c n) -> p c n", n=ap.shape[2])
            else:
                dst = dbg[:, 0:n]
            dma_op("pool" if is_bf16 else "sp", "dbg", [(dst, ap)], reads=keys, out=True)

        def mm_group(pbuf, parts, lhs_fn, rhs_fn, nk, reads):
            tiles = []
            if "s" in parts:
                tiles.append((ps[:, 3 * pbuf + 2, 0:SWID], PWID, SWID))
            if "p" in parts:
                tiles.append((ps[:, 3 * pbuf, :], 0, 512))
                tiles.append((ps[:, 3 * pbuf + 1, :], 512, 512))

            def fn(e):
                ins = None
                for k in range(nk):
                    lt = lhs_fn(k)
                    for (o, c0, n) in tiles:
                        ins = e.matmul(o, lt, rhs_fn(k, c0, n), start=(k == 0), stop=(k == nk - 1))
                return ins
            return S.add("pe", fn, reads=reads, writes=[("PB", pbuf)])

        ring_ctr = [0]

        def slot_k(s, kcn, cw):
            return ring[:, s, 0:kcn * cw].rearrange("p (k n) -> p k n", n=cw)

        def wsrc(w, kcn, c0, cw):
            return w.rearrange("(k p) n -> p k n", p=128)[:, 0:kcn, c0:c0 + cw]

        def wblock(dmas_fn):
            s = ring_ctr[0] % NSLOT
            ring_ctr[0] += 1
            dma_op("pool", "w%d" % s, dmas_fn(s), writes=[("wslot", s)])
            return s

        def modp(m, c):
            return modT[:, m * 8 + c, 0:1]

        def mods(m, c):
            return modT[:, m * 8 + c, 1:17]

        def pv(name, c):
            i = PV[name] * 8 + c
            return pT[:, i:i + 1]

        def pkeys(parts, name, c):
            return [(name, c, pk) for pk in parts]

        ada_pending = list(range(4, 18))
        DERIVE = {1: ("add", 1.0), 4: ("add", 1.0), 7: ("add", 1.0), 2: ("mul", 0.5 / ALPHA), 5: ("mul", 1.0 / ALPHA), 8: ("mul", 0.5 / ALPHA)}

        def ada_block(blk):
            s = wblock(lambda s, blk=blk: [(slot_k(s, KC, 512), wsrc(w_ada, KC, blk * 512, 512))])
            bank = 6 + (blk % 2)

            def fn(e, s=s, bank=bank):
                ins = None
                for m in range(4):
                    for k in range(KC):
                        ins = e.matmul(ps[:, bank, m * 17:(m + 1) * 17], slot_k(s, KC, 512)[:, k, m * 128:(m + 1) * 128],
                                       cT[:, k, :], start=(k == 0), stop=(k == KC - 1))
                return ins
            S.add("pe", fn, reads=[("wslot", s), ("cT",)], writes=[("bank", bank)])

            def ev(e, blk=blk, bank=bank):
                return e.tensor_tensor(out=modT[:, blk * 4:(blk + 1) * 4, :],
                                       in0=ps[:, bank, 0:68].rearrange("p (m t) -> p m t", t=17),
                                       in1=baT[:, blk * 4:(blk + 1) * 4].unsqueeze(2).to_broadcast([128, 4, 17]),
                                       op=ALU.add)
            S.add("dve", ev, reads=[("bank", bank), ("params",)], writes=[("mod",)])
            if blk % 2 == 1 and (blk // 2) in DERIVE:
                m = blk // 2
                kind, f = DERIVE[m]
                if kind == "add":
                    S.add("dve", lambda e, m=m, f=f: e.tensor_scalar_add(out=modT[:, m * 8:(m + 1) * 8, :], in0=modT[:, m * 8:(m + 1) * 8, :],
                                                                         scalar1=f), writes=[("mod",)])
                else:
                    S.add("dve", lambda e, m=m, f=f: e.tensor_scalar_mul(out=modT[:, m * 8:(m + 1) * 8, :], in0=modT[:, m * 8:(m + 1) * 8, :],
                                                                         scalar1=f), writes=[("mod",)])

        def ada_more(n=1):
            for _ in range(n):
                if ada_pending:
                    ada_block(ada_pending.pop(0))

        def phase_setup():
            cwv = conv_w.rearrange("k (c p) -> k c p", p=128)
            pairs = [
                (ident[:, :], ident_d[:, :]),
                (invc[:, :, :], invc_d[:, :, :]),
                (selc[:, :, :], selc_d[:, :, :]),
                (selp[:, :, :, :], selp_d[:, :, :, :]),
                (scr[0][0:80, 0:128], pvec[:, :]),
                (scr[0][0:72, 128:256], b_ada[:, :]),
                (scr[1][0:17, 0:D], c_all[:, :]),
            ]
            for k in range(31):
                half, r0 = (0, k * 8) if k < 16 else (1, (k - 16) * 8)
                pairs.append((R2[r0:r0 + 8, half * 128:(half + 1) * 128], cwv[k, :, :]))
            for r in range(2):
                pairs.append((cbrow[:, r, :], cb_row[r:r + 1, :].partition_broadcast(16)))
            dma_op("sp", "setup", pairs, writes=SCR(0) + SCR(1) + [("setupR2",), ("const",)])
            S.add("dve", lambda e: e.memset(ones_bf[:, :], 1.0 / D), writes=[("ones",)])
            S.add("dve", lambda e: e.memset(epsT[:, 0:1], EPS), writes=[("ones",)])
            S.add("dve", lambda e: e.memset(epsT[:, 1:2], EPS_DN), writes=[("ones",)])

            def t_params(e):
                e.transpose(ps[:, 6, 0:80], scr[0][0:80, 0:128], ident[0:80, 0:80])
                e.transpose(ps[:, 6, 80:152], scr[0][0:72, 128:256], ident[0:72, 0:72])
                e.transpose(ps[:, 6, 152:280], R2[0:128, 0:128], ident[:, :])
                ins = e.transpose(ps[:, 6, 280:400], R2[0:120, 128:256], ident[0:120, 0:120])
                for c in range(KC):
                    ins = e.transpose(ps[:, 7, c * 17:(c + 1) * 17], scr[1][0:17, c * 128:(c + 1) * 128], ident[0:17, 0:17])
                return ins
            S.add("pe", t_params, reads=SCR(0) + SCR(1) + [("setupR2",), ("const",)], writes=[("bank", 6), ("bank", 7)])
            S.add("dve", lambda e: e.tensor_copy(out=pT[:, :], in_=ps[:, 6, 0:80]), reads=[("bank", 6)], writes=[("params",)])
            S.add("dve", lambda e: e.tensor_copy(out=baT[:, :], in_=ps[:, 6, 80:152]), reads=[("bank", 6)], writes=[("params",)])
            S.add("dve", lambda e: e.tensor_copy(out=cwT[:, :], in_=ps[:, 6, 152:400]), reads=[("bank", 6)], writes=[("params",)])
            S.add("act", lambda e: e.activation(out=cT[:, :, :], in_=ps[:, 7, 0:KC * 17].rearrange("p (c t) -> p c t", t=17),
                                                func=AF.Silu), reads=[("bank", 7)], writes=[("cT",)])
            for blk in range(4):
                ada_block(blk)
            dma_op("pool", "poolw", [(poolw[:, g, :, :], pool_w[g].rearrange("(i p) j -> p i j", p=128)) for g in range(4)],
                   writes=[("poolw",)])
            dbg_dump("modT", modT[:, :, :].rearrange("p m t -> p (m t)"), [("mod",)])

        def x_load(parts, row0):
            S.fence(["xin", "xs_in"], R1N)
            for pk in parts:
                if pk == "p":
                    for hf in range(2):
                        src = x_p[row0 + hf * 512: row0 + (hf + 1) * 512, :].rearrange("(r p) d -> p r d", p=128)
                        dma_op("sp", "xin%d" % hf, [(xin[:, hf, :, :], src)], writes=[("xin", hf)])
                else:
                    dma_op("sp", "xsin", [(xs_in[0:SWID, :], x_s[:, :])], writes=[("xs_in",)])

        def phase_x(parts, row0):
            S.fence(["bank"], ["PB"])
            bank_ctr = [0]
            for pk in parts:
                if pk == "p":
                    for hf in range(2):
                        for c in range(KC):
                            bank = bank_ctr[0] % 6
                            bank_ctr[0] += 1

                            def tr(e, hf=hf, c=c, bank=bank):
                                ins = None
                                for r in range(4):
                                    ins = e.transpose(ps[:, bank, r * 128:(r + 1) * 128], xin[:, hf, r, c * 128:(c + 1) * 128], ident[:, :])
                                return ins
                            S.add("pe", tr, reads=[("xin", hf), ("const",)], writes=[("bank", bank)])
                            cs = slice(hf * 512, (hf + 1) * 512)
                            S.add("act", lambda e, c=c, bank=bank, cs=cs: e.activation(out=xT[:, c, cs], in_=ps[:, bank, :], func=AF.Copy),
                                  reads=[("bank", bank)], writes=[("xT", c, "p")])
                            S.add("dve", lambda e, c=c, cs=cs: e.tensor_scalar(
                                out=h[:, c, cs], in0=xT[:, c, cs], scalar1=modp(1, c), scalar2=modp(0, c), op0=ALU.mult, op1=ALU.add),
                                reads=[("xT", c, "p"), ("mod",)], writes=[("h", c, "p")])
                else:
                    bank = bank_ctr[0] % 6
                    bank_ctr[0] += 1

                    def tr(e, bank=bank):
                        ins = None
                        for c in range(KC):
                            ins = e.transpose(ps[:, bank, c * SWID:(c + 1) * SWID], xs_in[0:SWID, c * 128:(c + 1) * 128], ident[0:SWID, 0:SWID])
                        return ins
                    S.add("pe", tr, reads=[("xs_in",), ("const",)], writes=[("bank", bank)])
                    pv3 = ps[:, bank, 0:KC * SWID].rearrange("p (c t) -> p c t", t=SWID)
                    S.add("act", lambda e, pv3=pv3: e.activation(out=xT[:, :, PWID:NTC], in_=pv3, func=AF.Copy),
                          reads=[("bank", bank)], writes=[("xT", c, "s") for c in range(KC)])
                    S.add("dve", lambda e: e.tensor_tensor(out=stmp[:, :, :], in0=xT[:, :, PWID:NTC], in1=modT[:, 8:16, 1:17], op=ALU.mult),
                          reads=[("xT", c, "s") for c in range(KC)] + [("mod",)], writes=[("stmp",)])
                    S.add("dve", lambda e: e.tensor_tensor(out=h[:, :, PWID:NTC], in0=stmp[:, :, :], in1=modT[:, 0:8, 1:17], op=ALU.add),
                          reads=[("stmp",), ("mod",)], writes=[("h", c, "s") for c in range(KC)])
            S.fence(["PB"], ["bank"])
            dbg_dump("xT0", xT[:, :, :], [("xT", c, pk) for c in range(KC) for pk in parts])
            dbg_dump("h0", h[:, :, :], [("h", c, pk) for c in range(KC) for pk in parts], is_bf16=True)

        def layer_norm(parts, src_fn, src3_s, src_keys_fn, eps, gname, outs_p, outs_s, next_af=None, pre=False):
            has_p, has_s = "p" in parts, "s" in parts
            gi0 = PV[gname] * 8
            MB = [("bank", 6), ("bank", 7)]
            if has_s:
                skeys = [k for c in range(KC) for k in src_keys_fn(c, "s")]
                S.add("act", lambda e: e.activation(out=lns[:, 0, :, :], in_=src3_s, func=AF.Copy), reads=skeys, writes=[("lns", 0)])
                S.add("dve", lambda e: e.tensor_tensor(out=lns[:, 1, :, :], in0=src3_s, in1=src3_s, op=ALU.mult), reads=skeys, writes=[("lns", 1)])

                def st_s(e):
                    ins = None
                    for c in range(KC):
                        ins = e.matmul(ps[:, 2, 0:SWID], ones_bf[:, :], lns[:, 0, c, :], start=(c == 0), stop=(c == KC - 1))
                    for c in range(KC):
                        ins = e.matmul(ps[:, 2, SWID:2 * SWID], ones_bf[:, :], lns[:, 1, c, :], start=(c == 0), stop=(c == KC - 1))
                    return ins
                if not has_p:
                    S.add("pe", st_s, reads=[("lns", 0), ("lns", 1), ("ones",)], writes=[("PB", 0)])
            if has_p:
                for c in range(KC):
                    i = c % 2
                    if pre:
                        vbf, vsq = h[:, c, :], vsqf[:, c, :]
                        rk = [("h", c, "p"), ("vsq", c)]
                    else:
                        vbf = scr[i][:, 0:520].bitcast(BF16)
                        vsq = scr[i][:, 520:1040].bitcast(BF16)
                        rk = SCR(i)
                        S.add("act", lambda e, c=c, vbf=vbf: e.activation(out=vbf[:, 0:PWID], in_=src_fn(c)[:, 0:PWID], func=AF.Copy),
                              reads=src_keys_fn(c, "p"), writes=[("scr", i, "a")])
                        S.add("dve", lambda e, c=c, vsq=vsq: e.tensor_tensor(out=vsq[:, 0:PWID], in0=src_fn(c)[:, 0:PWID], in1=src_fn(c)[:, 0:PWID],
                                                                            op=ALU.mult),
                              reads=src_keys_fn(c, "p"), writes=[("scr", i, "b")])

                    def st(e, c=c, vbf=vbf, vsq=vsq):
                        ins = None
                        for (bo, c0) in ((0, 0), (1, 512)):
                            e.matmul(ps[:, 6 + bo, :], ones_bf[:, :], vbf[:, c0:c0 + 512], start=(c == 0), stop=(c == KC - 1))
                            ins = e.matmul(ps[:, bo, :], ones_bf[:, :], vsq[:, c0:c0 + 512], start=(c == 0), stop=(c == KC - 1))
                        return ins
                    S.add("pe", st, reads=rk + [("ones",)], writes=MB + [("PB", 0)])
                if has_s:
                    S.add("pe", st_s, reads=[("lns", 0), ("lns", 1), ("ones",)], writes=[("PB", 0)])
                S.add("act", lambda e: e.activation(out=jnk[:, 0:1], in_=epsT[:, 0:1], func=AF.Ln), reads=[("ones",)], writes=[("jnk",)])
            for pk in parts:
                if pk == "p":
                    m_ap, r_ap, t_ap, rs_ap = ps[:, 6:8, :], ps[:, 0:2, :], P3(scr[0][:, 0:PWID]), P3(rstd_sb[:, 0:PWID])
                    mk = MB
                else:
                    m_ap, r_ap, t_ap, rs_ap = ps[:, 2, 0:SWID], ps[:, 2, SWID:2 * SWID], sscr[:, 0, :], rstd_sb[:, PWID:NTC]
                    mk = [("PB", 0)]
                S.add("act", lambda e, m_ap=m_ap, t_ap=t_ap: e.activation(out=t_ap, in_=m_ap, func=AF.Square),
                      reads=mk, writes=SK(0, pk))
                S.add("dve", lambda e, r_ap=r_ap, t_ap=t_ap, rs_ap=rs_ap: e.tensor_tensor(out=rs_ap, in0=r_ap, in1=t_ap, op=ALU.subtract),
                      reads=SK(0, pk) + [("PB", 0)], writes=[("rstd", pk)])
            for pk in parts:
                rs_ap = P3(rstd_sb[:, 0:PWID]) if pk == "p" else rstd_sb[:, PWID:NTC]
                S.add("act", lambda e, rs_ap=rs_ap: e.activation(out=rs_ap, in_=rs_ap, func=AF.Ln,
                                                                 bias=epsT[:, (0 if eps == EPS else 1):(1 if eps == EPS else 2)], scale=1.0),
                      reads=[("ones",)], writes=[("rstd", pk)])
                S.add("act", lambda e, rs_ap=rs_ap: e.activation(out=rs_ap, in_=rs_ap, func=AF.Exp, scale=-0.5),
                      writes=[("rstd", pk)])
            if has_s:
                t3 = stmp[:, :, :]
                S.add("dve", lambda e: e.tensor_tensor(out=t3, in0=src3_s, in1=ps[:, 2, 0:SWID].unsqueeze(1).to_broadcast([128, KC, SWID]),
                                                       op=ALU.subtract),
                      reads=[k for c in range(KC) for k in src_keys_fn(c, "s")] + [("PB", 0)], writes=[("stmp",)])
                S.add("dve", lambda e: e.tensor_tensor(out=t3, in0=t3, in1=rstd_sb[:, PWID:NTC].unsqueeze(1).to_broadcast([128, KC, SWID]),
                                                       op=ALU.mult),
                      reads=[("rstd", "s")], writes=[("stmp",)])
                S.add("dve", lambda e: e.tensor_tensor(out=t3, in0=t3, in1=pT[:, gi0:gi0 + 8].unsqueeze(2).to_broadcast([128, KC, SWID]),
                                                       op=ALU.mult),
                      reads=[("params",)], writes=[("stmp",)])
                outs_s(t3, [("stmp",)])
            if has_p:
                for c in range(KC):
                    i = c % 2
                    cs = slice(0, PWID)
                    m_ap, r_ap = ps[:, 6:8, :], P3(rstd_sb[:, 0:PWID])
                    v_ap, t_ap = P3(src_fn(c)[:, cs]), P3(scr[i][:, cs])
                    S.add("dve", lambda e, v_ap=v_ap, t_ap=t_ap, m_ap=m_ap: e.tensor_tensor(out=t_ap, in0=v_ap, in1=m_ap, op=ALU.subtract),
                          reads=src_keys_fn(c, "p") + MB, writes=SCR(i))
                    bop = S.add("dve", lambda e, t_ap=t_ap, r_ap=r_ap, c=c: e.scalar_tensor_tensor(
                        out=t_ap, in0=t_ap, scalar=pv(gname, c), in1=r_ap, op0=ALU.mult, op1=ALU.mult),
                        reads=[("rstd", "p"), ("params",)], writes=SCR(i))
                    if c == WARM_C:
                        def warm(e):
                            ins = None
                            for _ in range(WARM_N):
                                ins = e.matmul(ps[:, 5, :], ones_bf[:, :], ring[:, 0, 0:512], start=True, stop=True)
                            return ins
                        S.add("pe", warm, writes=[("PB", 1)], extra=[bop])
                    outs_p(c, scr[i][:, cs], SCR(i))
            if next_af is not None:
                S.add("act", lambda e: e.activation(out=jnk[:, 1:2], in_=epsT[:, 0:1], func=next_af), reads=[("ones",)], writes=[("jnk",)])

        def ln_outs_resid(bname, mod_base, with_h):
            bi0 = PV[bname] * 8

            def outs_p(c, tB, tkeys):
                cs = slice(0, PWID)
                xk = [("xT", c, "p")]
                S.add("act", lambda e, c=c, tB=tB, cs=cs: e.activation(out=xT[:, c, cs], in_=tB, func=AF.Identity, bias=pv(bname, c), scale=1.0),
                      reads=tkeys + [("params",)], writes=xk)
                if with_h:
                    S.add("act", lambda e, c=c, cs=cs: e.activation(out=h[:, c, cs], in_=xT[:, c, cs], func=AF.Identity,
                                                                    bias=modp(mod_base, c), scale=modp(mod_base + 1, c)),
                          reads=xk + [("mod",)], writes=[("h", c, "p")])

            def outs_s(t3, tkeys):
                xk = [("xT", c, "s") for c in range(KC)]
                S.add("dve", lambda e: e.tensor_tensor(out=xT[:, :, PWID:NTC], in0=t3,
                                                       in1=pT[:, bi0:bi0 + 8].unsqueeze(2).to_broadcast([128, KC, SWID]), op=ALU.add),
                      reads=tkeys + [("params",)], writes=xk)
                if with_h:
                    mb = mod_base
                    S.add("dve", lambda e: e.tensor_tensor(out=t3, in0=xT[:, :, PWID:NTC], in1=modT[:, (mb + 1) * 8:(mb + 2) * 8, 1:17], op=ALU.mult),
                          reads=xk + [("mod",)], writes=[("stmp",)])
                    S.add("dve", lambda e: e.tensor_tensor(out=h[:, :, PWID:NTC], in0=t3, in1=modT[:, mb * 8:(mb + 1) * 8, 1:17], op=ALU.add),
                          reads=[("stmp",), ("mod",)], writes=[("h", c, "s") for c in range(KC)])
            return outs_p, outs_s

        def resid_add(parts, b, oc, gate_mod):
            for pk in parts:
                if pk == "p":
                    S.add("dve", lambda e, b=b, oc=oc: e.scalar_tensor_tensor(
                        out=P3(xT[:, oc, 0:PWID]), in0=pb_p(b), scalar=modp(gate_mod, oc), in1=P3(xT[:, oc, 0:PWID]),
                        op0=ALU.mult, op1=ALU.add),
                        reads=[("PB", b), ("mod",)], writes=[("xT", oc, "p")])
                    S.add("act", lambda e, oc=oc: e.activation(out=h[:, oc, 0:PWID], in_=xT[:, oc, 0:PWID], func=AF.Copy),
                          reads=[("xT", oc, "p")], writes=[("h", oc, "p")])
                    S.add("dve", lambda e, oc=oc: e.tensor_tensor(out=vsqf[:, oc, 0:PWID], in0=xT[:, oc, 0:PWID], in1=xT[:, oc, 0:PWID], op=ALU.mult),
                          reads=[("xT", oc, "p")], writes=[("vsq", oc)])
                else:
                    S.add("dve", lambda e, b=b, oc=oc: e.tensor_tensor(out=stmp[:, 1, :], in0=pb_s(b), in1=mods(gate_mod, oc), op=ALU.mult),
                          reads=[("PB", b), ("mod",)], writes=[("stmp",)])
                    S.add("dve", lambda e, oc=oc: e.tensor_tensor(out=xT[:, oc, PWID:NTC], in0=stmp[:, 1, :], in1=xT[:, oc, PWID:NTC], op=ALU.add),
                          reads=[("stmp",)], writes=[("xT", oc, "s")])

        def phase_ffn(parts, f, gate_mod, gname, bname, next_mod, with_h, tag):
            win, wout = ffn_w_in[f], ffn_w_out[f]
            S.fence(["hid"], R1N)
            hkeys = [("h", c, pk) for c in range(KC) for pk in parts]
            for blk in range(FC // 2):
                def dm(s, blk=blk):
                    v = slot_k(s, KC, 512)
                    return [(v[:, :, 0:256], wsrc(win, KC, blk * 256, 256)),
                            (v[:, :, 256:512], wsrc(win, KC, DFF + blk * 256, 256))]
                s = wblock(dm)
                for jj in range(2):
                    j = 2 * blk + jj
                    sv = slot_k(s, KC, 512)
                    mm_group(0, parts, lambda k, sv=sv, jj=jj: sv[:, k, jj * 128:(jj + 1) * 128],
                             lambda k, c0, n: h[:, k, c0:c0 + n], KC, reads=[("wslot", s)] + hkeys)
                    mm_group(1, parts, lambda k, sv=sv, jj=jj: sv[:, k, 256 + jj * 128:256 + (jj + 1) * 128],
                             lambda k, c0, n: h[:, k, c0:c0 + n], KC, reads=[("wslot", s)] + hkeys)
                    i = j % 2
                    for pk in parts:
                        if pk == "p":
                            g_ap, u_ap, s_ap, o_ap = pb_p(0), pb_p(1), P3(scr[i][:, 0:PWID]), P3(hid[:, j, 0:PWID])
                        else:
                            g_ap, u_ap, s_ap, o_ap = pb_s(0), pb_s(1), sscr[:, i, :], hid[:, j, PWID:NTC]
                        S.add("act", lambda e, g_ap=g_ap, s_ap=s_ap: e.activation(out=s_ap, in_=g_ap, func=AF.Silu),
                              reads=[("PB", 0)], writes=SK(i, pk))
                        S.add("dve", lambda e, u_ap=u_ap, s_ap=s_ap, o_ap=o_ap: e.tensor_tensor(out=o_ap, in0=u_ap, in1=s_ap, op=ALU.mult),
                              reads=[("PB", 1)] + SK(i, pk), writes=[("hid", j, pk)])
                if len(ada_pending) > 8:
                    ada_more(1)
            dbg_dump("hid" + tag, hid[:, 0:8, :], [("hid", j, pk) for j in range(FC) for pk in parts], is_bf16=True)
            hidkeys = [("hid", j, pk) for j in range(FC) for pk in parts]
            S.fence(["vsq"], R2N)
            for oc in range(KC):
                s = wblock(lambda s, oc=oc: [(slot_k(s, FC, 128), wsrc(wout, FC, oc * 128, 128))])
                sv = slot_k(s, FC, 128)
                b = oc % 2
                mm_group(b, parts, lambda k, sv=sv: sv[:, k, :], lambda k, c0, n: hid[:, k, c0:c0 + n], FC,
                         reads=[("wslot", s)] + hidkeys)
                resid_add(parts, b, oc, gate_mod)
            dbg_dump("v" + tag, xT[:, :, :], [("xT", c, pk) for c in range(KC) for pk in parts])
            layer_norm(parts, lambda c: xT[:, c, :], xT[:, :, PWID:NTC], lambda c, pk: [("xT", c, pk)], EPS_DN, gname,
                       *ln_outs_resid(bname, next_mod, with_h), next_af=(AF.Sigmoid if tag == "1" else None), pre=("p" in parts))
            dbg_dump("x" + tag, xT[:, :, :], [("xT", c, pk) for c in range(KC) for pk in parts])

        def phase_mixer(parts, first_prompt):
            hkeys = [("h", c, pk) for c in range(KC) for pk in parts]
            has_p = "p" in parts
            has_s = "s" in parts
            S.fence(["glu", "gluhalo"], R1N)
            S.fence(["sstage", "scprod"], R2N)
            if has_s:
                scv = sconv.rearrange("(q s) k d -> q (s k) d", s=4)
                spv = spool.rearrange("(q s) k d -> q (s k) d", s=8)
                pairs = [(sc_st[0:120, q, :], scv[q]) for q in range(4)]
                pairs += [(wrep[s4 * 30:(s4 + 1) * 30, :], conv_w[0:30, :]) for s4 in range(4)]
                pairs += [(sp_st[0:120, q, :], spv[q]) for q in range(2)]
                dma_op("sp", "sstate", pairs, writes=[("sstage",)])
                dma_op("sp", "sshift", [(ncs[:, 0:29, :], sconv[:, 1:30, :]), (nps[:, 0:14, :], spool[:, 1:15, :])], out=True)
                for q in range(4):
                    S.add("dve", lambda e, q=q: e.tensor_tensor(out=sc_st[0:120, q, :], in0=sc_st[0:120, q, :], in1=wrep[0:120, :], op=ALU.mult),
                          reads=[("sstage",)], writes=[("scprod", q)])

                def selmm(e):
                    ins = None
                    for hh in range(2):
                        for q in range(4):
                            ins = e.matmul(ps[0:16, 6 + hh, :], selc[0:120, q, :], sc_st[0:120, q, hh * 512:(hh + 1) * 512],
                                           start=(q == 0), stop=(q == 3))
                    return ins
                S.add("pe", selmm, reads=[("scprod", q) for q in range(4)] + [("const",)], writes=[("bank", 6), ("bank", 7)])
                S.add("dve", lambda e: e.tensor_tensor(out=P3(tmA[0:16, :]), in0=ps[0:16, 6:8, :], in1=P3(cbrow[:, 1, :]), op=ALU.add),
                      reads=[("bank", 6), ("bank", 7), ("const",)], writes=[("tmA",)])

                def selpm(e):
                    ins = None
                    for g in range(4):
                        for q in range(2):
                            ins = e.matmul(ps[0:16, 6 + g // 2, (g % 2) * 256:(g % 2) * 256 + 256], selp[0:120, q, g, :],
                                           sp_st[0:120, q, g * 256:(g + 1) * 256], start=(q == 0), stop=(q == 1))
                    return ins
                S.add("pe", selpm, reads=[("sstage",), ("const",)], writes=[("bank", 6), ("bank", 7)])
                S.add("act", lambda e: e.activation(out=P3(mprev[0:16, :]), in_=ps[0:16, 6:8, :], func=AF.Copy),
                      reads=[("bank", 6), ("bank", 7)], writes=[("mprev",)])
            if has_p:
                if first_prompt:
                    S.add("dve", lambda e: e.memset(glub[:, :, 0:30], 0.0), writes=[("gluhalo",)])
                else:
                    S.add("dve", lambda e: e.tensor_copy(out=glub[:, :, 0:30], in_=gl_halo[:, :, :]), reads=[("gl_halo",)],
                          writes=[("gluhalo",)])
            slots = {}
            for c in range(KC):
                blk, cc = c // 4, c % 4
                if cc == 0:
                    slots["bg"] = wblock(lambda s, blk=blk: [(slot_k(s, KC, 512), wsrc(w_in, KC, D + blk * 512, 512))])
                    slots["a"] = wblock(lambda s, blk=blk: [(slot_k(s, KC, 512), wsrc(w_in, KC, blk * 512, 512))])
                sa, sg = slot_k(slots["a"], KC, 512), slot_k(slots["bg"], KC, 512)
                mm_group(1, parts, lambda k, sg=sg, cc=cc: sg[:, k, cc * 128:(cc + 1) * 128], lambda k, c0, n: h[:, k, c0:c0 + n], KC,
                         reads=[("wslot", slots["bg"])] + hkeys)
                mm_group(0, parts, lambda k, sa=sa, cc=cc: sa[:, k, cc * 128:(cc + 1) * 128], lambda k, c0, n: h[:, k, c0:c0 + n], KC,
                         reads=[("wslot", slots["a"])] + hkeys)
                i = c % 2
                for pk in parts:
                    if pk == "p":
                        a_ap, g_ap, s_ap, o_ap = pb_p(0), pb_p(1), P3(scr[i][:, 0:PWID]), P3(glub[:, c, 30:30 + PWID])
                    else:
                        a_ap, g_ap, s_ap, o_ap = pb_s(0), pb_s(1), sscr[:, i, :], glus[:, c, :]
                    S.add("act", lambda e, g_ap=g_ap, s_ap=s_ap: e.activation(out=s_ap, in_=g_ap, func=AF.Sigmoid),
                          reads=[("PB", 1)], writes=SK(i, pk))
                    S.add("dve", lambda e, a_ap=a_ap, s_ap=s_ap, o_ap=o_ap: e.tensor_tensor(out=o_ap, in0=a_ap, in1=s_ap, op=ALU.mult),
                          reads=[("PB", 0)] + SK(i, pk), writes=[("glu", c, pk)])
                    if pk == "p":
                        S.add("dve", lambda e, c=c, i=i: e.tensor_tensor(out=gl_halo[:, c, :], in0=ps[:, 1, 482:512],
                                                                        in1=scr[i][:, PWID - 30:PWID], op=ALU.mult),
                              reads=[("PB", 0), ("gluhalo",)] + SCR(i), writes=[("gl_halo",)])
            S.fence(["conv"], R2N)
            if has_p:
                PT = 24
                dctr = [0]
                for c in range(KC):
                    b = c % 2
                    rk = [("glu", c, "p"), ("gluhalo",), ("params",)]
                    for k in range(PT, 31):
                        wk = cwT[:, k * 8 + c:k * 8 + c + 1]
                        if k == PT:
                            S.add("dve", lambda e, c=c, k=k, wk=wk: e.tensor_scalar(
                                out=conv[:, c, 0:PWID], in0=glub[:, c, k:k + PWID], scalar1=wk, scalar2=pv("conv_b", c),
                                op0=ALU.mult, op1=ALU.add), reads=rk, writes=[("conv", c, "p")])
                        else:
                            S.add("dve", lambda e, c=c, k=k, wk=wk: e.scalar_tensor_tensor(
                                out=conv[:, c, 0:PWID], in0=glub[:, c, k:k + PWID], scalar=wk, in1=conv[:, c, 0:PWID],
                                op0=ALU.mult, op1=ALU.add), reads=rk, writes=[("conv", c, "p")])
                    for k in range(PT):
                        di = dctr[0] % 8
                        dctr[0] += 1
                        wk = cwT[:, k * 8 + c:k * 8 + c + 1]
                        S.add("act", lambda e, di=di, wk=wk: e.activation(out=diag[:, di, :], in_=ident[:, :], func=AF.Copy, scale=wk),
                              reads=[("params",), ("const",)], writes=[("diag", di)])

                        def cmm(e, c=c, k=k, di=di, b=b):
                            e.matmul(ps[:, 3 * b, :], diag[:, di, :], glub[:, c, k:k + 512], start=(k == 0), stop=(k == PT - 1))
                            return e.matmul(ps[:, 3 * b + 1, :], diag[:, di, :], glub[:, c, k + 512:k + 1024], start=(k == 0), stop=(k == PT - 1))
                        S.add("pe", cmm, reads=[("diag", di), ("glu", c, "p"), ("gluhalo",)], writes=[("PB", b)])
                    S.add("dve", lambda e, c=c, b=b: e.tensor_tensor(out=P3(conv[:, c, 0:PWID]), in0=pb_p(b), in1=P3(conv[:, c, 0:PWID]), op=ALU.add),
                          reads=[("PB", b)], writes=[("conv", c, "p")])
                    ada_more(1)
            if has_s:
                def trg(e):
                    ins = None
                    for c in range(KC):
                        ins = e.matmul(ps[0:16, 6 + c // 4, (c % 4) * 128:(c % 4 + 1) * 128], glus[:, c, :], ident[:, :], start=True, stop=True)
                    return ins
                S.add("pe", trg, reads=[("glu", c, "s") for c in range(KC)] + [("const",)], writes=[("bank", 6), ("bank", 7)])
                S.add("act", lambda e: e.activation(out=P3(scr[1][0:16, 0:D]), in_=ps[0:16, 6:8, :], func=AF.Copy),
                      reads=[("bank", 6), ("bank", 7)], writes=SCR(1))
                dma_op("sp", "o_ncs", [(ncs[:, 29, :], scr[1][0:16, 0:D])], reads=SCR(1), out=True)
                S.add("dve", lambda e: e.tensor_tensor(out=scr[0][0:16, 0:D], in0=scr[1][0:16, 0:D], in1=cbrow[:, 0, :], op=ALU.mult),
                      reads=SCR(1) + [("const",)], writes=SCR(0))
                S.add("dve", lambda e: e.tensor_tensor(out=scr[0][0:16, 0:D], in0=scr[0][0:16, 0:D], in1=tmA[0:16, :], op=ALU.add),
                      reads=[("tmA",)], writes=SCR(0))

                def trb(e):
                    ins = None
                    for c in range(KC):
                        ins = e.transpose(ps[:, 6, c * SWID:(c + 1) * SWID], scr[0][0:16, c * 128:(c + 1) * 128], ident[0:16, 0:16])
                    return ins
                S.add("pe", trb, reads=SCR(0) + [("const",)], writes=[("bank", 6)])
                S.add("act", lambda e: e.activation(out=conv[:, :, PWID:NTC], in_=ps[:, 6, 0:KC * SWID].rearrange("p (c t) -> p c t", t=SWID),
                                                    func=AF.Copy),
                      reads=[("bank", 6)], writes=[("conv", c, "s") for c in range(KC)])
            dbg_dump("conv", conv[:, :, :], [("conv", c, pk) for c in range(KC) for pk in parts])
            S.fence(["siluln"], R1N)

            def cl_outs_p(c, tB, tkeys):
                S.add("act", lambda e, c=c, tB=tB: e.activation(out=siluln[:, c, 0:PWID], in_=tB, func=AF.Silu, bias=pv("cln_b", c), scale=1.0),
                      reads=tkeys + [("params",)], writes=[("siluln", c, "p")])

            def cl_outs_s(t3, tkeys):
                bi0 = PV["cln_b"] * 8
                S.add("dve", lambda e: e.tensor_tensor(out=t3, in0=t3, in1=pT[:, bi0:bi0 + 8].unsqueeze(2).to_broadcast([128, KC, SWID]), op=ALU.add),
                      reads=tkeys + [("params",)], writes=[("stmp",)])
                S.add("act", lambda e: e.activation(out=siluln[:, :, PWID:NTC], in_=t3, func=AF.Silu),
                      reads=[("stmp",)], writes=[("siluln", c, "s") for c in range(KC)])
            layer_norm(parts, lambda c: conv[:, c, :], conv[:, :, PWID:NTC], lambda c, pk: [("conv", c, pk)], EPS, "cln_g", cl_outs_p, cl_outs_s, next_af=AF.Sigmoid)
            dbg_dump("siluln", siluln[:, :, :], [("siluln", c, pk) for c in range(KC) for pk in parts], is_bf16=True)
            S.fence(["upb", "tmb"], R2N)
            S.fence(["mixs", "pooled"], ["hid", "xin", "xs_in", "yst", "glu", "gluhalo", "mixs", "pooled"])
            def emit_m3(g):
                for jh in range(2):
                    oc = 2 * g + jh
                    bb = jh
                    pk_g = [("pooled", 2 * (g % 3) + t, pk) for t in range(2) for pk in parts]
                    mm_group(bb, parts, lambda k, g=g, jh=jh: poolw[:, g, k, jh * 128:(jh + 1) * 128],
                             lambda k, c0, n, g=g: pooled[:, 2 * (g % 3) + k, c0:c0 + n], 2, reads=pk_g + [("poolw",)])
                    for pk in parts:
                        if pk == "p":
                            i_ap, o_ap = pb_p(bb), P3(mixs[:, oc, 0:PWID])
                        else:
                            i_ap, o_ap = pb_s(bb), mixs[:, oc, PWID:NTC]
                        S.add("act", lambda e, i_ap=i_ap, o_ap=o_ap, oc=oc: e.activation(out=o_ap, in_=i_ap, func=AF.Copy,
                                                                                     scale=pv("pscale", oc)),
                              reads=[("PB", bb), ("params",)], writes=[("mixs", oc, pk)])

            def s_stage1(c):
                w = POOL_W[c // 2]
                ut = tmbuf[0:16, 2 * (c % 2), 0:128]
                pt = tmbuf[0:16, 2 * (c % 2) + 1, 0:128]
                S.add("pe", lambda e, c=c: e.matmul(ps[0:16, 6, 0:128], glus[:, c, :], ident[:, :], start=True, stop=True),
                      reads=[("usfm", c), ("const",)], writes=[("bank", 6)])
                S.add("act", lambda e, ut=ut: e.activation(out=ut, in_=ps[0:16, 6, 0:128], func=AF.Copy), reads=[("bank", 6)],
                      writes=[("tmb", 2 * (c % 2))])
                dma_op("sp", "o_nps", [(nps[:, 14, c * 128:(c + 1) * 128], ut)], reads=[("tmb", 2 * (c % 2))], out=True)
                S.add("dve", lambda e, ut=ut, pt=pt, c=c, w=w: e.scalar_tensor_tensor(
                    out=pt, in0=ut, scalar=(1.0 / w - 1.0), in1=mprev[0:16, c * 128:(c + 1) * 128], op0=ALU.mult, op1=ALU.add),
                    reads=[("tmb", 2 * (c % 2)), ("mprev",)], writes=[("tmb", 2 * (c % 2) + 1)])

            def s_stage2(c):
                g = c // 2
                pidx = 2 * (g % 3) + (c % 2)
                pt = tmbuf[0:16, 2 * (c % 2) + 1, 0:128]
                S.add("pe", lambda e, pt=pt: e.transpose(ps[:, 7, 0:SWID], pt, ident[0:16, 0:16]), reads=[("tmb", 2 * (c % 2) + 1), ("const",)],
                      writes=[("bank", 7)])
                S.add("act", lambda e, pidx=pidx: e.activation(out=pooled[:, pidx, PWID:NTC], in_=ps[:, 7, 0:SWID], func=AF.Copy),
                      reads=[("bank", 7)], writes=[("pooled", pidx, "s")])

            uslot = None
            for c in range(KC):
                g = c // 2
                w = POOL_W[g]
                if c % 4 == 0:
                    uslot = wblock(lambda s, c=c: [(slot_k(s, KC, 512), wsrc(w_in, KC, 2 * D + (c // 4) * 512, 512))])
                su = slot_k(uslot, KC, 512)
                b = c % 2
                mm_group(b, parts, lambda k, su=su, c=c: su[:, k, (c % 4) * 128:(c % 4 + 1) * 128], lambda k, c0, n: h[:, k, c0:c0 + n], KC,
                         reads=[("wslot", uslot)] + hkeys)
                if has_s:
                    if c >= 3:
                        s_stage2(c - 3)
                    if c >= 1:
                        s_stage1(c - 1)
                U, Pb, Qb = upb[:, 3 * b + 0, :], upb[:, 3 * b + 1, :], upb[:, 3 * b + 2, :]
                uk = [("upb", 3 * b + t) for t in range(3)]
                pidx = 2 * (g % 3) + (c % 2)
                pl = pooled[:, pidx, :]
                if has_p:
                    E = 15 + PWID
                    if first_prompt:
                        S.add("dve", lambda e, U=U: e.memset(U[:, 0:15], 0.0), writes=[uk[0]])
                    else:
                        S.add("dve", lambda e, U=U, c=c: e.tensor_copy(out=U[:, 0:15], in_=up_halo[:, c, :]), reads=[("up_halo", c)],
                              writes=[uk[0]])
                    S.add("act", lambda e, U=U, b=b: e.activation(out=P3(U[:, 15:E]), in_=pb_p(b), func=AF.Copy),
                          reads=[("PB", b)], writes=[uk[0]])
                    S.add("act", lambda e, U=U, c=c: e.activation(out=up_halo[:, c, :], in_=U[:, PWID:PWID + 15], func=AF.Copy), reads=[uk[0]],
                          writes=[("up_halo", c)])
                    S.add("dve", lambda e, U=U, Pb=Pb: e.tensor_tensor(out=Pb[:, 1:E], in0=U[:, 1:E], in1=U[:, 0:E - 1], op=ALU.add),
                          reads=[uk[0]], writes=[uk[1]])
                    Sb, skey = Pb, uk[1]
                    if w >= 4:
                        S.add("dve", lambda e, Pb=Pb, Qb=Qb: e.tensor_tensor(out=Qb[:, 3:E], in0=Pb[:, 3:E], in1=Pb[:, 1:E - 2], op=ALU.add),
                              reads=[uk[1]], writes=[uk[2]])
                        Sb, skey = Qb, uk[2]
                    if w >= 8:
                        S.add("dve", lambda e, Pb=Pb, Qb=Qb: e.tensor_tensor(out=Pb[:, 7:E], in0=Qb[:, 7:E], in1=Qb[:, 3:E - 4], op=ALU.add),
                              reads=[uk[2]], writes=[uk[1]])
                        Sb, skey = Pb, uk[1]
                    if w >= 16:
                        S.add("dve", lambda e, Pb=Pb, Qb=Qb: e.tensor_tensor(out=Qb[:, 15:E], in0=Pb[:, 15:E], in1=Pb[:, 7:E - 8], op=ALU.add),
                              reads=[uk[1]], writes=[uk[2]])
                        Sb, skey = Qb, uk[2]
                    S.add("dve", lambda e, Sb=Sb, U=U, pl=pl, w=w: e.scalar_tensor_tensor(
                        out=pl[:, 0:PWID], in0=Sb[:, 15:E], scalar=1.0 / w, in1=U[:, 15:E], op0=ALU.mult, op1=ALU.subtract),
                        reads=[skey, uk[0]], writes=[("pooled", pidx, "p")])
                    if first_prompt:
                        S.add("dve", lambda e, Sb=Sb, g=g: e.tensor_tensor(out=stmp[:, 2, :], in0=Sb[:, 15:31], in1=invc[:, g, :], op=ALU.mult),
                              reads=[skey, ("const",)], writes=[("stmp",)])
                        S.add("dve", lambda e, U=U, pl=pl: e.tensor_tensor(out=pl[:, 0:16], in0=stmp[:, 2, :], in1=U[:, 15:31], op=ALU.subtract),
                              reads=[("stmp",), uk[0]], writes=[("pooled", pidx, "p")])
                if has_s:
                    S.add("act", lambda e, b=b, c=c: e.activation(out=glus[:, c, :], in_=pb_s(b), func=AF.Copy), reads=[("PB", b)],
                          writes=[("usfm", c)])
                if c % 2 == 1 and g >= 2:
                    emit_m3(g - 2)
            if has_s:
                s_stage2(KC - 3)
                s_stage1(KC - 1)
                s_stage2(KC - 2)
                s_stage2(KC - 1)
            emit_m3(2)
            emit_m3(3)
            dbg_dump("mixs", mixs[:, :, :], [("mixs", c, pk) for c in range(KC) for pk in parts], is_bf16=True)
            S.fence(["merged", "m1h"], R2N)
            slk = [("siluln", c, pk) for c in range(KC) for pk in parts]
            mxk = [("mixs", c, pk) for c in range(KC) for pk in parts]
            for half in range(2):
                s_ga = wblock(lambda s, half=half: [(slot_k(s, KC, 512), wsrc(w_in, KC, 3 * D + half * 512, 512))])
                s_co = wblock(lambda s, half=half: [(slot_k(s, KC, 512), wsrc(w_conv_out, KC, half * 512, 512))])
                for cc in range(4):
                    oc = half * 4 + cc
                    sv_g, sv_c = slot_k(s_ga, KC, 512), slot_k(s_co, KC, 512)
                    mm_group(0, parts, lambda k, sv_g=sv_g, cc=cc: sv_g[:, k, cc * 128:(cc + 1) * 128], lambda k, c0, n: h[:, k, c0:c0 + n], KC,
                             reads=[("wslot", s_ga)] + hkeys)
                    mm_group(1, parts, lambda k, sv_c=sv_c, cc=cc: sv_c[:, k, cc * 128:(cc + 1) * 128],
                             lambda k, c0, n: siluln[:, k, c0:c0 + n], KC, reads=[("wslot", s_co)] + slk)
                    i = oc % 2
                    for pk in parts:
                        if pk == "p":
                            g_ap, y_ap, s_ap, o_ap = pb_p(0), pb_p(1), P3(scr[i][:, 0:PWID]), P3(m1h[:, cc, 0:PWID])
                        else:
                            g_ap, y_ap, s_ap, o_ap = pb_s(0), pb_s(1), sscr[:, i, :], m1h[:, cc, PWID:NTC]
                        S.add("act", lambda e, g_ap=g_ap, s_ap=s_ap: e.activation(out=s_ap, in_=g_ap, func=AF.Sigmoid),
                              reads=[("PB", 0)], writes=SK(i, pk))
                        S.add("dve", lambda e, y_ap=y_ap, s_ap=s_ap, o_ap=o_ap: e.tensor_tensor(out=o_ap, in0=y_ap, in1=s_ap, op=ALU.mult),
                              reads=[("PB", 1)] + SK(i, pk), writes=[("m1h", cc, pk)])
                s_gb = wblock(lambda s, half=half: [(slot_k(s, KC, 512), wsrc(w_in, KC, 4 * D + half * 512, 512))])
                s_po = wblock(lambda s, half=half: [(slot_k(s, KC, 512), wsrc(w_pool_out, KC, half * 512, 512))])
                for cc in range(4):
                    oc = half * 4 + cc
                    sv_g, sv_c = slot_k(s_gb, KC, 512), slot_k(s_po, KC, 512)
                    mm_group(0, parts, lambda k, sv_g=sv_g, cc=cc: sv_g[:, k, cc * 128:(cc + 1) * 128], lambda k, c0, n: h[:, k, c0:c0 + n], KC,
                             reads=[("wslot", s_gb)] + hkeys)
                    mm_group(1, parts, lambda k, sv_c=sv_c, cc=cc: sv_c[:, k, cc * 128:(cc + 1) * 128],
                             lambda k, c0, n: mixs[:, k, c0:c0 + n], KC, reads=[("wslot", s_po)] + mxk)
                    i = oc % 2
                    for pk in parts:
                        if pk == "p":
                            g_ap, y_ap, s_ap, m_ap, o_ap = pb_p(0), pb_p(1), P3(scr[i][:, 0:PWID]), P3(m1h[:, cc, 0:PWID]), P3(merged[:, oc, 0:PWID])
                        else:
                            g_ap, y_ap, s_ap, m_ap, o_ap = pb_s(0), pb_s(1), sscr[:, i, :], m1h[:, cc, PWID:NTC], merged[:, oc, PWID:NTC]
                        S.add("act", lambda e, g_ap=g_ap, s_ap=s_ap: e.activation(out=s_ap, in_=g_ap, func=AF.Sigmoid),
                              reads=[("PB", 0)], writes=SK(i, pk))
                        S.add("dve", lambda e, y_ap=y_ap, s_ap=s_ap: e.tensor_tensor(out=s_ap, in0=y_ap, in1=s_ap, op=ALU.mult),
                              reads=[("PB", 1)], writes=SK(i, pk))
                        S.add("dve", lambda e, s_ap=s_ap, m_ap=m_ap, o_ap=o_ap: e.tensor_tensor(out=o_ap, in0=s_ap, in1=m_ap, op=ALU.add),
                              reads=SK(i, pk) + [("m1h", cc, pk)], writes=[("merged", oc, pk)])
            dbg_dump("merged", merged[:, :, :], [("merged", c, pk) for c in range(KC) for pk in parts], is_bf16=True)
            mgk = [("merged", c, pk) for c in range(KC) for pk in parts]
            S.fence(["vsq"], ["m1h", "upb", "tmb", "conv", "sstage", "scprod", "yst", "vsq"])
            for half in range(2):
                s_o = wblock(lambda s, half=half: [(slot_k(s, KC, 512), wsrc(w_out, KC, half * 512, 512))])
                for cc in range(4):
                    oc = half * 4 + cc
                    sv = slot_k(s_o, KC, 512)
                    b = oc % 2
                    mm_group(b, parts, lambda k, sv=sv, cc=cc: sv[:, k, cc * 128:(cc + 1) * 128], lambda k, c0, n: merged[:, k, c0:c0 + n], KC,
                             reads=[("wslot", s_o)] + mgk)
                    resid_add(parts, b, oc, 5)
            layer_norm(parts, lambda c: xT[:, c, :], xT[:, :, PWID:NTC], lambda c, pk: [("xT", c, pk)], EPS_DN, "ln2_g",
                       *ln_outs_resid("ln2_b", 6, True), next_af=AF.Silu, pre=("p" in parts))
            dbg_dump("x2", xT[:, :, :], [("xT", c, pk) for c in range(KC) for pk in parts])

        def phase_out(parts, row0):
            S.fence(["yst"], R2N)
            S.fence(["bank"], ["PB"])
            n = 0
            for pk in parts:
                if pk == "p":
                    for r in range(8):
                        bp = (n % 3) * 2
                        sl = n % 4
                        n += 1

                        def tr(e, r=r, bp=bp):
                            ins = None
                            for c in range(KC):
                                ins = e.transpose(ps[:, bp + c // 4, (c % 4) * 128:(c % 4 + 1) * 128], xT[:, c, r * 128:(r + 1) * 128], ident[:, :])
                            return ins
                        S.add("pe", tr, reads=[("xT", c, "p") for c in range(KC)] + [("const",)],
                              writes=[("bank", bp), ("bank", bp + 1)])
                        if r % 2 == 0:
                            S.add("act", lambda e, bp=bp, sl=sl: e.activation(out=P3(yst[:, sl, :]), in_=ps[:, bp:bp + 2, :], func=AF.Copy),
                                  reads=[("bank", bp), ("bank", bp + 1)], writes=[("yst", sl)])
                        else:
                            S.add("dve", lambda e, bp=bp, sl=sl: e.tensor_copy(out=P3(yst[:, sl, :]), in_=ps[:, bp:bp + 2, :]),
                                  reads=[("bank", bp), ("bank", bp + 1)], writes=[("yst", sl)])
                        dma_op("sp", "o_y%d" % sl, [(y_p[row0 + r * 128: row0 + (r + 1) * 128, :], yst[:, sl, :])], reads=[("yst", sl)], out=True)
                else:
                    def tr(e):
                        ins = None
                        for c in range(KC):
                            ins = e.matmul(ps[0:16, 6 + c // 4, (c % 4) * 128:(c % 4 + 1) * 128], xT[:, c, PWID:NTC], ident[:, :], start=True, stop=True)
                        return ins
                    S.add("pe", tr, reads=[("xT", c, "s") for c in range(KC)] + [("const",)], writes=[("bank", 6), ("bank", 7)])
                    S.add("act", lambda e: e.activation(out=P3(scr[1][0:16, 0:D]), in_=ps[0:16, 6:8, :], func=AF.Copy),
                          reads=[("bank", 6), ("bank", 7)], writes=SCR(1))
                    dma_op("sp", "o_ys", [(y_s[:, :], scr[1][0:16, 0:D])], reads=SCR(1), out=True)
            S.fence(["PB"], ["bank"])

        def phase_final_states():
            def tr(e):
                ins = None
                for c in range(KC):
                    ins = e.matmul(ps[0:30, 6 + c // 4, (c % 4) * 128:(c % 4 + 1) * 128], gl_halo[:, c, :], ident[:, :], start=True, stop=True)
                return ins
            S.add("pe", tr, reads=[("gl_halo",), ("const",)], writes=[("bank", 6), ("bank", 7)])
            S.add("act", lambda e: e.activation(out=P3(scr[0][0:30, 0:D]), in_=ps[0:30, 6:8, :], func=AF.Copy),
                  reads=[("bank", 6), ("bank", 7)], writes=SCR(0))
            dma_op("sp", "o_fs", [(ncp[:, :], scr[0][0:30, 0:D])], reads=SCR(0), out=True)

            def tr2(e):
                ins = None
                for c in range(KC):
                    ins = e.matmul(ps[0:15, 6 + c // 4, (c % 4) * 128:(c % 4 + 1) * 128], up_halo[:, c, :], ident[:, :], start=True, stop=True)
                return ins
            S.add("pe", tr2, reads=[("up_halo", c) for c in range(KC)] + [("const",)], writes=[("bank", 6), ("bank", 7)])
            S.add("act", lambda e: e.activation(out=P3(scr[1][0:15, 0:D]), in_=ps[0:15, 6:8, :], func=AF.Copy),
                  reads=[("bank", 6), ("bank", 7)], writes=SCR(1))
            dma_op("sp", "o_fs", [(npp[:, :], scr[1][0:15, 0:D])], reads=SCR(1), out=True)

        stop = None
        stop_g = 0
        if debug and ":" in debug:
            parts_ = debug.split(":")
            debug, stop = parts_[0], parts_[1]
            if len(parts_) > 2:
                stop_g = int(parts_[2])
        phase_setup()
        prow = 0
        pg = 0
        x_load(groups_cfg[0][1], 0)
        for gi, (gname, parts) in enumerate(groups_cfg):
            has_p = "p" in parts
            phase_x(parts, prow)
            if stop == "x" and gi == stop_g:
                break
            phase_ffn(parts, 0, 2, "ln1_g", "ln1_b", 3, True, "1")
            if stop == "ffn1" and gi == stop_g:
                break
            phase_mixer(parts, first_prompt=(pg == 0))
            if stop == "mixer" and gi == stop_g:
                break
            phase_ffn(parts, 1, 8, "ln3_g", "ln3_b", 0, False, "3")
            if stop == "ffn3" and gi == stop_g:
                break
            if gi + 1 < len(groups_cfg):
                x_load(groups_cfg[gi + 1][1], prow + (PWID if has_p else 0))
            phase_out(parts, prow)
            if stop == "out" and gi == stop_g:
                break
            if has_p:
                prow += PWID
                pg += 1
        if stop is None:
            phase_final_states()

        final = S.add("sp", lambda e: None, extra=OUT_OPS)
        S.finalize()
        prog_sems = {k: es.enter_context(nc.semaphore("prog_" + k)) for k in Sched.ENGS}
        for n in sem_names:
            SEMS[n] = es.enter_context(nc.semaphore("d_" + n))
        with nc.Block() as block:
            @block.tensor
            def _(e):
                S.emit("pe", e, prog_sems, SEMS)

            @block.scalar
            def _(e):
                S.emit("act", e, prog_sems, SEMS)

            @block.vector
            def _(e):
                S.emit("dve", e, prog_sems, SEMS)

            @block.gpsimd
            def _(e):
                S.emit("pool", e, prog_sems, SEMS)

            @block.sync
            def _(e):
                S.ops["sp"].remove(final)
                S.emit("sp", e, prog_sems, SEMS)
                for d in final.deps:
                    e.wait_ge(SEMS[d.dma_sem], S.dma_counts[d.dma_sem])
    return nc


_NC_CACHE = {}


def _consts():
    ident = np.eye(128, dtype=np.float32)
    selc = np.zeros((120, 4, 16), np.float32)
    for q in range(4):
        for s in range(4):
            selc[s * 30:(s + 1) * 30, q, 4 * q + s] = 1.0
    selp = np.zeros((120, 2, 4, 16), np.float32)
    for q in range(2):
        for g, w in enumerate(POOL_W):
            for s in range(8):
                for k in range(15):
                    if k >= 16 - w:
                        selp[s * 15 + k, q, g, 8 * q + s] = 1.0 / w
    invc = np.zeros((128, 4, 16), np.float32)
    for g, w in enumerate(POOL_W):
        for t in range(16):
            invc[:, g, t] = 1.0 / min(w, t + 1)
    return ident, selc, selp, invc


def make_in_maps(x_prompt, x_sample, state_conv, state_pool, c_prompt, c_sample,
                 w_ada, b_ada, ffn1_w_in, ffn1_w_out, ln1_g, ln1_b,
                 w_in, conv_w, conv_b, conv_ln_g, conv_ln_b, w_conv_out,
                 pool_w, pool_scale, w_pool_out, w_out, ln2_g, ln2_b,
                 ffn2_w_in, ffn2_w_out, ln3_g, ln3_b):
    f = lambda a: np.ascontiguousarray(np.asarray(a, dtype=np.float32))
    ident, selc, selp, invc = _consts()
    pvec = np.concatenate([f(v)[0].reshape(8, 128) for v in
                           (ln1_g, ln1_b, ln2_g, ln2_b, ln3_g, ln3_b, conv_b, conv_ln_g, conv_ln_b, pool_scale)], axis=0)
    cb_row = np.stack([f(conv_w)[0, 30], f(conv_b)[0]], axis=0)
    shared = {
        "w_ada": f(w_ada)[0], "b_ada": f(b_ada)[0].reshape(72, 128),
        "ffn1_w_in": f(ffn1_w_in)[0], "ffn2_w_in": f(ffn2_w_in)[0],
        "ffn1_w_out": f(ffn1_w_out)[0], "ffn2_w_out": f(ffn2_w_out)[0],
        "w_in": f(w_in)[0], "conv_w": f(conv_w)[0], "w_conv_out": f(w_conv_out)[0],
        "pool_w": f(pool_w)[0], "w_pool_out": f(w_pool_out)[0], "w_out": f(w_out)[0],
        "pvec": np.ascontiguousarray(pvec), "cb_row": np.ascontiguousarray(cb_row),
        "ident": ident, "selc": selc, "selp": selp, "invc": invc,
    }
    xp, xs = f(x_prompt), f(x_sample)
    sc, sp = f(state_conv)[0], f(state_pool)[0]
    cp, cs = f(c_prompt), f(c_sample)
    in_maps = []
    for i in range(8):
        m = dict(shared)
        m["x_p"] = xp[i]
        m["x_s"] = np.ascontiguousarray(xs[16 * i:16 * i + 16, 0, :])
        m["sconv"] = np.ascontiguousarray(sc[16 * i:16 * i + 16])
        m["spool"] = np.ascontiguousarray(sp[16 * i:16 * i + 16])
        m["c_all"] = np.ascontiguousarray(np.concatenate([cp[i:i + 1], cs[16 * i:16 * i + 16]], axis=0))
        in_maps.append(m)
    return in_maps


def kernel(**inputs):
    if "nc" not in _NC_CACHE:
        _NC_CACHE["nc"] = build_program()
    nc = _NC_CACHE["nc"]
    in_maps = make_in_maps(**inputs)
    res = run_bass_kernel_spmd(nc, in_maps, core_ids=list(range(8)))
    R = res.results
    y_prompt = np.stack([R[i]["y_p"] for i in range(8)], axis=0)
    y_sample = np.concatenate([R[i]["y_s"] for i in range(8)], axis=0)[:, None, :]
    ncp = np.stack([R[i]["ncp"] for i in range(8)], axis=0)[None]
    npp = np.stack([R[i]["npp"] for i in range(8)], axis=0)[None]
    ncs = np.concatenate([R[i]["ncs"] for i in range(8)], axis=0)[None]
    nps = np.concatenate([R[i]["nps"] for i in range(8)], axis=0)[None]
    return (y_prompt.astype(np.float32), y_sample.astype(np.float32), ncp.astype(np.float32),
            npp.astype(np.float32), ncs.astype(np.float32), nps.astype(np.float32))
```

```python
import numpy as np
from contextlib import ExitStack
import concourse.bass as bass
import concourse.mybir as mybir
from concourse.bass_utils import run_bass_kernel_spmd

F32 = mybir.dt.float32
BF16 = mybir.dt.bfloat16
AF = mybir.ActivationFunctionType
ALU = mybir.AluOpType
AX = mybir.AxisListType

D = 1024
KC = 8
DFF = 2816
FC = 22
PWID = 1024
SWID = 16
NTC = PWID + SWID
EPS = 1e-5
ALPHA = 2.0 ** 0.25
EPS_DN = EPS / (ALPHA * ALPHA)
POOL_W = (2, 4, 8, 16)
NSLOT = 4
OUT_IN_LN = True
WARM_C = 5
WARM_N = 20
SLOT_ELEMS = 4096


class Op:
    __slots__ = ("eng", "fn", "deps", "marked", "count", "dma_sem", "dma_val", "idx")

    def __init__(self, eng, fn, deps):
        self.eng = eng
        self.fn = fn
        self.deps = deps
        self.marked = False
        self.count = 0
        self.dma_sem = None
        self.dma_val = 0


class Sched:
    ENGS = ("pe", "act", "dve", "pool", "sp")

    def __init__(self):
        self.ops = {e: [] for e in self.ENGS}
        self.last_w = {}
        self.readers = {}
        self.dma_counts = {}
        self.all_ops = []
        self.known = set()
        self.pending = {}

    def add(self, eng, fn, reads=(), writes=(), dma_sem=None, n_dma=0, extra=()):
        deps = []
        seen = set()

        def push(o):
            if o is not None and id(o) not in seen:
                seen.add(id(o))
                deps.append(o)

        for k in list(reads) + list(writes):
            if k not in self.known:
                self.known.add(k)
                if k[0] in self.pending:
                    self.readers.setdefault(k, []).extend(self.pending[k[0]])
        for k in reads:
            push(self.last_w.get(k))
        for k in writes:
            push(self.last_w.get(k))
            for r in self.readers.get(k, ()):
                push(r)
        for o in extra:
            push(o)
        op = Op(eng, fn, deps)
        if dma_sem is not None:
            c = self.dma_counts.get(dma_sem, 0) + 16 * n_dma
            self.dma_counts[dma_sem] = c
            op.dma_sem = dma_sem
            op.dma_val = c
        for k in reads:
            self.readers.setdefault(k, []).append(op)
        for k in writes:
            self.last_w[k] = op
            self.readers[k] = []
        self.ops[eng].append(op)
        self.all_ops.append(op)
        return op

    def fence(self, new_names, old_names):
        olds = []
        seen = set()
        for k in list(self.known):
            if k[0] in old_names:
                for o in [self.last_w.get(k)] + list(self.readers.get(k, ())):
                    if o is not None and id(o) not in seen:
                        seen.add(id(o))
                        olds.append(o)
        for n in new_names:
            self.pending[n] = list(olds)
        for k in list(self.known):
            if k[0] in new_names:
                self.readers.setdefault(k, []).extend(olds)

    def finalize(self):
        for op in self.all_ops:
            for d in op.deps:
                if d.dma_sem is None:
                    if d.eng == "pe" and op.eng == "pe":
                        continue
                    d.marked = True
        for e in self.ENGS:
            c = 0
            for op in self.ops[e]:
                if op.marked:
                    c += 1
                    op.count = c

    def emit(self, eng, handle, prog_sems, dma_sems):
        waited = {}
        for op in self.ops[eng]:
            for d in op.deps:
                if d.dma_sem is not None:
                    key, val, sem = ("dma", d.dma_sem), d.dma_val, dma_sems[d.dma_sem]
                else:
                    if d.eng == "pe" and eng == "pe":
                        continue
                    key, val, sem = ("eng", d.eng), d.count, prog_sems[d.eng]
                if waited.get(key, 0) < val:
                    handle.wait_ge(sem, val)
                    waited[key] = val
            ins = op.fn(handle)
            if op.marked:
                ins.then_inc(prog_sems[eng], 1)


def build_program(groups_cfg=None, debug=None):
    nc = bass.Bass("TRN2", target_bir_lowering=False)
    S = Sched()

    def din(name, shape):
        return nc.dram_tensor(name, list(shape), F32, kind="ExternalInput").ap()

    def dout(name, shape):
        return nc.dram_tensor(name, list(shape), F32, kind="ExternalOutput").ap()

    x_p = din("x_p", [2048, D])
    x_s = din("x_s", [SWID, D])
    sconv = din("sconv", [SWID, 30, D])
    spool = din("spool", [SWID, 15, D])
    c_all = din("c_all", [17, D])
    w_ada = din("w_ada", [D, 9 * D])
    b_ada = din("b_ada", [72, 128])
    ffn_w_in = [din("ffn1_w_in", [D, 2 * DFF]), din("ffn2_w_in", [D, 2 * DFF])]
    ffn_w_out = [din("ffn1_w_out", [DFF, D]), din("ffn2_w_out", [DFF, D])]
    w_in = din("w_in", [D, 5 * D])
    conv_w = din("conv_w", [31, D])
    w_conv_out = din("w_conv_out", [D, D])
    pool_w = din("pool_w", [4, 256, 256])
    w_pool_out = din("w_pool_out", [D, D])
    w_out = din("w_out", [D, D])
    pvec = din("pvec", [80, 128])
    cb_row = din("cb_row", [2, D])
    ident_d = din("ident", [128, 128])
    selc_d = din("selc", [120, 4, 16])
    selp_d = din("selp", [120, 2, 4, 16])
    invc_d = din("invc", [128, 4, 16])

    y_p = dout("y_p", [2048, D])
    y_s = dout("y_s", [SWID, D])
    ncp = dout("ncp", [30, D])
    npp = dout("npp", [15, D])
    ncs = dout("ncs", [SWID, 30, D])
    nps = dout("nps", [SWID, 15, D])
    dbg = dout("dbg", [128, KC * NTC]) if debug else None

    PV = {n: i for i, n in enumerate(
        ["ln1_g", "ln1_b", "ln2_g", "ln2_b", "ln3_g", "ln3_b", "conv_b", "cln_g", "cln_b", "pscale"])}
    R1N = ["hid", "xin", "xs_in", "glu", "gluhalo", "siluln", "mixs", "pooled"]
    R2N = ["conv", "merged", "m1h", "upb", "tmb", "sstage", "scprod", "setupR2", "yst", "vsq"]

    es = ExitStack()
    with es:
        def sb(name, shape, dt=F32):
            return es.enter_context(nc.sbuf_tensor(name, list(shape), dt))

        ident = sb("ident_sb", [128, 128])
        ones_bf = sb("ones_bf", [128, 128], BF16)
        pT = sb("pT", [128, 80])
        baT = sb("baT", [128, 72])
        cwT = sb("cwT", [128, 248])
        modT = sb("modT", [128, 72, 17])
        cT = sb("cT", [128, KC, 17], BF16)
        invc = sb("invc_sb", [128, 4, 16])
        selc = sb("selc_sb", [120, 4, 16])
        selp = sb("selp_sb", [120, 2, 4, 16])
        cbrow = sb("cbrow", [16, 2, D])
        xT = sb("xT", [128, KC, NTC])
        h = sb("h", [128, KC, NTC], BF16)
        R1 = sb("R1", [128, 11440])
        R2 = sb("R2", [128, KC * NTC])
        scr = [sb("scr0", [128, NTC]), sb("scr1", [128, NTC])]
        ring = sb("ring", [128, NSLOT, SLOT_ELEMS], BF16)
        poolw = sb("poolw", [128, 4, 2, 256], BF16)
        gl_halo = sb("gl_halo", [128, KC, 30])
        up_halo = sb("up_halo", [128, KC, 15])
        stmp = sb("stmp", [128, 8, 16])
        rstd_sb = sb("rstd_sb", [128, NTC])
        glus = sb("glus", [128, KC, SWID])
        lns = sb("lns", [128, 2, KC, SWID], BF16)
        epsT = sb("epsT", [128, 2])
        sscr = sb("sscr", [128, 2, SWID])
        jnk = sb("jnk", [128, 2])
        diag = sb("diag", [128, 8, 128], BF16)
        tmA = sb("tmA", [16, D])
        mprev = sb("mprev_sb", [16, D])
        ps = es.enter_context(nc.psum_tensor("ps", [128, 8, 512], F32))

        hid = R1[:, :].bitcast(BF16).rearrange("p (j n) -> p j n", n=NTC)
        xin = R1[:, 0:8192].rearrange("p (s r d) -> p s r d", s=2, r=4)
        xs_in = R1[:, 8192:9216]
        yst = R2[:, 0:4096].rearrange("p (s d) -> p s d", s=4)
        glu = R1[:, 0:KC * 1070].rearrange("p (c n) -> p c n", n=1070)
        glub = R1[:, 0:KC * 535].bitcast(BF16).rearrange("p (c n) -> p c n", n=1070)
        siluln = R1[:, 0:4160].bitcast(BF16).rearrange("p (c n) -> p c n", n=NTC)
        mixs = R1[:, 4160:8320].bitcast(BF16).rearrange("p (c n) -> p c n", n=NTC)
        pooled = R1[:, 8320:11440].bitcast(BF16).rearrange("p (c n) -> p c n", n=NTC)
        conv = R2[:, :].rearrange("p (c n) -> p c n", n=NTC)
        merged = R2[:, 0:4160].bitcast(BF16).rearrange("p (c n) -> p c n", n=NTC)
        m1h = R2[:, 4160:8320].rearrange("p (c n) -> p c n", n=NTC)
        vsqf = R2[:, 4160:8320].bitcast(BF16).rearrange("p (c n) -> p c n", n=NTC)
        upb = R2[:, 0:6 * 1056].rearrange("p (b n) -> p b n", n=1056)
        tmbuf = R2[:, 6400:8320].rearrange("p (b n) -> p b n", n=384)
        sc_st = R2[:, 0:4096].rearrange("p (q d) -> p q d", q=4)
        wrep = R2[:, 4096:5120]
        sp_st = R2[:, 5120:7168].rearrange("p (q d) -> p q d", q=2)

        def pb_p(b):
            return ps[:, 3 * b:3 * b + 2, :]

        def pb_s(b):
            return ps[:, 3 * b + 2, 0:SWID]

        sem_names = []
        SEMS = {}
        OUT_OPS = []

        def semname(n):
            if n not in sem_names:
                sem_names.append(n)
            return n

        if groups_cfg is None:
            groups_cfg = [("A", ["p"]), ("B", ["p", "s"])]

        def P3(ap2):
            return ap2.rearrange("p (t n) -> p t n", n=512)

        def SCR(i):
            return [("scr", i, "a"), ("scr", i, "b")]

        def SK(i, pk):
            return SCR(i) if pk == "p" else [("sscr", i)]

        def dma_op(eng, sem, pairs, reads=(), writes=(), out=False):
            sem = semname(sem)

            def fn(e):
                ins = None
                for (dst, src) in pairs:
                    ins = e.dma_start(out=dst, in_=src)
                    ins.then_inc(SEMS[sem], 16)
                return ins
            op = S.add(eng, fn, reads=reads, writes=writes, dma_sem=sem, n_dma=len(pairs))
            if out:
                OUT_OPS.append(op)
            return op

        def dbg_dump(name, ap, keys, is_bf16=False):
            if debug != name:
                return
            n = ap.shape[-1] if len(ap.shape) == 2 else None
            if len(ap.shape) == 3:
                dst = dbg[:, 0:ap.shape[1] * ap.shape[2]].rearrange("p (c n) -> p c n", n=ap.shape[2])
            else:
                dst = dbg[:, 0:n]
            dma_op("pool" if is_bf16 else "sp", "dbg", [(dst, ap)], reads=keys, out=True)

        def mm_group(pbuf, parts, lhs_fn, rhs_fn, nk, reads):
            tiles = []
            if "s" in parts:
                tiles.append((ps[:, 3 * pbuf + 2, 0:SWID], PWID, SWID))
            if "p" in parts:
                tiles.append((ps[:, 3 * pbuf, :], 0, 512))
                tiles.append((ps[:, 3 * pbuf + 1, :], 512, 512))

            def fn(e):
                ins = None
                for k in range(nk):
                    lt = lhs_fn(k)
                    for (o, c0, n) in tiles:
                        ins = e.matmul(o, lt, rhs_fn(k, c0, n), start=(k == 0), stop=(k == nk - 1))
                return ins
            return S.add("pe", fn, reads=reads, writes=[("PB", pbuf)])

        ring_ctr = [0]

        def slot_k(s, kcn, cw):
            return ring[:, s, 0:kcn * cw].rearrange("p (k n) -> p k n", n=cw)

        def wsrc(w, kcn, c0, cw):
            return w.rearrange("(k p) n -> p k n", p=128)[:, 0:kcn, c0:c0 + cw]

        def wblock(dmas_fn):
            s = ring_ctr[0] % NSLOT
            ring_ctr[0] += 1
            dma_op("pool", "w%d" % s, dmas_fn(s), writes=[("wslot", s)])
            return s

        def modp(m, c):
            return modT[:, m * 8 + c, 0:1]

        def mods(m, c):
            return modT[:, m * 8 + c, 1:17]

        def pv(name, c):
            i = PV[name] * 8 + c
            return pT[:, i:i + 1]

        def pkeys(parts, name, c):
            return [(name, c, pk) for pk in parts]

        ada_pending = list(range(4, 18))
        DERIVE = {1: ("add", 1.0), 4: ("add", 1.0), 7: ("add", 1.0), 2: ("mul", 0.5 / ALPHA), 5: ("mul", 1.0 / ALPHA), 8: ("mul", 0.5 / ALPHA)}

        def ada_block(blk):
            s = wblock(lambda s, blk=blk: [(slot_k(s, KC, 512), wsrc(w_ada, KC, blk * 512, 512))])
            bank = 6 + (blk % 2)

            def fn(e, s=s, bank=bank):
                ins = None
                for m in range(4):
                    for k in range(KC):
                        ins = e.matmul(ps[:, bank, m * 17:(m + 1) * 17], slot_k(s, KC, 512)[:, k, m * 128:(m + 1) * 128],
                                       cT[:, k, :], start=(k == 0), stop=(k == KC - 1))
                return ins
            S.add("pe", fn, reads=[("wslot", s), ("cT",)], writes=[("bank", bank)])

            def ev(e, blk=blk, bank=bank):
                return e.tensor_tensor(out=modT[:, blk * 4:(blk + 1) * 4, :],
                                       in0=ps[:, bank, 0:68].rearrange("p (m t) -> p m t", t=17),
                                       in1=baT[:, blk * 4:(blk + 1) * 4].unsqueeze(2).to_broadcast([128, 4, 17]),
                                       op=ALU.add)
            S.add("dve", ev, reads=[("bank", bank), ("params",)], writes=[("mod",)])
            if blk % 2 == 1 and (blk // 2) in DERIVE:
                m = blk // 2
                kind, f = DERIVE[m]
                if kind == "add":
                    S.add("dve", lambda e, m=m, f=f: e.tensor_scalar_add(out=modT[:, m * 8:(m + 1) * 8, :], in0=modT[:, m * 8:(m + 1) * 8, :],
                                                                         scalar1=f), writes=[("mod",)])
                else:
                    S.add("dve", lambda e, m=m, f=f: e.tensor_scalar_mul(out=modT[:, m * 8:(m + 1) * 8, :], in0=modT[:, m * 8:(m + 1) * 8, :],
                                                                         scalar1=f), writes=[("mod",)])

        def ada_more(n=1):
            for _ in range(n):
                if ada_pending:
                    ada_block(ada_pending.pop(0))

        def phase_setup():
            cwv = conv_w.rearrange("k (c p) -> k c p", p=128)
            pairs = [
                (ident[:, :], ident_d[:, :]),
                (invc[:, :, :], invc_d[:, :, :]),
                (selc[:, :, :], selc_d[:, :, :]),
                (selp[:, :, :, :], selp_d[:, :, :, :]),
                (scr[0][0:80, 0:128], pvec[:, :]),
                (scr[0][0:72, 128:256], b_ada[:, :]),
                (scr[1][0:17, 0:D], c_all[:, :]),
            ]
            for k in range(31):
                half, r0 = (0, k * 8) if k < 16 else (1, (k - 16) * 8)
                pairs.append((R2[r0:r0 + 8, half * 128:(half + 1) * 128], cwv[k, :, :]))
            for r in range(2):
                pairs.append((cbrow[:, r, :], cb_row[r:r + 1, :].partition_broadcast(16)))
            dma_op("sp", "setup", pairs, writes=SCR(0) + SCR(1) + [("setupR2",), ("const",)])
            S.add("dve", lambda e: e.memset(ones_bf[:, :], 1.0 / D), writes=[("ones",)])
            S.add("dve", lambda e: e.memset(epsT[:, 0:1], EPS), writes=[("ones",)])
            S.add("dve", lambda e: e.memset(epsT[:, 1:2], EPS_DN), writes=[("ones",)])

            def t_params(e):
                e.transpose(ps[:, 6, 0:80], scr[0][0:80, 0:128], ident[0:80, 0:80])
                e.transpose(ps[:, 6, 80:152], scr[0][0:72, 128:256], ident[0:72, 0:72])
                e.transpose(ps[:, 6, 152:280], R2[0:128, 0:128], ident[:, :])
                ins = e.transpose(ps[:, 6, 280:400], R2[0:120, 128:256], ident[0:120, 0:120])
                for c in range(KC):
                    ins = e.transpose(ps[:, 7, c * 17:(c + 1) * 17], scr[1][0:17, c * 128:(c + 1) * 128], ident[0:17, 0:17])
                return ins
            S.add("pe", t_params, reads=SCR(0) + SCR(1) + [("setupR2",), ("const",)], writes=[("bank", 6), ("bank", 7)])
            S.add("dve", lambda e: e.tensor_copy(out=pT[:, :], in_=ps[:, 6, 0:80]), reads=[("bank", 6)], writes=[("params",)])
            S.add("dve", lambda e: e.tensor_copy(out=baT[:, :], in_=ps[:, 6, 80:152]), reads=[("bank", 6)], writes=[("params",)])
            S.add("dve", lambda e: e.tensor_copy(out=cwT[:, :], in_=ps[:, 6, 152:400]), reads=[("bank", 6)], writes=[("params",)])
            S.add("act", lambda e: e.activation(out=cT[:, :, :], in_=ps[:, 7, 0:KC * 17].rearrange("p (c t) -> p c t", t=17),
                                                func=AF.Silu), reads=[("bank", 7)], writes=[("cT",)])
            for blk in range(4):
                ada_block(blk)
            dma_op("pool", "poolw", [(poolw[:, g, :, :], pool_w[g].rearrange("(i p) j -> p i j", p=128)) for g in range(4)],
                   writes=[("poolw",)])
            dbg_dump("modT", modT[:, :, :].rearrange("p m t -> p (m t)"), [("mod",)])

        def x_load(parts, row0):
            S.fence(["xin", "xs_in"], R1N)
            for pk in parts:
                if pk == "p":
                    for hf in range(2):
                        src = x_p[row0 + hf * 512: row0 + (hf + 1) * 512, :].rearrange("(r p) d -> p r d", p=128)
                        dma_op("sp", "xin%d" % hf, [(xin[:, hf, :, :], src)], writes=[("xin", hf)])
                else:
                    dma_op("sp", "xsin", [(xs_in[0:SWID, :], x_s[:, :])], writes=[("xs_in",)])

        def phase_x(parts, row0):
            S.fence(["bank"], ["PB"])
            bank_ctr = [0]
            for pk in parts:
                if pk == "p":
                    for hf in range(2):
                        for c in range(KC):
                            bank = bank_ctr[0] % 6
                            bank_ctr[0] += 1

                            def tr(e, hf=hf, c=c, bank=bank):
                                ins = None
                                for r in range(4):
                                    ins = e.transpose(ps[:, bank, r * 128:(r + 1) * 128], xin[:, hf, r, c * 128:(c + 1) * 128], ident[:, :])
                                return ins
                            S.add("pe", tr, reads=[("xin", hf), ("const",)], writes=[("bank", bank)])
                            cs = slice(hf * 512, (hf + 1) * 512)
                            S.add("act", lambda e, c=c, bank=bank, cs=cs: e.activation(out=xT[:, c, cs], in_=ps[:, bank, :], func=AF.Copy),
                                  reads=[("bank", bank)], writes=[("xT", c, "p")])
                            S.add("dve", lambda e, c=c, cs=cs: e.tensor_scalar(
                                out=h[:, c, cs], in0=xT[:, c, cs], scalar1=modp(1, c), scalar2=modp(0, c), op0=ALU.mult, op1=ALU.add),
                                reads=[("xT", c, "p"), ("mod",)], writes=[("h", c, "p")])
                else:
                    bank = bank_ctr[0] % 6
                    bank_ctr[0] += 1

                    def tr(e, bank=bank):
                        ins = None
                        for c in range(KC):
                            ins = e.transpose(ps[:, bank, c * SWID:(c + 1) * SWID], xs_in[0:SWID, c * 128:(c + 1) * 128], ident[0:SWID, 0:SWID])
                        return ins
                    S.add("pe", tr, reads=[("xs_in",), ("const",)], writes=[("bank", bank)])
                    pv3 = ps[:, bank, 0:KC * SWID].rearrange("p (c t) -> p c t", t=SWID)
                    S.add("act", lambda e, pv3=pv3: e.activation(out=xT[:, :, PWID:NTC], in_=pv3, func=AF.Copy),
                          reads=[("bank", bank)], writes=[("xT", c, "s") for c in range(KC)])
                    S.add("dve", lambda e: e.tensor_tensor(out=stmp[:, :, :], in0=xT[:, :, PWID:NTC], in1=modT[:, 8:16, 1:17], op=ALU.mult),
                          reads=[("xT", c, "s") for c in range(KC)] + [("mod",)], writes=[("stmp",)])
                    S.add("dve", lambda e: e.tensor_tensor(out=h[:, :, PWID:NTC], in0=stmp[:, :, :], in1=modT[:, 0:8, 1:17], op=ALU.add),
                          reads=[("stmp",), ("mod",)], writes=[("h", c, "s") for c in range(KC)])
            S.fence(["PB"], ["bank"])
            dbg_dump("xT0", xT[:, :, :], [("xT", c, pk) for c in range(KC) for pk in parts])
            dbg_dump("h0", h[:, :, :], [("h", c, pk) for c in range(KC) for pk in parts], is_bf16=True)

        def layer_norm(parts, src_fn, src3_s, src_keys_fn, eps, gname, outs_p, outs_s, next_af=None, pre=False):
            has_p, has_s = "p" in parts, "s" in parts
            gi0 = PV[gname] * 8
            MB = [("bank", 6), ("bank", 7)]
            if has_s:
                skeys = [k for c in range(KC) for k in src_keys_fn(c, "s")]
                S.add("act", lambda e: e.activation(out=lns[:, 0, :, :], in_=src3_s, func=AF.Copy), reads=skeys, writes=[("lns", 0)])
                S.add("dve", lambda e: e.tensor_tensor(out=lns[:, 1, :, :], in0=src3_s, in1=src3_s, op=ALU.mult), reads=skeys, writes=[("lns", 1)])

                def st_s(e):
                    ins = None
                    for c in range(KC):
                        ins = e.matmul(ps[:, 2, 0:SWID], ones_bf[:, :], lns[:, 0, c, :], start=(c == 0), stop=(c == KC - 1))
                    for c in range(KC):
                        ins = e.matmul(ps[:, 2, SWID:2 * SWID], ones_bf[:, :], lns[:, 1, c, :], start=(c == 0), stop=(c == KC - 1))
                    return ins
                if not has_p:
                    S.add("pe", st_s, reads=[("lns", 0), ("lns", 1), ("ones",)], writes=[("PB", 0)])
            if has_p:
                for c in range(KC):
                    i = c % 2
                    if pre:
                        vbf, vsq = h[:, c, :], vsqf[:, c, :]
                        rk = [("h", c, "p"), ("vsq", c)]
                    else:
                        vbf = scr[i][:, 0:520].bitcast(BF16)
                        vsq = scr[i][:, 520:1040].bitcast(BF16)
                        rk = SCR(i)
                        S.add("act", lambda e, c=c, vbf=vbf: e.activation(out=vbf[:, 0:PWID], in_=src_fn(c)[:, 0:PWID], func=AF.Copy),
                              reads=src_keys_fn(c, "p"), writes=[("scr", i, "a")])
                        S.add("dve", lambda e, c=c, vsq=vsq: e.tensor_tensor(out=vsq[:, 0:PWID], in0=src_fn(c)[:, 0:PWID], in1=src_fn(c)[:, 0:PWID],
                                                                            op=ALU.mult),
                              reads=src_keys_fn(c, "p"), writes=[("scr", i, "b")])

                    def st(e, c=c, vbf=vbf, vsq=vsq):
                        ins = None
                        for (bo, c0) in ((0, 0), (1, 512)):
                            e.matmul(ps[:, 6 + bo, :], ones_bf[:, :], vbf[:, c0:c0 + 512], start=(c == 0), stop=(c == KC - 1))
                            ins = e.matmul(ps[:, bo, :], ones_bf[:, :], vsq[:, c0:c0 + 512], start=(c == 0), stop=(c == KC - 1))
                        return ins
                    S.add("pe", st, reads=rk + [("ones",)], writes=MB + [("PB", 0)])
                if has_s:
                    S.add("pe", st_s, reads=[("lns", 0), ("lns", 1), ("ones",)], writes=[("PB", 0)])
                S.add("act", lambda e: e.activation(out=jnk[:, 0:1], in_=epsT[:, 0:1], func=AF.Ln), reads=[("ones",)], writes=[("jnk",)])
            for pk in parts:
                if pk == "p":
                    m_ap, r_ap, t_ap, rs_ap = ps[:, 6:8, :], ps[:, 0:2, :], P3(scr[0][:, 0:PWID]), P3(rstd_sb[:, 0:PWID])
                    mk = MB
                else:
                    m_ap, r_ap, t_ap, rs_ap = ps[:, 2, 0:SWID], ps[:, 2, SWID:2 * SWID], sscr[:, 0, :], rstd_sb[:, PWID:NTC]
                    mk = [("PB", 0)]
                S.add("act", lambda e, m_ap=m_ap, t_ap=t_ap: e.activation(out=t_ap, in_=m_ap, func=AF.Square),
                      reads=mk, writes=SK(0, pk))
                S.add("dve", lambda e, r_ap=r_ap, t_ap=t_ap, rs_ap=rs_ap: e.tensor_tensor(out=rs_ap, in0=r_ap, in1=t_ap, op=ALU.subtract),
                      reads=SK(0, pk) + [("PB", 0)], writes=[("rstd", pk)])
            for pk in parts:
                rs_ap = P3(rstd_sb[:, 0:PWID]) if pk == "p" else rstd_sb[:, PWID:NTC]
                S.add("act", lambda e, rs_ap=rs_ap: e.activation(out=rs_ap, in_=rs_ap, func=AF.Ln,
                                                                 bias=epsT[:, (0 if eps == EPS else 1):(1 if eps == EPS else 2)], scale=1.0),
                      reads=[("ones",)], writes=[("rstd", pk)])
                S.add("act", lambda e, rs_ap=rs_ap: e.activation(out=rs_ap, in_=rs_ap, func=AF.Exp, scale=-0.5),
                      writes=[("rstd", pk)])
            if has_s:
                t3 = stmp[:, :, :]
                S.add("dve", lambda e: e.tensor_tensor(out=t3, in0=src3_s, in1=ps[:, 2, 0:SWID].unsqueeze(1).to_broadcast([128, KC, SWID]),
                                                       op=ALU.subtract),
                      reads=[k for c in range(KC) for k in src_keys_fn(c, "s")] + [("PB", 0)], writes=[("stmp",)])
                S.add("dve", lambda e: e.tensor_tensor(out=t3, in0=t3, in1=rstd_sb[:, PWID:NTC].unsqueeze(1).to_broadcast([128, KC, SWID]),
                                                       op=ALU.mult),
                      reads=[("rstd", "s")], writes=[("stmp",)])
                S.add("dve", lambda e: e.tensor_tensor(out=t3, in0=t3, in1=pT[:, gi0:gi0 + 8].unsqueeze(2).to_broadcast([128, KC, SWID]),
                                                       op=ALU.mult),
                      reads=[("params",)], writes=[("stmp",)])
                outs_s(t3, [("stmp",)])
            if has_p:
                for c in range(KC):
                    i = c % 2
                    cs = slice(0, PWID)
                    m_ap, r_ap = ps[:, 6:8, :], P3(rstd_sb[:, 0:PWID])
                    v_ap, t_ap = P3(src_fn(c)[:, cs]), P3(scr[i][:, cs])
                    S.add("dve", lambda e, v_ap=v_ap, t_ap=t_ap, m_ap=m_ap: e.tensor_tensor(out=t_ap, in0=v_ap, in1=m_ap, op=ALU.subtract),
                          reads=src_keys_fn(c, "p") + MB, writes=SCR(i))
                    bop = S.add("dve", lambda e, t_ap=t_ap, r_ap=r_ap, c=c: e.scalar_tensor_tensor(
                        out=t_ap, in0=t_ap, scalar=pv(gname, c), in1=r_ap, op0=ALU.mult, op1=ALU.mult),
                        reads=[("rstd", "p"), ("params",)], writes=SCR(i))
                    if c == WARM_C:
                        def warm(e):
                            ins = None
                            for _ in range(WARM_N):
                                ins = e.matmul(ps[:, 5, :], ones_bf[:, :], ring[:, 0, 0:512], start=True, stop=True)
                            return ins
                        S.add("pe", warm, writes=[("PB", 1)], extra=[bop])
                    outs_p(c, scr[i][:, cs], SCR(i))
            if next_af is not None:
                S.add("act", lambda e: e.activation(out=jnk[:, 1:2], in_=epsT[:, 0:1], func=next_af), reads=[("ones",)], writes=[("jnk",)])

        def ln_outs_resid(bname, mod_base, with_h, after_c=None):
            bi0 = PV[bname] * 8

            def outs_p(c, tB, tkeys):
                cs = slice(0, PWID)
                xk = [("xT", c, "p")]
                S.add("act", lambda e, c=c, tB=tB, cs=cs: e.activation(out=xT[:, c, cs], in_=tB, func=AF.Identity, bias=pv(bname, c), scale=1.0),
                      reads=tkeys + [("params",)], writes=xk)
                if with_h:
                    S.add("act", lambda e, c=c, cs=cs: e.activation(out=h[:, c, cs], in_=xT[:, c, cs], func=AF.Identity,
                                                                    bias=modp(mod_base, c), scale=modp(mod_base + 1, c)),
                          reads=xk + [("mod",)], writes=[("h", c, "p")])
                if after_c is not None:
                    after_c(c)

            def outs_s(t3, tkeys):
                xk = [("xT", c, "s") for c in range(KC)]
                S.add("dve", lambda e: e.tensor_tensor(out=xT[:, :, PWID:NTC], in0=t3,
                                                       in1=pT[:, bi0:bi0 + 8].unsqueeze(2).to_broadcast([128, KC, SWID]), op=ALU.add),
                      reads=tkeys + [("params",)], writes=xk)
                if with_h:
                    mb = mod_base
                    S.add("dve", lambda e: e.tensor_tensor(out=t3, in0=xT[:, :, PWID:NTC], in1=modT[:, (mb + 1) * 8:(mb + 2) * 8, 1:17], op=ALU.mult),
                          reads=xk + [("mod",)], writes=[("stmp",)])
                    S.add("dve", lambda e: e.tensor_tensor(out=h[:, :, PWID:NTC], in0=t3, in1=modT[:, mb * 8:(mb + 1) * 8, 1:17], op=ALU.add),
                          reads=[("stmp",), ("mod",)], writes=[("h", c, "s") for c in range(KC)])
            return outs_p, outs_s

        def resid_add(parts, b, oc, gate_mod):
            for pk in parts:
                if pk == "p":
                    S.add("dve", lambda e, b=b, oc=oc: e.scalar_tensor_tensor(
                        out=P3(xT[:, oc, 0:PWID]), in0=pb_p(b), scalar=modp(gate_mod, oc), in1=P3(xT[:, oc, 0:PWID]),
                        op0=ALU.mult, op1=ALU.add),
                        reads=[("PB", b), ("mod",)], writes=[("xT", oc, "p")])
                    S.add("act", lambda e, oc=oc: e.activation(out=h[:, oc, 0:PWID], in_=xT[:, oc, 0:PWID], func=AF.Copy),
                          reads=[("xT", oc, "p")], writes=[("h", oc, "p")])
                    S.add("dve", lambda e, oc=oc: e.tensor_tensor(out=vsqf[:, oc, 0:PWID], in0=xT[:, oc, 0:PWID], in1=xT[:, oc, 0:PWID], op=ALU.mult),
                          reads=[("xT", oc, "p")], writes=[("vsq", oc)])
                else:
                    S.add("dve", lambda e, b=b, oc=oc: e.tensor_tensor(out=stmp[:, 1, :], in0=pb_s(b), in1=mods(gate_mod, oc), op=ALU.mult),
                          reads=[("PB", b), ("mod",)], writes=[("stmp",)])
                    S.add("dve", lambda e, oc=oc: e.tensor_tensor(out=xT[:, oc, PWID:NTC], in0=stmp[:, 1, :], in1=xT[:, oc, PWID:NTC], op=ALU.add),
                          reads=[("stmp",)], writes=[("xT", oc, "s")])

        def phase_ffn(parts, f, gate_mod, gname, bname, next_mod, with_h, tag, after_c=None):
            win, wout = ffn_w_in[f], ffn_w_out[f]
            S.fence(["hid"], R1N)
            hkeys = [("h", c, pk) for c in range(KC) for pk in parts]
            for blk in range(FC // 2):
                def dm(s, blk=blk):
                    v = slot_k(s, KC, 512)
                    return [(v[:, :, 0:256], wsrc(win, KC, blk * 256, 256)),
                            (v[:, :, 256:512], wsrc(win, KC, DFF + blk * 256, 256))]
                s = wblock(dm)
                for jj in range(2):
                    j = 2 * blk + jj
                    sv = slot_k(s, KC, 512)
                    mm_group(0, parts, lambda k, sv=sv, jj=jj: sv[:, k, jj * 128:(jj + 1) * 128],
                             lambda k, c0, n: h[:, k, c0:c0 + n], KC, reads=[("wslot", s)] + hkeys)
                    mm_group(1, parts, lambda k, sv=sv, jj=jj: sv[:, k, 256 + jj * 128:256 + (jj + 1) * 128],
                             lambda k, c0, n: h[:, k, c0:c0 + n], KC, reads=[("wslot", s)] + hkeys)
                    i = j % 2
                    for pk in parts:
                        if pk == "p":
                            g_ap, u_ap, s_ap, o_ap = pb_p(0), pb_p(1), P3(scr[i][:, 0:PWID]), P3(hid[:, j, 0:PWID])
                        else:
                            g_ap, u_ap, s_ap, o_ap = pb_s(0), pb_s(1), sscr[:, i, :], hid[:, j, PWID:NTC]
                        S.add("act", lambda e, g_ap=g_ap, s_ap=s_ap: e.activation(out=s_ap, in_=g_ap, func=AF.Silu),
                              reads=[("PB", 0)], writes=SK(i, pk))
                        S.add("dve", lambda e, u_ap=u_ap, s_ap=s_ap, o_ap=o_ap: e.tensor_tensor(out=o_ap, in0=u_ap, in1=s_ap, op=ALU.mult),
                              reads=[("PB", 1)] + SK(i, pk), writes=[("hid", j, pk)])
                if len(ada_pending) > 8:
                    ada_more(1)
            dbg_dump("hid" + tag, hid[:, 0:8, :], [("hid", j, pk) for j in range(FC) for pk in parts], is_bf16=True)
            hidkeys = [("hid", j, pk) for j in range(FC) for pk in parts]
            S.fence(["vsq"], R2N)
            for oc in range(KC):
                s = wblock(lambda s, oc=oc: [(slot_k(s, FC, 128), wsrc(wout, FC, oc * 128, 128))])
                sv = slot_k(s, FC, 128)
                b = oc % 2
                mm_group(b, parts, lambda k, sv=sv: sv[:, k, :], lambda k, c0, n: hid[:, k, c0:c0 + n], FC,
                         reads=[("wslot", s)] + hidkeys)
                resid_add(parts, b, oc, gate_mod)
            dbg_dump("v" + tag, xT[:, :, :], [("xT", c, pk) for c in range(KC) for pk in parts])
            layer_norm(parts, lambda c: xT[:, c, :], xT[:, :, PWID:NTC], lambda c, pk: [("xT", c, pk)], EPS_DN, gname,
                       *ln_outs_resid(bname, next_mod, with_h, after_c), next_af=(AF.Sigmoid if tag == "1" else None), pre=("p" in parts))
            dbg_dump("x" + tag, xT[:, :, :], [("xT", c, pk) for c in range(KC) for pk in parts])

        def phase_mixer(parts, first_prompt):
            hkeys = [("h", c, pk) for c in range(KC) for pk in parts]
            has_p = "p" in parts
            has_s = "s" in parts
            S.fence(["glu", "gluhalo"], R1N)
            S.fence(["sstage", "scprod"], R2N)
            if has_s:
                scv = sconv.rearrange("(q s) k d -> q (s k) d", s=4)
                spv = spool.rearrange("(q s) k d -> q (s k) d", s=8)
                pairs = [(sc_st[0:120, q, :], scv[q]) for q in range(4)]
                pairs += [(wrep[s4 * 30:(s4 + 1) * 30, :], conv_w[0:30, :]) for s4 in range(4)]
                pairs += [(sp_st[0:120, q, :], spv[q]) for q in range(2)]
                dma_op("sp", "sstate", pairs, writes=[("sstage",)])
                dma_op("sp", "sshift", [(ncs[:, 0:29, :], sconv[:, 1:30, :]), (nps[:, 0:14, :], spool[:, 1:15, :])], out=True)
                for q in range(4):
                    S.add("dve", lambda e, q=q: e.tensor_tensor(out=sc_st[0:120, q, :], in0=sc_st[0:120, q, :], in1=wrep[0:120, :], op=ALU.mult),
                          reads=[("sstage",)], writes=[("scprod", q)])

                def selmm(e):
                    ins = None
                    for hh in range(2):
                        for q in range(4):
                            ins = e.matmul(ps[0:16, 6 + hh, :], selc[0:120, q, :], sc_st[0:120, q, hh * 512:(hh + 1) * 512],
                                           start=(q == 0), stop=(q == 3))
                    return ins
                S.add("pe", selmm, reads=[("scprod", q) for q in range(4)] + [("const",)], writes=[("bank", 6), ("bank", 7)])
                S.add("dve", lambda e: e.tensor_tensor(out=P3(tmA[0:16, :]), in0=ps[0:16, 6:8, :], in1=P3(cbrow[:, 1, :]), op=ALU.add),
                      reads=[("bank", 6), ("bank", 7), ("const",)], writes=[("tmA",)])

                def selpm(e):
                    ins = None
                    for g in range(4):
                        for q in range(2):
                            ins = e.matmul(ps[0:16, 6 + g // 2, (g % 2) * 256:(g % 2) * 256 + 256], selp[0:120, q, g, :],
                                           sp_st[0:120, q, g * 256:(g + 1) * 256], start=(q == 0), stop=(q == 1))
                    return ins
                S.add("pe", selpm, reads=[("sstage",), ("const",)], writes=[("bank", 6), ("bank", 7)])
                S.add("act", lambda e: e.activation(out=P3(mprev[0:16, :]), in_=ps[0:16, 6:8, :], func=AF.Copy),
                      reads=[("bank", 6), ("bank", 7)], writes=[("mprev",)])
            if has_p:
                if first_prompt:
                    S.add("dve", lambda e: e.memset(glub[:, :, 0:30], 0.0), writes=[("gluhalo",)])
                else:
                    S.add("dve", lambda e: e.tensor_copy(out=glub[:, :, 0:30], in_=gl_halo[:, :, :]), reads=[("gl_halo",)],
                          writes=[("gluhalo",)])
            slots = {}
            for c in range(KC):
                blk, cc = c // 4, c % 4
                if cc == 0:
                    slots["bg"] = wblock(lambda s, blk=blk: [(slot_k(s, KC, 512), wsrc(w_in, KC, D + blk * 512, 512))])
                    slots["a"] = wblock(lambda s, blk=blk: [(slot_k(s, KC, 512), wsrc(w_in, KC, blk * 512, 512))])
                sa, sg = slot_k(slots["a"], KC, 512), slot_k(slots["bg"], KC, 512)
                mm_group(1, parts, lambda k, sg=sg, cc=cc: sg[:, k, cc * 128:(cc + 1) * 128], lambda k, c0, n: h[:, k, c0:c0 + n], KC,
                         reads=[("wslot", slots["bg"])] + hkeys)
                mm_group(0, parts, lambda k, sa=sa, cc=cc: sa[:, k, cc * 128:(cc + 1) * 128], lambda k, c0, n: h[:, k, c0:c0 + n], KC,
                         reads=[("wslot", slots["a"])] + hkeys)
                i = c % 2
                for pk in parts:
                    if pk == "p":
                        a_ap, g_ap, s_ap, o_ap = pb_p(0), pb_p(1), P3(scr[i][:, 0:PWID]), P3(glub[:, c, 30:30 + PWID])
                    else:
                        a_ap, g_ap, s_ap, o_ap = pb_s(0), pb_s(1), sscr[:, i, :], glus[:, c, :]
                    S.add("act", lambda e, g_ap=g_ap, s_ap=s_ap: e.activation(out=s_ap, in_=g_ap, func=AF.Sigmoid),
                          reads=[("PB", 1)], writes=SK(i, pk))
                    S.add("dve", lambda e, a_ap=a_ap, s_ap=s_ap, o_ap=o_ap: e.tensor_tensor(out=o_ap, in0=a_ap, in1=s_ap, op=ALU.mult),
                          reads=[("PB", 0)] + SK(i, pk), writes=[("glu", c, pk)])
                    if pk == "p":
                        S.add("dve", lambda e, c=c, i=i: e.tensor_tensor(out=gl_halo[:, c, :], in0=ps[:, 1, 482:512],
                                                                        in1=scr[i][:, PWID - 30:PWID], op=ALU.mult),
                              reads=[("PB", 0), ("gluhalo",)] + SCR(i), writes=[("gl_halo",)])
            S.fence(["conv"], R2N)
            if has_p:
                PT = 24
                dctr = [0]
                for c in range(KC):
                    b = c % 2
                    rk = [("glu", c, "p"), ("gluhalo",), ("params",)]
                    for k in range(PT, 31):
                        wk = cwT[:, k * 8 + c:k * 8 + c + 1]
                        if k == PT:
                            S.add("dve", lambda e, c=c, k=k, wk=wk: e.tensor_scalar(
                                out=conv[:, c, 0:PWID], in0=glub[:, c, k:k + PWID], scalar1=wk, scalar2=pv("conv_b", c),
                                op0=ALU.mult, op1=ALU.add), reads=rk, writes=[("conv", c, "p")])
                        else:
                            S.add("dve", lambda e, c=c, k=k, wk=wk: e.scalar_tensor_tensor(
                                out=conv[:, c, 0:PWID], in0=glub[:, c, k:k + PWID], scalar=wk, in1=conv[:, c, 0:PWID],
                                op0=ALU.mult, op1=ALU.add), reads=rk, writes=[("conv", c, "p")])
                    for k in range(PT):
                        di = dctr[0] % 8
                        dctr[0] += 1
                        wk = cwT[:, k * 8 + c:k * 8 + c + 1]
                        S.add("act", lambda e, di=di, wk=wk: e.activation(out=diag[:, di, :], in_=ident[:, :], func=AF.Copy, scale=wk),
                              reads=[("params",), ("const",)], writes=[("diag", di)])

                        def cmm(e, c=c, k=k, di=di, b=b):
                            e.matmul(ps[:, 3 * b, :], diag[:, di, :], glub[:, c, k:k + 512], start=(k == 0), stop=(k == PT - 1))
                            return e.matmul(ps[:, 3 * b + 1, :], diag[:, di, :], glub[:, c, k + 512:k + 1024], start=(k == 0), stop=(k == PT - 1))
                        S.add("pe", cmm, reads=[("diag", di), ("glu", c, "p"), ("gluhalo",)], writes=[("PB", b)])
                    S.add("dve", lambda e, c=c, b=b: e.tensor_tensor(out=P3(conv[:, c, 0:PWID]), in0=pb_p(b), in1=P3(conv[:, c, 0:PWID]), op=ALU.add),
                          reads=[("PB", b)], writes=[("conv", c, "p")])
                    ada_more(1)
            if has_s:
                def trg(e):
                    ins = None
                    for c in range(KC):
                        ins = e.matmul(ps[0:16, 6 + c // 4, (c % 4) * 128:(c % 4 + 1) * 128], glus[:, c, :], ident[:, :], start=True, stop=True)
                    return ins
                S.add("pe", trg, reads=[("glu", c, "s") for c in range(KC)] + [("const",)], writes=[("bank", 6), ("bank", 7)])
                S.add("act", lambda e: e.activation(out=P3(scr[1][0:16, 0:D]), in_=ps[0:16, 6:8, :], func=AF.Copy),
                      reads=[("bank", 6), ("bank", 7)], writes=SCR(1))
                dma_op("sp", "o_ncs", [(ncs[:, 29, :], scr[1][0:16, 0:D])], reads=SCR(1), out=True)
                S.add("dve", lambda e: e.tensor_tensor(out=scr[0][0:16, 0:D], in0=scr[1][0:16, 0:D], in1=cbrow[:, 0, :], op=ALU.mult),
                      reads=SCR(1) + [("const",)], writes=SCR(0))
                S.add("dve", lambda e: e.tensor_tensor(out=scr[0][0:16, 0:D], in0=scr[0][0:16, 0:D], in1=tmA[0:16, :], op=ALU.add),
                      reads=[("tmA",)], writes=SCR(0))

                def trb(e):
                    ins = None
                    for c in range(KC):
                        ins = e.transpose(ps[:, 6, c * SWID:(c + 1) * SWID], scr[0][0:16, c * 128:(c + 1) * 128], ident[0:16, 0:16])
                    return ins
                S.add("pe", trb, reads=SCR(0) + [("const",)], writes=[("bank", 6)])
                S.add("act", lambda e: e.activation(out=conv[:, :, PWID:NTC], in_=ps[:, 6, 0:KC * SWID].rearrange("p (c t) -> p c t", t=SWID),
                                                    func=AF.Copy),
                      reads=[("bank", 6)], writes=[("conv", c, "s") for c in range(KC)])
            dbg_dump("conv", conv[:, :, :], [("conv", c, pk) for c in range(KC) for pk in parts])
            S.fence(["siluln"], R1N)

            def cl_outs_p(c, tB, tkeys):
                S.add("act", lambda e, c=c, tB=tB: e.activation(out=siluln[:, c, 0:PWID], in_=tB, func=AF.Silu, bias=pv("cln_b", c), scale=1.0),
                      reads=tkeys + [("params",)], writes=[("siluln", c, "p")])

            def cl_outs_s(t3, tkeys):
                bi0 = PV["cln_b"] * 8
                S.add("dve", lambda e: e.tensor_tensor(out=t3, in0=t3, in1=pT[:, bi0:bi0 + 8].unsqueeze(2).to_broadcast([128, KC, SWID]), op=ALU.add),
                      reads=tkeys + [("params",)], writes=[("stmp",)])
                S.add("act", lambda e: e.activation(out=siluln[:, :, PWID:NTC], in_=t3, func=AF.Silu),
                      reads=[("stmp",)], writes=[("siluln", c, "s") for c in range(KC)])
            layer_norm(parts, lambda c: conv[:, c, :], conv[:, :, PWID:NTC], lambda c, pk: [("conv", c, pk)], EPS, "cln_g", cl_outs_p, cl_outs_s, next_af=AF.Sigmoid)
            dbg_dump("siluln", siluln[:, :, :], [("siluln", c, pk) for c in range(KC) for pk in parts], is_bf16=True)
            S.fence(["upb", "tmb"], R2N)
            S.fence(["mixs", "pooled"], ["hid", "xin", "xs_in", "yst", "glu", "gluhalo", "mixs", "pooled"])
            def emit_m3(g):
                for jh in range(2):
                    oc = 2 * g + jh
                    bb = jh
                    pk_g = [("pooled", 2 * (g % 3) + t, pk) for t in range(2) for pk in parts]
                    mm_group(bb, parts, lambda k, g=g, jh=jh: poolw[:, g, k, jh * 128:(jh + 1) * 128],
                             lambda k, c0, n, g=g: pooled[:, 2 * (g % 3) + k, c0:c0 + n], 2, reads=pk_g + [("poolw",)])
                    for pk in parts:
                        if pk == "p":
                            i_ap, o_ap = pb_p(bb), P3(mixs[:, oc, 0:PWID])
                        else:
                            i_ap, o_ap = pb_s(bb), mixs[:, oc, PWID:NTC]
                        S.add("act", lambda e, i_ap=i_ap, o_ap=o_ap, oc=oc: e.activation(out=o_ap, in_=i_ap, func=AF.Copy,
                                                                                     scale=pv("pscale", oc)),
                              reads=[("PB", bb), ("params",)], writes=[("mixs", oc, pk)])

            def s_stage1(c):
                w = POOL_W[c // 2]
                ut = tmbuf[0:16, 2 * (c % 2), 0:128]
                pt = tmbuf[0:16, 2 * (c % 2) + 1, 0:128]
                S.add("pe", lambda e, c=c: e.matmul(ps[0:16, 6, 0:128], glus[:, c, :], ident[:, :], start=True, stop=True),
                      reads=[("usfm", c), ("const",)], writes=[("bank", 6)])
                S.add("act", lambda e, ut=ut: e.activation(out=ut, in_=ps[0:16, 6, 0:128], func=AF.Copy), reads=[("bank", 6)],
                      writes=[("tmb", 2 * (c % 2))])
                dma_op("sp", "o_nps", [(nps[:, 14, c * 128:(c + 1) * 128], ut)], reads=[("tmb", 2 * (c % 2))], out=True)
                S.add("dve", lambda e, ut=ut, pt=pt, c=c, w=w: e.scalar_tensor_tensor(
                    out=pt, in0=ut, scalar=(1.0 / w - 1.0), in1=mprev[0:16, c * 128:(c + 1) * 128], op0=ALU.mult, op1=ALU.add),
                    reads=[("tmb", 2 * (c % 2)), ("mprev",)], writes=[("tmb", 2 * (c % 2) + 1)])

            def s_stage2(c):
                g = c // 2
                pidx = 2 * (g % 3) + (c % 2)
                pt = tmbuf[0:16, 2 * (c % 2) + 1, 0:128]
                S.add("pe", lambda e, pt=pt: e.transpose(ps[:, 7, 0:SWID], pt, ident[0:16, 0:16]), reads=[("tmb", 2 * (c % 2) + 1), ("const",)],
                      writes=[("bank", 7)])
                S.add("act", lambda e, pidx=pidx: e.activation(out=pooled[:, pidx, PWID:NTC], in_=ps[:, 7, 0:SWID], func=AF.Copy),
                      reads=[("bank", 7)], writes=[("pooled", pidx, "s")])

            uslot = None
            for c in range(KC):
                g = c // 2
                w = POOL_W[g]
                if c % 4 == 0:
                    uslot = wblock(lambda s, c=c: [(slot_k(s, KC, 512), wsrc(w_in, KC, 2 * D + (c // 4) * 512, 512))])
                su = slot_k(uslot, KC, 512)
                b = c % 2
                mm_group(b, parts, lambda k, su=su, c=c: su[:, k, (c % 4) * 128:(c % 4 + 1) * 128], lambda k, c0, n: h[:, k, c0:c0 + n], KC,
                         reads=[("wslot", uslot)] + hkeys)
                if has_s:
                    if c >= 3:
                        s_stage2(c - 3)
                    if c >= 1:
                        s_stage1(c - 1)
                U, Pb, Qb = upb[:, 3 * b + 0, :], upb[:, 3 * b + 1, :], upb[:, 3 * b + 2, :]
                uk = [("upb", 3 * b + t) for t in range(3)]
                pidx = 2 * (g % 3) + (c % 2)
                pl = pooled[:, pidx, :]
                if has_p:
                    E = 15 + PWID
                    if first_prompt:
                        S.add("dve", lambda e, U=U: e.memset(U[:, 0:15], 0.0), writes=[uk[0]])
                    else:
                        S.add("dve", lambda e, U=U, c=c: e.tensor_copy(out=U[:, 0:15], in_=up_halo[:, c, :]), reads=[("up_halo", c)],
                              writes=[uk[0]])
                    S.add("act", lambda e, U=U, b=b: e.activation(out=P3(U[:, 15:E]), in_=pb_p(b), func=AF.Copy),
                          reads=[("PB", b)], writes=[uk[0]])
                    S.add("act", lambda e, U=U, c=c: e.activation(out=up_halo[:, c, :], in_=U[:, PWID:PWID + 15], func=AF.Copy), reads=[uk[0]],
                          writes=[("up_halo", c)])
                    S.add("dve", lambda e, U=U, Pb=Pb: e.tensor_tensor(out=Pb[:, 1:E], in0=U[:, 1:E], in1=U[:, 0:E - 1], op=ALU.add),
                          reads=[uk[0]], writes=[uk[1]])
                    Sb, skey = Pb, uk[1]
                    if w >= 4:
                        S.add("dve", lambda e, Pb=Pb, Qb=Qb: e.tensor_tensor(out=Qb[:, 3:E], in0=Pb[:, 3:E], in1=Pb[:, 1:E - 2], op=ALU.add),
                              reads=[uk[1]], writes=[uk[2]])
                        Sb, skey = Qb, uk[2]
                    if w >= 8:
                        S.add("dve", lambda e, Pb=Pb, Qb=Qb: e.tensor_tensor(out=Pb[:, 7:E], in0=Qb[:, 7:E], in1=Qb[:, 3:E - 4], op=ALU.add),
                              reads=[uk[2]], writes=[uk[1]])
                        Sb, skey = Pb, uk[1]
                    if w >= 16:
                        S.add("dve", lambda e, Pb=Pb, Qb=Qb: e.tensor_tensor(out=Qb[:, 15:E], in0=Pb[:, 15:E], in1=Pb[:, 7:E - 8], op=ALU.add),
                              reads=[uk[1]], writes=[uk[2]])
                        Sb, skey = Qb, uk[2]
                    S.add("dve", lambda e, Sb=Sb, U=U, pl=pl, w=w: e.scalar_tensor_tensor(
                        out=pl[:, 0:PWID], in0=Sb[:, 15:E], scalar=1.0 / w, in1=U[:, 15:E], op0=ALU.mult, op1=ALU.subtract),
                        reads=[skey, uk[0]], writes=[("pooled", pidx, "p")])
                    if first_prompt:
                        S.add("dve", lambda e, Sb=Sb, g=g: e.tensor_tensor(out=stmp[:, 2, :], in0=Sb[:, 15:31], in1=invc[:, g, :], op=ALU.mult),
                              reads=[skey, ("const",)], writes=[("stmp",)])
                        S.add("dve", lambda e, U=U, pl=pl: e.tensor_tensor(out=pl[:, 0:16], in0=stmp[:, 2, :], in1=U[:, 15:31], op=ALU.subtract),
                              reads=[("stmp",), uk[0]], writes=[("pooled", pidx, "p")])
                if has_s:
                    S.add("act", lambda e, b=b, c=c: e.activation(out=glus[:, c, :], in_=pb_s(b), func=AF.Copy), reads=[("PB", b)],
                          writes=[("usfm", c)])
                if c % 2 == 1 and g >= 2:
                    emit_m3(g - 2)
            if has_s:
                s_stage2(KC - 3)
                s_stage1(KC - 1)
                s_stage2(KC - 2)
                s_stage2(KC - 1)
            emit_m3(2)
            emit_m3(3)
            dbg_dump("mixs", mixs[:, :, :], [("mixs", c, pk) for c in range(KC) for pk in parts], is_bf16=True)
            S.fence(["merged", "m1h"], R2N)
            slk = [("siluln", c, pk) for c in range(KC) for pk in parts]
            mxk = [("mixs", c, pk) for c in range(KC) for pk in parts]
            for half in range(2):
                s_ga = wblock(lambda s, half=half: [(slot_k(s, KC, 512), wsrc(w_in, KC, 3 * D + half * 512, 512))])
                s_co = wblock(lambda s, half=half: [(slot_k(s, KC, 512), wsrc(w_conv_out, KC, half * 512, 512))])
                for cc in range(4):
                    oc = half * 4 + cc
                    sv_g, sv_c = slot_k(s_ga, KC, 512), slot_k(s_co, KC, 512)
                    mm_group(0, parts, lambda k, sv_g=sv_g, cc=cc: sv_g[:, k, cc * 128:(cc + 1) * 128], lambda k, c0, n: h[:, k, c0:c0 + n], KC,
                             reads=[("wslot", s_ga)] + hkeys)
                    mm_group(1, parts, lambda k, sv_c=sv_c, cc=cc: sv_c[:, k, cc * 128:(cc + 1) * 128],
                             lambda k, c0, n: siluln[:, k, c0:c0 + n], KC, reads=[("wslot", s_co)] + slk)
                    i = oc % 2
                    for pk in parts:
                        if pk == "p":
                            g_ap, y_ap, s_ap, o_ap = pb_p(0), pb_p(1), P3(scr[i][:, 0:PWID]), P3(m1h[:, cc, 0:PWID])
                        else:
                            g_ap, y_ap, s_ap, o_ap = pb_s(0), pb_s(1), sscr[:, i, :], m1h[:, cc, PWID:NTC]
                        S.add("act", lambda e, g_ap=g_ap, s_ap=s_ap: e.activation(out=s_ap, in_=g_ap, func=AF.Sigmoid),
                              reads=[("PB", 0)], writes=SK(i, pk))
                        S.add("dve", lambda e, y_ap=y_ap, s_ap=s_ap, o_ap=o_ap: e.tensor_tensor(out=o_ap, in0=y_ap, in1=s_ap, op=ALU.mult),
                              reads=[("PB", 1)] + SK(i, pk), writes=[("m1h", cc, pk)])
                s_gb = wblock(lambda s, half=half: [(slot_k(s, KC, 512), wsrc(w_in, KC, 4 * D + half * 512, 512))])
                s_po = wblock(lambda s, half=half: [(slot_k(s, KC, 512), wsrc(w_pool_out, KC, half * 512, 512))])
                for cc in range(4):
                    oc = half * 4 + cc
                    sv_g, sv_c = slot_k(s_gb, KC, 512), slot_k(s_po, KC, 512)
                    mm_group(0, parts, lambda k, sv_g=sv_g, cc=cc: sv_g[:, k, cc * 128:(cc + 1) * 128], lambda k, c0, n: h[:, k, c0:c0 + n], KC,
                             reads=[("wslot", s_gb)] + hkeys)
                    mm_group(1, parts, lambda k, sv_c=sv_c, cc=cc: sv_c[:, k, cc * 128:(cc + 1) * 128],
                             lambda k, c0, n: mixs[:, k, c0:c0 + n], KC, reads=[("wslot", s_po)] + mxk)
                    i = oc % 2
                    for pk in parts:
                        if pk == "p":
                            g_ap, y_ap, s_ap, m_ap, o_ap = pb_p(0), pb_p(1), P3(scr[i][:, 0:PWID]), P3(m1h[:, cc, 0:PWID]), P3(merged[:, oc, 0:PWID])
                        else:
                            g_ap, y_ap, s_ap, m_ap, o_ap = pb_s(0), pb_s(1), sscr[:, i, :], m1h[:, cc, PWID:NTC], merged[:, oc, PWID:NTC]
                        S.add("act", lambda e, g_ap=g_ap, s_ap=s_ap: e.activation(out=s_ap, in_=g_ap, func=AF.Sigmoid),
                              reads=[("PB", 0)], writes=SK(i, pk))
                        S.add("dve", lambda e, y_ap=y_ap, s_ap=s_ap: e.tensor_tensor(out=s_ap, in0=y_ap, in1=s_ap, op=ALU.mult),
                              reads=[("PB", 1)], writes=SK(i, pk))
                        S.add("dve", lambda e, s_ap=s_ap, m_ap=m_ap, o_ap=o_ap: e.tensor_tensor(out=o_ap, in0=s_ap, in1=m_ap, op=ALU.add),
                              reads=SK(i, pk) + [("m1h", cc, pk)], writes=[("merged", oc, pk)])
            dbg_dump("merged", merged[:, :, :], [("merged", c, pk) for c in range(KC) for pk in parts], is_bf16=True)
            mgk = [("merged", c, pk) for c in range(KC) for pk in parts]
            S.fence(["vsq"], ["m1h", "upb", "tmb", "conv", "sstage", "scprod", "yst", "vsq"])
            for half in range(2):
                s_o = wblock(lambda s, half=half: [(slot_k(s, KC, 512), wsrc(w_out, KC, half * 512, 512))])
                for cc in range(4):
                    oc = half * 4 + cc
                    sv = slot_k(s_o, KC, 512)
                    b = oc % 2
                    mm_group(b, parts, lambda k, sv=sv, cc=cc: sv[:, k, cc * 128:(cc + 1) * 128], lambda k, c0, n: merged[:, k, c0:c0 + n], KC,
                             reads=[("wslot", s_o)] + mgk)
                    resid_add(parts, b, oc, 5)
            layer_norm(parts, lambda c: xT[:, c, :], xT[:, :, PWID:NTC], lambda c, pk: [("xT", c, pk)], EPS_DN, "ln2_g",
                       *ln_outs_resid("ln2_b", 6, True), next_af=AF.Silu, pre=("p" in parts))
            dbg_dump("x2", xT[:, :, :], [("xT", c, pk) for c in range(KC) for pk in parts])

        def out_chunk_emitter(row0):
            S.fence(["yst"], R2N)

            def emit(c):
                sl = c % 4
                if c % 2 == 0:
                    b0, bkeys = 3, [("PB", 1)]
                else:
                    b0, bkeys = 0, [("PB", 0)]

                def tr(e, c=c, b0=b0):
                    ins = None
                    for r in range(8):
                        ins = e.transpose(ps[:, b0 + r // 4, (r % 4) * 128:(r % 4 + 1) * 128], xT[:, c, r * 128:(r + 1) * 128], ident[:, :])
                    return ins
                S.add("pe", tr, reads=[("xT", c, "p"), ("const",)], writes=bkeys)
                S.add("act", lambda e, b0=b0, sl=sl: e.activation(out=P3(yst[:, sl, :]), in_=ps[:, b0:b0 + 2, :], func=AF.Copy),
                      reads=bkeys, writes=[("yst", sl)])
                dst = y_p[row0:row0 + PWID, c * 128:(c + 1) * 128].rearrange("(r p) f -> p r f", p=128)
                dma_op("sp", "o_y%d" % sl, [(dst, yst[:, sl, :].rearrange("p (r f) -> p r f", f=128))], reads=[("yst", sl)], out=True)
            return emit

        def phase_out(parts, row0):
            S.fence(["yst"], R2N)
            S.fence(["bank"], ["PB"])
            n = 0
            for pk in parts:
                if pk == "p" and not OUT_IN_LN:
                    for r in range(8):
                        bp = (n % 3) * 2
                        sl = n % 4
                        n += 1

                        def tr(e, r=r, bp=bp):
                            ins = None
                            for c in range(KC):
                                ins = e.transpose(ps[:, bp + c // 4, (c % 4) * 128:(c % 4 + 1) * 128], xT[:, c, r * 128:(r + 1) * 128], ident[:, :])
                            return ins
                        S.add("pe", tr, reads=[("xT", c, "p") for c in range(KC)] + [("const",)],
                              writes=[("bank", bp), ("bank", bp + 1)])
                        if r % 2 == 0:
                            S.add("act", lambda e, bp=bp, sl=sl: e.activation(out=P3(yst[:, sl, :]), in_=ps[:, bp:bp + 2, :], func=AF.Copy),
                                  reads=[("bank", bp), ("bank", bp + 1)], writes=[("yst", sl)])
                        else:
                            S.add("dve", lambda e, bp=bp, sl=sl: e.tensor_copy(out=P3(yst[:, sl, :]), in_=ps[:, bp:bp + 2, :]),
                                  reads=[("bank", bp), ("bank", bp + 1)], writes=[("yst", sl)])
                        dma_op("sp", "o_y%d" % sl, [(y_p[row0 + r * 128: row0 + (r + 1) * 128, :], yst[:, sl, :])], reads=[("yst", sl)], out=True)
                elif pk == "s":
                    def tr(e):
                        ins = None
                        for c in range(KC):
                            ins = e.matmul(ps[0:16, 6 + c // 4, (c % 4) * 128:(c % 4 + 1) * 128], xT[:, c, PWID:NTC], ident[:, :], start=True, stop=True)
                        return ins
                    S.add("pe", tr, reads=[("xT", c, "s") for c in range(KC)] + [("const",)], writes=[("bank", 6), ("bank", 7)])
                    S.add("act", lambda e: e.activation(out=P3(scr[1][0:16, 0:D]), in_=ps[0:16, 6:8, :], func=AF.Copy),
                          reads=[("bank", 6), ("bank", 7)], writes=SCR(1))
                    dma_op("sp", "o_ys", [(y_s[:, :], scr[1][0:16, 0:D])], reads=SCR(1), out=True)
            S.fence(["PB"], ["bank"])

        def phase_final_states():
            def tr(e):
                ins = None
                for c in range(KC):
                    ins = e.matmul(ps[0:30, 6 + c // 4, (c % 4) * 128:(c % 4 + 1) * 128], gl_halo[:, c, :], ident[:, :], start=True, stop=True)
                return ins
            S.add("pe", tr, reads=[("gl_halo",), ("const",)], writes=[("bank", 6), ("bank", 7)])
            S.add("act", lambda e: e.activation(out=P3(scr[0][0:30, 0:D]), in_=ps[0:30, 6:8, :], func=AF.Copy),
                  reads=[("bank", 6), ("bank", 7)], writes=SCR(0))
            dma_op("sp", "o_fs", [(ncp[:, :], scr[0][0:30, 0:D])], reads=SCR(0), out=True)

            def tr2(e):
                ins = None
                for c in range(KC):
                    ins = e.matmul(ps[0:15, 6 + c // 4, (c % 4) * 128:(c % 4 + 1) * 128], up_halo[:, c, :], ident[:, :], start=True, stop=True)
                return ins
            S.add("pe", tr2, reads=[("up_halo", c) for c in range(KC)] + [("const",)], writes=[("bank", 6), ("bank", 7)])
            S.add("act", lambda e: e.activation(out=P3(scr[1][0:15, 0:D]), in_=ps[0:15, 6:8, :], func=AF.Copy),
                  reads=[("bank", 6), ("bank", 7)], writes=SCR(1))
            dma_op("sp", "o_fs", [(npp[:, :], scr[1][0:15, 0:D])], reads=SCR(1), out=True)

        stop = None
        stop_g = 0
        if debug and ":" in debug:
            parts_ = debug.split(":")
            debug, stop = parts_[0], parts_[1]
            if len(parts_) > 2:
                stop_g = int(parts_[2])
        phase_setup()
        prow = 0
        pg = 0
        x_load(groups_cfg[0][1], 0)
        for gi, (gname, parts) in enumerate(groups_cfg):
            has_p = "p" in parts
            phase_x(parts, prow)
            if stop == "x" and gi == stop_g:
                break
            phase_ffn(parts, 0, 2, "ln1_g", "ln1_b", 3, True, "1")
            if stop == "ffn1" and gi == stop_g:
                break
            phase_mixer(parts, first_prompt=(pg == 0))
            if stop == "mixer" and gi == stop_g:
                break
            phase_ffn(parts, 1, 8, "ln3_g", "ln3_b", 0, False, "3", after_c=(out_chunk_emitter(prow) if (has_p and OUT_IN_LN) else None))
            if stop == "ffn3" and gi == stop_g:
                break
            if gi + 1 < len(groups_cfg):
                x_load(groups_cfg[gi + 1][1], prow + (PWID if has_p else 0))
            phase_out(parts, prow)
            if stop == "out" and gi == stop_g:
                break
            if has_p:
                prow += PWID
                pg += 1
        if stop is None:
            phase_final_states()

        final = S.add("sp", lambda e: None, extra=OUT_OPS)
        S.finalize()
        prog_sems = {k: es.enter_context(nc.semaphore("prog_" + k)) for k in Sched.ENGS}
        for n in sem_names:
            SEMS[n] = es.enter_context(nc.semaphore("d_" + n))
        with nc.Block() as block:
            @block.tensor
            def _(e):
                S.emit("pe", e, prog_sems, SEMS)

            @block.scalar
            def _(e):
                S.emit("act", e, prog_sems, SEMS)

            @block.vector
            def _(e):
                S.emit("dve", e, prog_sems, SEMS)

            @block.gpsimd
            def _(e):
                S.emit("pool", e, prog_sems, SEMS)

            @block.sync
            def _(e):
                S.ops["sp"].remove(final)
                S.emit("sp", e, prog_sems, SEMS)
                for d in final.deps:
                    e.wait_ge(SEMS[d.dma_sem], S.dma_counts[d.dma_sem])
    return nc


_NC_CACHE = {}


def _consts():
    ident = np.eye(128, dtype=np.float32)
    selc = np.zeros((120, 4, 16), np.float32)
    for q in range(4):
        for s in range(4):
            selc[s * 30:(s + 1) * 30, q, 4 * q + s] = 1.0
    selp = np.zeros((120, 2, 4, 16), np.float32)
    for q in range(2):
        for g, w in enumerate(POOL_W):
            for s in range(8):
                for k in range(15):
                    if k >= 16 - w:
                        selp[s * 15 + k, q, g, 8 * q + s] = 1.0 / w
    invc = np.zeros((128, 4, 16), np.float32)
    for g, w in enumerate(POOL_W):
        for t in range(16):
            invc[:, g, t] = 1.0 / min(w, t + 1)
    return ident, selc, selp, invc


def make_in_maps(x_prompt, x_sample, state_conv, state_pool, c_prompt, c_sample,
                 w_ada, b_ada, ffn1_w_in, ffn1_w_out, ln1_g, ln1_b,
                 w_in, conv_w, conv_b, conv_ln_g, conv_ln_b, w_conv_out,
                 pool_w, pool_scale, w_pool_out, w_out, ln2_g, ln2_b,
                 ffn2_w_in, ffn2_w_out, ln3_g, ln3_b):
    f = lambda a: np.ascontiguousarray(np.asarray(a, dtype=np.float32))
    ident, selc, selp, invc = _consts()
    pvec = np.concatenate([f(v)[0].reshape(8, 128) for v in
                           (ln1_g, ln1_b, ln2_g, ln2_b, ln3_g, ln3_b, conv_b, conv_ln_g, conv_ln_b, pool_scale)], axis=0)
    cb_row = np.stack([f(conv_w)[0, 30], f(conv_b)[0]], axis=0)
    shared = {
        "w_ada": f(w_ada)[0], "b_ada": f(b_ada)[0].reshape(72, 128),
        "ffn1_w_in": f(ffn1_w_in)[0], "ffn2_w_in": f(ffn2_w_in)[0],
        "ffn1_w_out": f(ffn1_w_out)[0], "ffn2_w_out": f(ffn2_w_out)[0],
        "w_in": f(w_in)[0], "conv_w": f(conv_w)[0], "w_conv_out": f(w_conv_out)[0],
        "pool_w": f(pool_w)[0], "w_pool_out": f(w_pool_out)[0], "w_out": f(w_out)[0],
        "pvec": np.ascontiguousarray(pvec), "cb_row": np.ascontiguousarray(cb_row),
        "ident": ident, "selc": selc, "selp": selp, "invc": invc,
    }
    xp, xs = f(x_prompt), f(x_sample)
    sc, sp = f(state_conv)[0], f(state_pool)[0]
    cp, cs = f(c_prompt), f(c_sample)
    in_maps = []
    for i in range(8):
        m = dict(shared)
        m["x_p"] = xp[i]
        m["x_s"] = np.ascontiguousarray(xs[16 * i:16 * i + 16, 0, :])
        m["sconv"] = np.ascontiguousarray(sc[16 * i:16 * i + 16])
        m["spool"] = np.ascontiguousarray(sp[16 * i:16 * i + 16])
        m["c_all"] = np.ascontiguousarray(np.concatenate([cp[i:i + 1], cs[16 * i:16 * i + 16]], axis=0))
        in_maps.append(m)
    return in_maps


def kernel(**inputs):
    if "nc" not in _NC_CACHE:
        _NC_CACHE["nc"] = build_program()
    nc = _NC_CACHE["nc"]
    in_maps = make_in_maps(**inputs)
    res = run_bass_kernel_spmd(nc, in_maps, core_ids=list(range(8)))
    R = res.results
    y_prompt = np.stack([R[i]["y_p"] for i in range(8)], axis=0)
    y_sample = np.concatenate([R[i]["y_s"] for i in range(8)], axis=0)[:, None, :]
    ncp = np.stack([R[i]["ncp"] for i in range(8)], axis=0)[None]
    npp = np.stack([R[i]["npp"] for i in range(8)], axis=0)[None]
    ncs = np.concatenate([R[i]["ncs"] for i in range(8)], axis=0)[None]
    nps = np.concatenate([R[i]["nps"] for i in range(8)], axis=0)[None]
    return (y_prompt.astype(np.float32), y_sample.astype(np.float32), ncp.astype(np.float32),
            npp.astype(np.float32), ncs.astype(np.float32), nps.astype(np.float32))
```

```python
import numpy as np
from contextlib import ExitStack
import concourse.bass as bass
import concourse.mybir as mybir
from concourse.bass_utils import run_bass_kernel_spmd

F32 = mybir.dt.float32
BF16 = mybir.dt.bfloat16
AF = mybir.ActivationFunctionType
ALU = mybir.AluOpType
AX = mybir.AxisListType

D = 1024
KC = 8
DFF = 2816
FC = 22
PWID = 1024
SWID = 16
NTC = PWID + SWID
EPS = 1e-5
ALPHA = 2.0 ** 0.25
EPS_DN = EPS / (ALPHA * ALPHA)
POOL_W = (2, 4, 8, 16)
NSLOT = 4
WARM_C = 5
WARM_N = 20
SLOT_ELEMS = 4096


class Op:
    __slots__ = ("eng", "fn", "deps", "marked", "count", "dma_sem", "dma_val", "idx")

    def __init__(self, eng, fn, deps):
        self.eng = eng
        self.fn = fn
        self.deps = deps
        self.marked = False
        self.count = 0
        self.dma_sem = None
        self.dma_val = 0


class Sched:
    ENGS = ("pe", "act", "dve", "pool", "sp")

    def __init__(self):
        self.ops = {e: [] for e in self.ENGS}
        self.last_w = {}
        self.readers = {}
        self.dma_counts = {}
        self.all_ops = []
        self.known = set()
        self.pending = {}

    def add(self, eng, fn, reads=(), writes=(), dma_sem=None, n_dma=0, extra=()):
        deps = []
        seen = set()

        def push(o):
            if o is not None and id(o) not in seen:
                seen.add(id(o))
                deps.append(o)

        for k in list(reads) + list(writes):
            if k not in self.known:
                self.known.add(k)
                if k[0] in self.pending:
                    self.readers.setdefault(k, []).extend(self.pending[k[0]])
        for k in reads:
            push(self.last_w.get(k))
        for k in writes:
            push(self.last_w.get(k))
            for r in self.readers.get(k, ()):
                push(r)
        for o in extra:
            push(o)
        op = Op(eng, fn, deps)
        if dma_sem is not None:
            c = self.dma_counts.get(dma_sem, 0) + 16 * n_dma
            self.dma_counts[dma_sem] = c
            op.dma_sem = dma_sem
            op.dma_val = c
        for k in reads:
            self.readers.setdefault(k, []).append(op)
        for k in writes:
            self.last_w[k] = op
            self.readers[k] = []
        self.ops[eng].append(op)
        self.all_ops.append(op)
        return op

    def fence(self, new_names, old_names):
        olds = []
        seen = set()
        for k in list(self.known):
            if k[0] in old_names:
                for o in [self.last_w.get(k)] + list(self.readers.get(k, ())):
                    if o is not None and id(o) not in seen:
                        seen.add(id(o))
                        olds.append(o)
        for n in new_names:
            self.pending[n] = list(olds)
        for k in list(self.known):
            if k[0] in new_names:
                self.readers.setdefault(k, []).extend(olds)

    def finalize(self):
        for op in self.all_ops:
            for d in op.deps:
                if d.dma_sem is None:
                    if d.eng == "pe" and op.eng == "pe":
                        continue
                    d.marked = True
        for e in self.ENGS:
            c = 0
            for op in self.ops[e]:
                if op.marked:
                    c += 1
                    op.count = c

    def emit(self, eng, handle, prog_sems, dma_sems):
        waited = {}
        for op in self.ops[eng]:
            for d in op.deps:
                if d.dma_sem is not None:
                    key, val, sem = ("dma", d.dma_sem), d.dma_val, dma_sems[d.dma_sem]
                else:
                    if d.eng == "pe" and eng == "pe":
                        continue
                    key, val, sem = ("eng", d.eng), d.count, prog_sems[d.eng]
                if waited.get(key, 0) < val:
                    handle.wait_ge(sem, val)
                    waited[key] = val
            ins = op.fn(handle)
            if op.marked:
                ins.then_inc(prog_sems[eng], 1)


def build_program(groups_cfg=None, debug=None):
    nc = bass.Bass("TRN2", target_bir_lowering=False)
    S = Sched()

    def din(name, shape):
        return nc.dram_tensor(name, list(shape), F32, kind="ExternalInput").ap()

    def dout(name, shape):
        return nc.dram_tensor(name, list(shape), F32, kind="ExternalOutput").ap()

    x_p = din("x_p", [2048, D])
    x_s = din("x_s", [SWID, D])
    sconv = din("sconv", [SWID, 30, D])
    spool = din("spool", [SWID, 15, D])
    c_all = din("c_all", [17, D])
    w_ada = din("w_ada", [D, 9 * D])
    b_ada = din("b_ada", [72, 128])
    ffn_w_in = [din("ffn1_w_in", [D, 2 * DFF]), din("ffn2_w_in", [D, 2 * DFF])]
    ffn_w_out = [din("ffn1_w_out", [DFF, D]), din("ffn2_w_out", [DFF, D])]
    w_in = din("w_in", [D, 5 * D])
    conv_w = din("conv_w", [31, D])
    w_conv_out = din("w_conv_out", [D, D])
    pool_w = din("pool_w", [4, 256, 256])
    w_pool_out = din("w_pool_out", [D, D])
    w_out = din("w_out", [D, D])
    pvec = din("pvec", [80, 128])
    cb_row = din("cb_row", [2, D])
    ident_d = din("ident", [128, 128])
    selc_d = din("selc", [120, 4, 16])
    selp_d = din("selp", [120, 2, 4, 16])
    invc_d = din("invc", [128, 4, 16])

    y_p = dout("y_p", [2048, D])
    y_s = dout("y_s", [SWID, D])
    ncp = dout("ncp", [30, D])
    npp = dout("npp", [15, D])
    ncs = dout("ncs", [SWID, 30, D])
    nps = dout("nps", [SWID, 15, D])
    dbg = dout("dbg", [128, KC * NTC]) if debug else None

    PV = {n: i for i, n in enumerate(
        ["ln1_g", "ln1_b", "ln2_g", "ln2_b", "ln3_g", "ln3_b", "conv_b", "cln_g", "cln_b", "pscale"])}
    R1N = ["hid", "xin", "xs_in", "glu", "gluhalo", "siluln", "mixs", "pooled"]
    R2N = ["conv", "merged", "m1h", "upb", "tmb", "sstage", "scprod", "setupR2", "yst", "vsq"]

    es = ExitStack()
    with es:
        def sb(name, shape, dt=F32):
            return es.enter_context(nc.sbuf_tensor(name, list(shape), dt))

        ident = sb("ident_sb", [128, 128])
        ones_bf = sb("ones_bf", [128, 128], BF16)
        pT = sb("pT", [128, 80])
        baT = sb("baT", [128, 72])
        cwT = sb("cwT", [128, 248])
        modT = sb("modT", [128, 72, 17])
        cT = sb("cT", [128, KC, 17], BF16)
        invc = sb("invc_sb", [128, 4, 16])
        selc = sb("selc_sb", [120, 4, 16])
        selp = sb("selp_sb", [120, 2, 4, 16])
        cbrow = sb("cbrow", [16, 2, D])
        xT = sb("xT", [128, KC, NTC])
        h = sb("h", [128, KC, NTC], BF16)
        R1 = sb("R1", [128, 11440])
        R2 = sb("R2", [128, KC * NTC])
        scr = [sb("scr0", [128, NTC]), sb("scr1", [128, NTC])]
        ring = sb("ring", [128, NSLOT, SLOT_ELEMS], BF16)
        poolw = sb("poolw", [128, 4, 2, 256], BF16)
        gl_halo = sb("gl_halo", [128, KC, 30])
        up_halo = sb("up_halo", [128, KC, 15])
        stmp = sb("stmp", [128, 8, 16])
        rstd_sb = sb("rstd_sb", [128, NTC])
        glus = sb("glus", [128, KC, SWID])
        lns = sb("lns", [128, 2, KC, SWID], BF16)
        epsT = sb("epsT", [128, 2])
        sscr = sb("sscr", [128, 2, SWID])
        jnk = sb("jnk", [128, 2])
        warm_src = sb("warm_src", [128, 512], BF16)
        diag = sb("diag", [128, 8, 128], BF16)
        tmA = sb("tmA", [16, D])
        mprev = sb("mprev_sb", [16, D])
        ps = es.enter_context(nc.psum_tensor("ps", [128, 8, 512], F32))

        hid = R1[:, :].bitcast(BF16).rearrange("p (j n) -> p j n", n=NTC)
        xin = R1[:, 0:8192].rearrange("p (s r d) -> p s r d", s=2, r=4)
        xs_in = R1[:, 8192:9216]
        yst = R2[:, 0:4096].rearrange("p (s d) -> p s d", s=4)
        glu = R1[:, 0:KC * 1070].rearrange("p (c n) -> p c n", n=1070)
        glub = R1[:, 0:KC * 535].bitcast(BF16).rearrange("p (c n) -> p c n", n=1070)
        siluln = R1[:, 0:4160].bitcast(BF16).rearrange("p (c n) -> p c n", n=NTC)
        mixs = R1[:, 4160:8320].bitcast(BF16).rearrange("p (c n) -> p c n", n=NTC)
        pooled = R1[:, 8320:11440].bitcast(BF16).rearrange("p (c n) -> p c n", n=NTC)
        conv = R2[:, :].rearrange("p (c n) -> p c n", n=NTC)
        merged = R2[:, 0:4160].bitcast(BF16).rearrange("p (c n) -> p c n", n=NTC)
        m1h = R2[:, 4160:8320].rearrange("p (c n) -> p c n", n=NTC)
        vsqf = R2[:, 4160:8320].bitcast(BF16).rearrange("p (c n) -> p c n", n=NTC)
        upb = R2[:, 0:6 * 1056].rearrange("p (b n) -> p b n", n=1056)
        tmbuf = R2[:, 6400:8320].rearrange("p (b n) -> p b n", n=384)
        sc_st = R2[:, 0:4096].rearrange("p (q d) -> p q d", q=4)
        wrep = R2[:, 4096:5120]
        sp_st = R2[:, 5120:7168].rearrange("p (q d) -> p q d", q=2)

        def pb_p(b):
            return ps[:, 3 * b:3 * b + 2, :]

        def pb_s(b):
            return ps[:, 3 * b + 2, 0:SWID]

        sem_names = []
        SEMS = {}
        OUT_OPS = []

        def semname(n):
            if n not in sem_names:
                sem_names.append(n)
            return n

        if groups_cfg is None:
            groups_cfg = [("A", ["p"]), ("B", ["p", "s"])]

        def P3(ap2):
            return ap2.rearrange("p (t n) -> p t n", n=512)

        def SCR(i):
            return [("scr", i, "a"), ("scr", i, "b")]

        def SK(i, pk):
            return SCR(i) if pk == "p" else [("sscr", i)]

        def dma_op(eng, sem, pairs, reads=(), writes=(), out=False):
            sem = semname(sem)

            def fn(e):
                ins = None
                for (dst, src) in pairs:
                    ins = e.dma_start(out=dst, in_=src)
                    ins.then_inc(SEMS[sem], 16)
                return ins
            op = S.add(eng, fn, reads=reads, writes=writes, dma_sem=sem, n_dma=len(pairs))
            if out:
                OUT_OPS.append(op)
            return op

        def dbg_dump(name, ap, keys, is_bf16=False):
            if debug != name:
                return
            n = ap.shape[-1] if len(ap.shape) == 2 else None
            if len(ap.shape) == 3:
                dst = dbg[:, 0:ap.shape[1] * ap.shape[2]].rearrange("p (c n) -> p c n", n=ap.shape[2])
            else:
                dst = dbg[:, 0:n]
            dma_op("pool" if is_bf16 else "sp", "dbg", [(dst, ap)], reads=keys, out=True)

        def mm_group(pbuf, parts, lhs_fn, rhs_fn, nk, reads):
            tiles = []
            if "s" in parts:
                tiles.append((ps[:, 3 * pbuf + 2, 0:SWID], PWID, SWID))
            if "p" in parts:
                tiles.append((ps[:, 3 * pbuf, :], 0, 512))
                tiles.append((ps[:, 3 * pbuf + 1, :], 512, 512))

            def fn(e):
                ins = None
                for k in range(nk):
                    lt = lhs_fn(k)
                    for (o, c0, n) in tiles:
                        ins = e.matmul(o, lt, rhs_fn(k, c0, n), start=(k == 0), stop=(k == nk - 1))
                return ins
            return S.add("pe", fn, reads=reads, writes=[("PB", pbuf)])

        ring_ctr = [0]

        def slot_k(s, kcn, cw):
            return ring[:, s, 0:kcn * cw].rearrange("p (k n) -> p k n", n=cw)

        def wsrc(w, kcn, c0, cw):
            return w.rearrange("(k p) n -> p k n", p=128)[:, 0:kcn, c0:c0 + cw]

        def wblock(dmas_fn):
            s = ring_ctr[0] % NSLOT
            ring_ctr[0] += 1
            dma_op("pool", "w%d" % s, dmas_fn(s), writes=[("wslot", s)])
            return s

        def modp(m, c):
            return modT[:, m * 8 + c, 0:1]

        def mods(m, c):
            return modT[:, m * 8 + c, 1:17]

        def pv(name, c):
            i = PV[name] * 8 + c
            return pT[:, i:i + 1]

        def pkeys(parts, name, c):
            return [(name, c, pk) for pk in parts]

        ada_pending = list(range(4, 18))
        DERIVE = {1: ("add", 1.0), 4: ("add", 1.0), 7: ("add", 1.0), 2: ("mul", 0.5 / ALPHA), 5: ("mul", 1.0 / ALPHA), 8: ("mul", 0.5 / ALPHA)}

        def ada_block(blk):
            s = wblock(lambda s, blk=blk: [(slot_k(s, KC, 512), wsrc(w_ada, KC, blk * 512, 512))])
            bank = 6 + (blk % 2)

            def fn(e, s=s, bank=bank):
                ins = None
                for m in range(4):
                    for k in range(KC):
                        ins = e.matmul(ps[:, bank, m * 17:(m + 1) * 17], slot_k(s, KC, 512)[:, k, m * 128:(m + 1) * 128],
                                       cT[:, k, :], start=(k == 0), stop=(k == KC - 1))
                return ins
            S.add("pe", fn, reads=[("wslot", s), ("cT",)], writes=[("bank", bank)])

            def ev(e, blk=blk, bank=bank):
                return e.tensor_tensor(out=modT[:, blk * 4:(blk + 1) * 4, :],
                                       in0=ps[:, bank, 0:68].rearrange("p (m t) -> p m t", t=17),
                                       in1=baT[:, blk * 4:(blk + 1) * 4].unsqueeze(2).to_broadcast([128, 4, 17]),
                                       op=ALU.add)
            S.add("dve", ev, reads=[("bank", bank), ("params",)], writes=[("mod",)])
            if blk % 2 == 1 and (blk // 2) in DERIVE:
                m = blk // 2
                kind, f = DERIVE[m]
                if kind == "add":
                    S.add("dve", lambda e, m=m, f=f: e.tensor_scalar_add(out=modT[:, m * 8:(m + 1) * 8, :], in0=modT[:, m * 8:(m + 1) * 8, :],
                                                                         scalar1=f), writes=[("mod",)])
                else:
                    S.add("dve", lambda e, m=m, f=f: e.tensor_scalar_mul(out=modT[:, m * 8:(m + 1) * 8, :], in0=modT[:, m * 8:(m + 1) * 8, :],
                                                                         scalar1=f), writes=[("mod",)])

        def ada_more(n=1):
            for _ in range(n):
                if ada_pending:
                    ada_block(ada_pending.pop(0))

        def phase_setup():
            cwv = conv_w.rearrange("k (c p) -> k c p", p=128)
            pairs = [
                (ident[:, :], ident_d[:, :]),
                (invc[:, :, :], invc_d[:, :, :]),
                (selc[:, :, :], selc_d[:, :, :]),
                (selp[:, :, :, :], selp_d[:, :, :, :]),
                (scr[0][0:80, 0:128], pvec[:, :]),
                (scr[0][0:72, 128:256], b_ada[:, :]),
                (scr[1][0:17, 0:D], c_all[:, :]),
            ]
            for k in range(31):
                half, r0 = (0, k * 8) if k < 16 else (1, (k - 16) * 8)
                pairs.append((R2[r0:r0 + 8, half * 128:(half + 1) * 128], cwv[k, :, :]))
            for r in range(2):
                pairs.append((cbrow[:, r, :], cb_row[r:r + 1, :].partition_broadcast(16)))
            dma_op("sp", "setup", pairs, writes=SCR(0) + SCR(1) + [("setupR2",), ("const",)])
            S.add("dve", lambda e: e.memset(ones_bf[:, :], 1.0 / D), writes=[("ones",)])
            S.add("dve", lambda e: e.memset(warm_src[:, :], 1.0), writes=[("ones",)])
            S.add("dve", lambda e: e.memset(epsT[:, 0:1], EPS), writes=[("ones",)])
            S.add("dve", lambda e: e.memset(epsT[:, 1:2], EPS_DN), writes=[("ones",)])

            def t_params(e):
                e.transpose(ps[:, 6, 0:80], scr[0][0:80, 0:128], ident[0:80, 0:80])
                e.transpose(ps[:, 6, 80:152], scr[0][0:72, 128:256], ident[0:72, 0:72])
                e.transpose(ps[:, 6, 152:280], R2[0:128, 0:128], ident[:, :])
                ins = e.transpose(ps[:, 6, 280:400], R2[0:120, 128:256], ident[0:120, 0:120])
                for c in range(KC):
                    ins = e.transpose(ps[:, 7, c * 17:(c + 1) * 17], scr[1][0:17, c * 128:(c + 1) * 128], ident[0:17, 0:17])
                return ins
            S.add("pe", t_params, reads=SCR(0) + SCR(1) + [("setupR2",), ("const",)], writes=[("bank", 6), ("bank", 7)])
            S.add("dve", lambda e: e.tensor_copy(out=pT[:, :], in_=ps[:, 6, 0:80]), reads=[("bank", 6)], writes=[("params",)])
            S.add("dve", lambda e: e.tensor_copy(out=baT[:, :], in_=ps[:, 6, 80:152]), reads=[("bank", 6)], writes=[("params",)])
            S.add("dve", lambda e: e.tensor_copy(out=cwT[:, :], in_=ps[:, 6, 152:400]), reads=[("bank", 6)], writes=[("params",)])
            S.add("act", lambda e: e.activation(out=cT[:, :, :], in_=ps[:, 7, 0:KC * 17].rearrange("p (c t) -> p c t", t=17),
                                                func=AF.Silu), reads=[("bank", 7)], writes=[("cT",)])
            for blk in range(4):
                ada_block(blk)
            dma_op("pool", "poolw", [(poolw[:, g, :, :], pool_w[g].rearrange("(i p) j -> p i j", p=128)) for g in range(4)],
                   writes=[("poolw",)])
            dbg_dump("modT", modT[:, :, :].rearrange("p m t -> p (m t)"), [("mod",)])

        def x_load(parts, row0):
            S.fence(["xin", "xs_in"], R1N)
            for pk in parts:
                if pk == "p":
                    for hf in range(2):
                        src = x_p[row0 + hf * 512: row0 + (hf + 1) * 512, :].rearrange("(r p) d -> p r d", p=128)
                        dma_op("sp", "xin%d" % hf, [(xin[:, hf, :, :], src)], writes=[("xin", hf)])
                else:
                    dma_op("sp", "xsin", [(xs_in[0:SWID, :], x_s[:, :])], writes=[("xs_in",)])

        def phase_x(parts, row0):
            S.fence(["bank"], ["PB"])
            bank_ctr = [0]
            for pk in parts:
                if pk == "p":
                    for hf in range(2):
                        for c in range(KC):
                            bank = bank_ctr[0] % 6
                            bank_ctr[0] += 1

                            def tr(e, hf=hf, c=c, bank=bank):
                                ins = None
                                for r in range(4):
                                    ins = e.transpose(ps[:, bank, r * 128:(r + 1) * 128], xin[:, hf, r, c * 128:(c + 1) * 128], ident[:, :])
                                return ins
                            S.add("pe", tr, reads=[("xin", hf), ("const",)], writes=[("bank", bank)])
                            cs = slice(hf * 512, (hf + 1) * 512)
                            S.add("act", lambda e, c=c, bank=bank, cs=cs: e.activation(out=xT[:, c, cs], in_=ps[:, bank, :], func=AF.Copy),
                                  reads=[("bank", bank)], writes=[("xT", c, "p")])
                            S.add("dve", lambda e, c=c, cs=cs: e.tensor_scalar(
                                out=h[:, c, cs], in0=xT[:, c, cs], scalar1=modp(1, c), scalar2=modp(0, c), op0=ALU.mult, op1=ALU.add),
                                reads=[("xT", c, "p"), ("mod",)], writes=[("h", c, "p")])
                else:
                    bank = bank_ctr[0] % 6
                    bank_ctr[0] += 1

                    def tr(e, bank=bank):
                        ins = None
                        for c in range(KC):
                            ins = e.transpose(ps[:, bank, c * SWID:(c + 1) * SWID], xs_in[0:SWID, c * 128:(c + 1) * 128], ident[0:SWID, 0:SWID])
                        return ins
                    S.add("pe", tr, reads=[("xs_in",), ("const",)], writes=[("bank", bank)])
                    pv3 = ps[:, bank, 0:KC * SWID].rearrange("p (c t) -> p c t", t=SWID)
                    S.add("act", lambda e, pv3=pv3: e.activation(out=xT[:, :, PWID:NTC], in_=pv3, func=AF.Copy),
                          reads=[("bank", bank)], writes=[("xT", c, "s") for c in range(KC)])
                    S.add("dve", lambda e: e.tensor_tensor(out=stmp[:, :, :], in0=xT[:, :, PWID:NTC], in1=modT[:, 8:16, 1:17], op=ALU.mult),
                          reads=[("xT", c, "s") for c in range(KC)] + [("mod",)], writes=[("stmp",)])
                    S.add("dve", lambda e: e.tensor_tensor(out=h[:, :, PWID:NTC], in0=stmp[:, :, :], in1=modT[:, 0:8, 1:17], op=ALU.add),
                          reads=[("stmp",), ("mod",)], writes=[("h", c, "s") for c in range(KC)])
            S.fence(["PB"], ["bank"])
            dbg_dump("xT0", xT[:, :, :], [("xT", c, pk) for c in range(KC) for pk in parts])
            dbg_dump("h0", h[:, :, :], [("h", c, pk) for c in range(KC) for pk in parts], is_bf16=True)

        def layer_norm(parts, src_fn, src3_s, src_keys_fn, eps, gname, outs_p, outs_s, next_af=None, pre=False):
            has_p, has_s = "p" in parts, "s" in parts
            gi0 = PV[gname] * 8
            MB = [("bank", 6), ("bank", 7)]
            if has_s:
                skeys = [k for c in range(KC) for k in src_keys_fn(c, "s")]
                S.add("act", lambda e: e.activation(out=lns[:, 0, :, :], in_=src3_s, func=AF.Copy), reads=skeys, writes=[("lns", 0)])
                S.add("dve", lambda e: e.tensor_tensor(out=lns[:, 1, :, :], in0=src3_s, in1=src3_s, op=ALU.mult), reads=skeys, writes=[("lns", 1)])

                def st_s(e):
                    ins = None
                    for c in range(KC):
                        ins = e.matmul(ps[:, 2, 0:SWID], ones_bf[:, :], lns[:, 0, c, :], start=(c == 0), stop=(c == KC - 1))
                    for c in range(KC):
                        ins = e.matmul(ps[:, 2, SWID:2 * SWID], ones_bf[:, :], lns[:, 1, c, :], start=(c == 0), stop=(c == KC - 1))
                    return ins
                if not has_p:
                    S.add("pe", st_s, reads=[("lns", 0), ("lns", 1), ("ones",)], writes=[("PB", 0)])
            if has_p:
                for c in range(KC):
                    i = c % 2
                    if pre:
                        vbf, vsq = h[:, c, :], vsqf[:, c, :]
                        rk = [("h", c, "p"), ("vsq", c)]
                    else:
                        vbf = scr[i][:, 0:520].bitcast(BF16)
                        vsq = scr[i][:, 520:1040].bitcast(BF16)
                        rk = SCR(i)
                        S.add("act", lambda e, c=c, vbf=vbf: e.activation(out=vbf[:, 0:PWID], in_=src_fn(c)[:, 0:PWID], func=AF.Copy),
                              reads=src_keys_fn(c, "p"), writes=[("scr", i, "a")])
                        S.add("dve", lambda e, c=c, vsq=vsq: e.tensor_tensor(out=vsq[:, 0:PWID], in0=src_fn(c)[:, 0:PWID], in1=src_fn(c)[:, 0:PWID],
                                                                            op=ALU.mult),
                              reads=src_keys_fn(c, "p"), writes=[("scr", i, "b")])

                    def st(e, c=c, vbf=vbf, vsq=vsq):
                        ins = None
                        for (bo, c0) in ((0, 0), (1, 512)):
                            e.matmul(ps[:, 6 + bo, :], ones_bf[:, :], vbf[:, c0:c0 + 512], start=(c == 0), stop=(c == KC - 1))
                            ins = e.matmul(ps[:, bo, :], ones_bf[:, :], vsq[:, c0:c0 + 512], start=(c == 0), stop=(c == KC - 1))
                        return ins
                    S.add("pe", st, reads=rk + [("ones",)], writes=MB + [("PB", 0)])
                if has_s:
                    S.add("pe", st_s, reads=[("lns", 0), ("lns", 1), ("ones",)], writes=[("PB", 0)])
                S.add("act", lambda e: e.activation(out=jnk[:, 0:1], in_=epsT[:, 0:1], func=AF.Ln), reads=[("ones",)], writes=[("jnk",)])
            for pk in parts:
                if pk == "p":
                    m_ap, r_ap, t_ap, rs_ap = ps[:, 6:8, :], ps[:, 0:2, :], P3(scr[0][:, 0:PWID]), P3(rstd_sb[:, 0:PWID])
                    mk = MB
                else:
                    m_ap, r_ap, t_ap, rs_ap = ps[:, 2, 0:SWID], ps[:, 2, SWID:2 * SWID], sscr[:, 0, :], rstd_sb[:, PWID:NTC]
                    mk = [("PB", 0)]
                S.add("act", lambda e, m_ap=m_ap, t_ap=t_ap: e.activation(out=t_ap, in_=m_ap, func=AF.Square),
                      reads=mk, writes=SK(0, pk))
                S.add("dve", lambda e, r_ap=r_ap, t_ap=t_ap, rs_ap=rs_ap: e.tensor_tensor(out=rs_ap, in0=r_ap, in1=t_ap, op=ALU.subtract),
                      reads=SK(0, pk) + [("PB", 0)], writes=[("rstd", pk)])
            for pk in parts:
                rs_ap = P3(rstd_sb[:, 0:PWID]) if pk == "p" else rstd_sb[:, PWID:NTC]
                S.add("act", lambda e, rs_ap=rs_ap: e.activation(out=rs_ap, in_=rs_ap, func=AF.Ln,
                                                                 bias=epsT[:, (0 if eps == EPS else 1):(1 if eps == EPS else 2)], scale=1.0),
                      reads=[("ones",)], writes=[("rstd", pk)])
                S.add("act", lambda e, rs_ap=rs_ap: e.activation(out=rs_ap, in_=rs_ap, func=AF.Exp, scale=-0.5),
                      writes=[("rstd", pk)])
            if has_s:
                t3 = stmp[:, :, :]
                S.add("dve", lambda e: e.tensor_tensor(out=t3, in0=src3_s, in1=ps[:, 2, 0:SWID].unsqueeze(1).to_broadcast([128, KC, SWID]),
                                                       op=ALU.subtract),
                      reads=[k for c in range(KC) for k in src_keys_fn(c, "s")] + [("PB", 0)], writes=[("stmp",)])
                S.add("dve", lambda e: e.tensor_tensor(out=t3, in0=t3, in1=rstd_sb[:, PWID:NTC].unsqueeze(1).to_broadcast([128, KC, SWID]),
                                                       op=ALU.mult),
                      reads=[("rstd", "s")], writes=[("stmp",)])
                S.add("dve", lambda e: e.tensor_tensor(out=t3, in0=t3, in1=pT[:, gi0:gi0 + 8].unsqueeze(2).to_broadcast([128, KC, SWID]),
                                                       op=ALU.mult),
                      reads=[("params",)], writes=[("stmp",)])
                outs_s(t3, [("stmp",)])
            if has_p:
                for c in range(KC):
                    i = c % 2
                    cs = slice(0, PWID)
                    m_ap, r_ap = ps[:, 6:8, :], P3(rstd_sb[:, 0:PWID])
                    v_ap, t_ap = P3(src_fn(c)[:, cs]), P3(scr[i][:, cs])
                    S.add("dve", lambda e, v_ap=v_ap, t_ap=t_ap, m_ap=m_ap: e.tensor_tensor(out=t_ap, in0=v_ap, in1=m_ap, op=ALU.subtract),
                          reads=src_keys_fn(c, "p") + MB, writes=SCR(i))
                    bop = S.add("dve", lambda e, t_ap=t_ap, r_ap=r_ap, c=c: e.scalar_tensor_tensor(
                        out=t_ap, in0=t_ap, scalar=pv(gname, c), in1=r_ap, op0=ALU.mult, op1=ALU.mult),
                        reads=[("rstd", "p"), ("params",)], writes=SCR(i))
                    if c == WARM_C:
                        def warm(e):
                            ins = None
                            for _ in range(WARM_N):
                                ins = e.matmul(ps[:, 5, :], ones_bf[:, :], warm_src[:, :], start=True, stop=True)
                            return ins
                        S.add("pe", warm, reads=[("ones",)], writes=[("PB", 1)], extra=[bop])
                    outs_p(c, scr[i][:, cs], SCR(i))
            if next_af is not None:
                S.add("act", lambda e: e.activation(out=jnk[:, 1:2], in_=epsT[:, 0:1], func=next_af), reads=[("ones",)], writes=[("jnk",)])

        def ln_outs_resid(bname, mod_base, with_h):
            bi0 = PV[bname] * 8

            def outs_p(c, tB, tkeys):
                cs = slice(0, PWID)
                xk = [("xT", c, "p")]
                S.add("act", lambda e, c=c, tB=tB, cs=cs: e.activation(out=xT[:, c, cs], in_=tB, func=AF.Identity, bias=pv(bname, c), scale=1.0),
                      reads=tkeys + [("params",)], writes=xk)
                if with_h:
                    S.add("act", lambda e, c=c, cs=cs: e.activation(out=h[:, c, cs], in_=xT[:, c, cs], func=AF.Identity,
                                                                    bias=modp(mod_base, c), scale=modp(mod_base + 1, c)),
                          reads=xk + [("mod",)], writes=[("h", c, "p")])

            def outs_s(t3, tkeys):
                xk = [("xT", c, "s") for c in range(KC)]
                S.add("dve", lambda e: e.tensor_tensor(out=xT[:, :, PWID:NTC], in0=t3,
                                                       in1=pT[:, bi0:bi0 + 8].unsqueeze(2).to_broadcast([128, KC, SWID]), op=ALU.add),
                      reads=tkeys + [("params",)], writes=xk)
                if with_h:
                    mb = mod_base
                    S.add("dve", lambda e: e.tensor_tensor(out=t3, in0=xT[:, :, PWID:NTC], in1=modT[:, (mb + 1) * 8:(mb + 2) * 8, 1:17], op=ALU.mult),
                          reads=xk + [("mod",)], writes=[("stmp",)])
                    S.add("dve", lambda e: e.tensor_tensor(out=h[:, :, PWID:NTC], in0=t3, in1=modT[:, mb * 8:(mb + 1) * 8, 1:17], op=ALU.add),
                          reads=[("stmp",), ("mod",)], writes=[("h", c, "s") for c in range(KC)])
            return outs_p, outs_s

        def resid_add(parts, b, oc, gate_mod):
            for pk in parts:
                if pk == "p":
                    S.add("dve", lambda e, b=b, oc=oc: e.scalar_tensor_tensor(
                        out=P3(xT[:, oc, 0:PWID]), in0=pb_p(b), scalar=modp(gate_mod, oc), in1=P3(xT[:, oc, 0:PWID]),
                        op0=ALU.mult, op1=ALU.add),
                        reads=[("PB", b), ("mod",)], writes=[("xT", oc, "p")])
                    S.add("act", lambda e, oc=oc: e.activation(out=h[:, oc, 0:PWID], in_=xT[:, oc, 0:PWID], func=AF.Copy),
                          reads=[("xT", oc, "p")], writes=[("h", oc, "p")])
                    S.add("dve", lambda e, oc=oc: e.tensor_tensor(out=vsqf[:, oc, 0:PWID], in0=xT[:, oc, 0:PWID], in1=xT[:, oc, 0:PWID], op=ALU.mult),
                          reads=[("xT", oc, "p")], writes=[("vsq", oc)])
                else:
                    S.add("dve", lambda e, b=b, oc=oc: e.tensor_tensor(out=stmp[:, 1, :], in0=pb_s(b), in1=mods(gate_mod, oc), op=ALU.mult),
                          reads=[("PB", b), ("mod",)], writes=[("stmp",)])
                    S.add("dve", lambda e, oc=oc: e.tensor_tensor(out=xT[:, oc, PWID:NTC], in0=stmp[:, 1, :], in1=xT[:, oc, PWID:NTC], op=ALU.add),
                          reads=[("stmp",)], writes=[("xT", oc, "s")])

        def phase_ffn(parts, f, gate_mod, gname, bname, next_mod, with_h, tag):
            win, wout = ffn_w_in[f], ffn_w_out[f]
            S.fence(["hid"], R1N)
            hkeys = [("h", c, pk) for c in range(KC) for pk in parts]
            for blk in range(FC // 2):
                def dm(s, blk=blk):
                    v = slot_k(s, KC, 512)
                    return [(v[:, :, 0:256], wsrc(win, KC, blk * 256, 256)),
                            (v[:, :, 256:512], wsrc(win, KC, DFF + blk * 256, 256))]
                s = wblock(dm)
                for jj in range(2):
                    j = 2 * blk + jj
                    sv = slot_k(s, KC, 512)
                    mm_group(0, parts, lambda k, sv=sv, jj=jj: sv[:, k, jj * 128:(jj + 1) * 128],
                             lambda k, c0, n: h[:, k, c0:c0 + n], KC, reads=[("wslot", s)] + hkeys)
                    mm_group(1, parts, lambda k, sv=sv, jj=jj: sv[:, k, 256 + jj * 128:256 + (jj + 1) * 128],
                             lambda k, c0, n: h[:, k, c0:c0 + n], KC, reads=[("wslot", s)] + hkeys)
                    i = j % 2
                    for pk in parts:
                        if pk == "p":
                            g_ap, u_ap, s_ap, o_ap = pb_p(0), pb_p(1), P3(scr[i][:, 0:PWID]), P3(hid[:, j, 0:PWID])
                        else:
                            g_ap, u_ap, s_ap, o_ap = pb_s(0), pb_s(1), sscr[:, i, :], hid[:, j, PWID:NTC]
                        S.add("act", lambda e, g_ap=g_ap, s_ap=s_ap: e.activation(out=s_ap, in_=g_ap, func=AF.Silu),
                              reads=[("PB", 0)], writes=SK(i, pk))
                        S.add("dve", lambda e, u_ap=u_ap, s_ap=s_ap, o_ap=o_ap: e.tensor_tensor(out=o_ap, in0=u_ap, in1=s_ap, op=ALU.mult),
                              reads=[("PB", 1)] + SK(i, pk), writes=[("hid", j, pk)])
                if len(ada_pending) > 8:
                    ada_more(1)
            dbg_dump("hid" + tag, hid[:, 0:8, :], [("hid", j, pk) for j in range(FC) for pk in parts], is_bf16=True)
            hidkeys = [("hid", j, pk) for j in range(FC) for pk in parts]
            S.fence(["vsq"], R2N)
            for oc in range(KC):
                s = wblock(lambda s, oc=oc: [(slot_k(s, FC, 128), wsrc(wout, FC, oc * 128, 128))])
                sv = slot_k(s, FC, 128)
                b = oc % 2
                mm_group(b, parts, lambda k, sv=sv: sv[:, k, :], lambda k, c0, n: hid[:, k, c0:c0 + n], FC,
                         reads=[("wslot", s)] + hidkeys)
                resid_add(parts, b, oc, gate_mod)
            dbg_dump("v" + tag, xT[:, :, :], [("xT", c, pk) for c in range(KC) for pk in parts])
            layer_norm(parts, lambda c: xT[:, c, :], xT[:, :, PWID:NTC], lambda c, pk: [("xT", c, pk)], EPS_DN, gname,
                       *ln_outs_resid(bname, next_mod, with_h), next_af=(AF.Sigmoid if tag == "1" else None), pre=("p" in parts))
            dbg_dump("x" + tag, xT[:, :, :], [("xT", c, pk) for c in range(KC) for pk in parts])

        def phase_mixer(parts, first_prompt):
            hkeys = [("h", c, pk) for c in range(KC) for pk in parts]
            has_p = "p" in parts
            has_s = "s" in parts
            S.fence(["glu", "gluhalo"], R1N)
            S.fence(["sstage", "scprod"], R2N)
            if has_s:
                scv = sconv.rearrange("(q s) k d -> q (s k) d", s=4)
                spv = spool.rearrange("(q s) k d -> q (s k) d", s=8)
                pairs = [(sc_st[0:120, q, :], scv[q]) for q in range(4)]
                pairs += [(wrep[s4 * 30:(s4 + 1) * 30, :], conv_w[0:30, :]) for s4 in range(4)]
                pairs += [(sp_st[0:120, q, :], spv[q]) for q in range(2)]
                dma_op("sp", "sstate", pairs, writes=[("sstage",)])
                dma_op("sp", "sshift", [(ncs[:, 0:29, :], sconv[:, 1:30, :]), (nps[:, 0:14, :], spool[:, 1:15, :])], out=True)
                for q in range(4):
                    S.add("dve", lambda e, q=q: e.tensor_tensor(out=sc_st[0:120, q, :], in0=sc_st[0:120, q, :], in1=wrep[0:120, :], op=ALU.mult),
                          reads=[("sstage",)], writes=[("scprod", q)])

                def selmm(e):
                    ins = None
                    for hh in range(2):
                        for q in range(4):
                            ins = e.matmul(ps[0:16, 6 + hh, :], selc[0:120, q, :], sc_st[0:120, q, hh * 512:(hh + 1) * 512],
                                           start=(q == 0), stop=(q == 3))
                    return ins
                S.add("pe", selmm, reads=[("scprod", q) for q in range(4)] + [("const",)], writes=[("bank", 6), ("bank", 7)])
                S.add("dve", lambda e: e.tensor_tensor(out=P3(tmA[0:16, :]), in0=ps[0:16, 6:8, :], in1=P3(cbrow[:, 1, :]), op=ALU.add),
                      reads=[("bank", 6), ("bank", 7), ("const",)], writes=[("tmA",)])

                def selpm(e):
                    ins = None
                    for g in range(4):
                        for q in range(2):
                            ins = e.matmul(ps[0:16, 6 + g // 2, (g % 2) * 256:(g % 2) * 256 + 256], selp[0:120, q, g, :],
                                           sp_st[0:120, q, g * 256:(g + 1) * 256], start=(q == 0), stop=(q == 1))
                    return ins
                S.add("pe", selpm, reads=[("sstage",), ("const",)], writes=[("bank", 6), ("bank", 7)])
                S.add("act", lambda e: e.activation(out=P3(mprev[0:16, :]), in_=ps[0:16, 6:8, :], func=AF.Copy),
                      reads=[("bank", 6), ("bank", 7)], writes=[("mprev",)])
            if has_p:
                if first_prompt:
                    S.add("dve", lambda e: e.memset(glub[:, :, 0:30], 0.0), writes=[("gluhalo",)])
                else:
                    S.add("dve", lambda e: e.tensor_copy(out=glub[:, :, 0:30], in_=gl_halo[:, :, :]), reads=[("gl_halo",)],
                          writes=[("gluhalo",)])
            slots = {}
            for c in range(KC):
                blk, cc = c // 4, c % 4
                if cc == 0:
                    slots["bg"] = wblock(lambda s, blk=blk: [(slot_k(s, KC, 512), wsrc(w_in, KC, D + blk * 512, 512))])
                    slots["a"] = wblock(lambda s, blk=blk: [(slot_k(s, KC, 512), wsrc(w_in, KC, blk * 512, 512))])
                sa, sg = slot_k(slots["a"], KC, 512), slot_k(slots["bg"], KC, 512)
                mm_group(1, parts, lambda k, sg=sg, cc=cc: sg[:, k, cc * 128:(cc + 1) * 128], lambda k, c0, n: h[:, k, c0:c0 + n], KC,
                         reads=[("wslot", slots["bg"])] + hkeys)
                mm_group(0, parts, lambda k, sa=sa, cc=cc: sa[:, k, cc * 128:(cc + 1) * 128], lambda k, c0, n: h[:, k, c0:c0 + n], KC,
                         reads=[("wslot", slots["a"])] + hkeys)
                i = c % 2
                for pk in parts:
                    if pk == "p":
                        a_ap, g_ap, s_ap, o_ap = pb_p(0), pb_p(1), P3(scr[i][:, 0:PWID]), P3(glub[:, c, 30:30 + PWID])
                    else:
                        a_ap, g_ap, s_ap, o_ap = pb_s(0), pb_s(1), sscr[:, i, :], glus[:, c, :]
                    S.add("act", lambda e, g_ap=g_ap, s_ap=s_ap: e.activation(out=s_ap, in_=g_ap, func=AF.Sigmoid),
                          reads=[("PB", 1)], writes=SK(i, pk))
                    S.add("dve", lambda e, a_ap=a_ap, s_ap=s_ap, o_ap=o_ap: e.tensor_tensor(out=o_ap, in0=a_ap, in1=s_ap, op=ALU.mult),
                          reads=[("PB", 0)] + SK(i, pk), writes=[("glu", c, pk)])
                    if pk == "p":
                        S.add("dve", lambda e, c=c, i=i: e.tensor_tensor(out=gl_halo[:, c, :], in0=ps[:, 1, 482:512],
                                                                        in1=scr[i][:, PWID - 30:PWID], op=ALU.mult),
                              reads=[("PB", 0), ("gluhalo",)] + SCR(i), writes=[("gl_halo",)])
            S.fence(["conv"], R2N)
            if has_p:
                PT = 24
                dctr = [0]
                for c in range(KC):
                    b = c % 2
                    rk = [("glu", c, "p"), ("gluhalo",), ("params",)]
                    for k in range(PT, 31):
                        wk = cwT[:, k * 8 + c:k * 8 + c + 1]
                        if k == PT:
                            S.add("dve", lambda e, c=c, k=k, wk=wk: e.tensor_scalar(
                                out=conv[:, c, 0:PWID], in0=glub[:, c, k:k + PWID], scalar1=wk, scalar2=pv("conv_b", c),
                                op0=ALU.mult, op1=ALU.add), reads=rk, writes=[("conv", c, "p")])
                        else:
                            S.add("dve", lambda e, c=c, k=k, wk=wk: e.scalar_tensor_tensor(
                                out=conv[:, c, 0:PWID], in0=glub[:, c, k:k + PWID], scalar=wk, in1=conv[:, c, 0:PWID],
                                op0=ALU.mult, op1=ALU.add), reads=rk, writes=[("conv", c, "p")])
                    for k in range(PT):
                        di = dctr[0] % 8
                        dctr[0] += 1
                        wk = cwT[:, k * 8 + c:k * 8 + c + 1]
                        S.add("act", lambda e, di=di, wk=wk: e.activation(out=diag[:, di, :], in_=ident[:, :], func=AF.Copy, scale=wk),
                              reads=[("params",), ("const",)], writes=[("diag", di)])

                        def cmm(e, c=c, k=k, di=di, b=b):
                            e.matmul(ps[:, 3 * b, :], diag[:, di, :], glub[:, c, k:k + 512], start=(k == 0), stop=(k == PT - 1))
                            return e.matmul(ps[:, 3 * b + 1, :], diag[:, di, :], glub[:, c, k + 512:k + 1024], start=(k == 0), stop=(k == PT - 1))
                        S.add("pe", cmm, reads=[("diag", di), ("glu", c, "p"), ("gluhalo",)], writes=[("PB", b)])
                    S.add("dve", lambda e, c=c, b=b: e.tensor_tensor(out=P3(conv[:, c, 0:PWID]), in0=pb_p(b), in1=P3(conv[:, c, 0:PWID]), op=ALU.add),
                          reads=[("PB", b)], writes=[("conv", c, "p")])
                    ada_more(1)
            if has_s:
                def trg(e):
                    ins = None
                    for c in range(KC):
                        ins = e.matmul(ps[0:16, 6 + c // 4, (c % 4) * 128:(c % 4 + 1) * 128], glus[:, c, :], ident[:, :], start=True, stop=True)
                    return ins
                S.add("pe", trg, reads=[("glu", c, "s") for c in range(KC)] + [("const",)], writes=[("bank", 6), ("bank", 7)])
                S.add("act", lambda e: e.activation(out=P3(scr[1][0:16, 0:D]), in_=ps[0:16, 6:8, :], func=AF.Copy),
                      reads=[("bank", 6), ("bank", 7)], writes=SCR(1))
                dma_op("sp", "o_ncs", [(ncs[:, 29, :], scr[1][0:16, 0:D])], reads=SCR(1), out=True)
                S.add("dve", lambda e: e.tensor_tensor(out=scr[0][0:16, 0:D], in0=scr[1][0:16, 0:D], in1=cbrow[:, 0, :], op=ALU.mult),
                      reads=SCR(1) + [("const",)], writes=SCR(0))
                S.add("dve", lambda e: e.tensor_tensor(out=scr[0][0:16, 0:D], in0=scr[0][0:16, 0:D], in1=tmA[0:16, :], op=ALU.add),
                      reads=[("tmA",)], writes=SCR(0))

                def trb(e):
                    ins = None
                    for c in range(KC):
                        ins = e.transpose(ps[:, 6, c * SWID:(c + 1) * SWID], scr[0][0:16, c * 128:(c + 1) * 128], ident[0:16, 0:16])
                    return ins
                S.add("pe", trb, reads=SCR(0) + [("const",)], writes=[("bank", 6)])
                S.add("act", lambda e: e.activation(out=conv[:, :, PWID:NTC], in_=ps[:, 6, 0:KC * SWID].rearrange("p (c t) -> p c t", t=SWID),
                                                    func=AF.Copy),
                      reads=[("bank", 6)], writes=[("conv", c, "s") for c in range(KC)])
            dbg_dump("conv", conv[:, :, :], [("conv", c, pk) for c in range(KC) for pk in parts])
            S.fence(["siluln"], R1N)

            def cl_outs_p(c, tB, tkeys):
                S.add("act", lambda e, c=c, tB=tB: e.activation(out=siluln[:, c, 0:PWID], in_=tB, func=AF.Silu, bias=pv("cln_b", c), scale=1.0),
                      reads=tkeys + [("params",)], writes=[("siluln", c, "p")])

            def cl_outs_s(t3, tkeys):
                bi0 = PV["cln_b"] * 8
                S.add("dve", lambda e: e.tensor_tensor(out=t3, in0=t3, in1=pT[:, bi0:bi0 + 8].unsqueeze(2).to_broadcast([128, KC, SWID]), op=ALU.add),
                      reads=tkeys + [("params",)], writes=[("stmp",)])
                S.add("act", lambda e: e.activation(out=siluln[:, :, PWID:NTC], in_=t3, func=AF.Silu),
                      reads=[("stmp",)], writes=[("siluln", c, "s") for c in range(KC)])
            layer_norm(parts, lambda c: conv[:, c, :], conv[:, :, PWID:NTC], lambda c, pk: [("conv", c, pk)], EPS, "cln_g", cl_outs_p, cl_outs_s, next_af=AF.Sigmoid)
            dbg_dump("siluln", siluln[:, :, :], [("siluln", c, pk) for c in range(KC) for pk in parts], is_bf16=True)
            S.fence(["upb", "tmb"], R2N)
            S.fence(["mixs", "pooled"], ["hid", "xin", "xs_in", "yst", "glu", "gluhalo", "mixs", "pooled"])
            def emit_m3(g):
                for jh in range(2):
                    oc = 2 * g + jh
                    bb = jh
                    pk_g = [("pooled", 2 * (g % 3) + t, pk) for t in range(2) for pk in parts]
                    mm_group(bb, parts, lambda k, g=g, jh=jh: poolw[:, g, k, jh * 128:(jh + 1) * 128],
                             lambda k, c0, n, g=g: pooled[:, 2 * (g % 3) + k, c0:c0 + n], 2, reads=pk_g + [("poolw",)])
                    for pk in parts:
                        if pk == "p":
                            i_ap, o_ap = pb_p(bb), P3(mixs[:, oc, 0:PWID])
                        else:
                            i_ap, o_ap = pb_s(bb), mixs[:, oc, PWID:NTC]
                        S.add("act", lambda e, i_ap=i_ap, o_ap=o_ap, oc=oc: e.activation(out=o_ap, in_=i_ap, func=AF.Copy,
                                                                                     scale=pv("pscale", oc)),
                              reads=[("PB", bb), ("params",)], writes=[("mixs", oc, pk)])

            def s_stage1(c):
                w = POOL_W[c // 2]
                ut = tmbuf[0:16, 2 * (c % 2), 0:128]
                pt = tmbuf[0:16, 2 * (c % 2) + 1, 0:128]
                S.add("pe", lambda e, c=c: e.matmul(ps[0:16, 6, 0:128], glus[:, c, :], ident[:, :], start=True, stop=True),
                      reads=[("usfm", c), ("const",)], writes=[("bank", 6)])
                S.add("act", lambda e, ut=ut: e.activation(out=ut, in_=ps[0:16, 6, 0:128], func=AF.Copy), reads=[("bank", 6)],
                      writes=[("tmb", 2 * (c % 2))])
                dma_op("sp", "o_nps", [(nps[:, 14, c * 128:(c + 1) * 128], ut)], reads=[("tmb", 2 * (c % 2))], out=True)
                S.add("dve", lambda e, ut=ut, pt=pt, c=c, w=w: e.scalar_tensor_tensor(
                    out=pt, in0=ut, scalar=(1.0 / w - 1.0), in1=mprev[0:16, c * 128:(c + 1) * 128], op0=ALU.mult, op1=ALU.add),
                    reads=[("tmb", 2 * (c % 2)), ("mprev",)], writes=[("tmb", 2 * (c % 2) + 1)])

            def s_stage2(c):
                g = c // 2
                pidx = 2 * (g % 3) + (c % 2)
                pt = tmbuf[0:16, 2 * (c % 2) + 1, 0:128]
                S.add("pe", lambda e, pt=pt: e.transpose(ps[:, 7, 0:SWID], pt, ident[0:16, 0:16]), reads=[("tmb", 2 * (c % 2) + 1), ("const",)],
                      writes=[("bank", 7)])
                S.add("act", lambda e, pidx=pidx: e.activation(out=pooled[:, pidx, PWID:NTC], in_=ps[:, 7, 0:SWID], func=AF.Copy),
                      reads=[("bank", 7)], writes=[("pooled", pidx, "s")])

            uslot = None
            for c in range(KC):
                g = c // 2
                w = POOL_W[g]
                if c % 4 == 0:
                    uslot = wblock(lambda s, c=c: [(slot_k(s, KC, 512), wsrc(w_in, KC, 2 * D + (c // 4) * 512, 512))])
                su = slot_k(uslot, KC, 512)
                b = c % 2
                mm_group(b, parts, lambda k, su=su, c=c: su[:, k, (c % 4) * 128:(c % 4 + 1) * 128], lambda k, c0, n: h[:, k, c0:c0 + n], KC,
                         reads=[("wslot", uslot)] + hkeys)
                if has_s:
                    if c >= 3:
                        s_stage2(c - 3)
                    if c >= 1:
                        s_stage1(c - 1)
                U, Pb, Qb = upb[:, 3 * b + 0, :], upb[:, 3 * b + 1, :], upb[:, 3 * b + 2, :]
                uk = [("upb", 3 * b + t) for t in range(3)]
                pidx = 2 * (g % 3) + (c % 2)
                pl = pooled[:, pidx, :]
                if has_p:
                    E = 15 + PWID
                    if first_prompt:
                        S.add("dve", lambda e, U=U: e.memset(U[:, 0:15], 0.0), writes=[uk[0]])
                    else:
                        S.add("dve", lambda e, U=U, c=c: e.tensor_copy(out=U[:, 0:15], in_=up_halo[:, c, :]), reads=[("up_halo", c)],
                              writes=[uk[0]])
                    S.add("act", lambda e, U=U, b=b: e.activation(out=P3(U[:, 15:E]), in_=pb_p(b), func=AF.Copy),
                          reads=[("PB", b)], writes=[uk[0]])
                    S.add("act", lambda e, U=U, c=c: e.activation(out=up_halo[:, c, :], in_=U[:, PWID:PWID + 15], func=AF.Copy), reads=[uk[0]],
                          writes=[("up_halo", c)])
                    S.add("dve", lambda e, U=U, Pb=Pb: e.tensor_tensor(out=Pb[:, 1:E], in0=U[:, 1:E], in1=U[:, 0:E - 1], op=ALU.add),
                          reads=[uk[0]], writes=[uk[1]])
                    Sb, skey = Pb, uk[1]
                    if w >= 4:
                        S.add("dve", lambda e, Pb=Pb, Qb=Qb: e.tensor_tensor(out=Qb[:, 3:E], in0=Pb[:, 3:E], in1=Pb[:, 1:E - 2], op=ALU.add),
                              reads=[uk[1]], writes=[uk[2]])
                        Sb, skey = Qb, uk[2]
                    if w >= 8:
                        S.add("dve", lambda e, Pb=Pb, Qb=Qb: e.tensor_tensor(out=Pb[:, 7:E], in0=Qb[:, 7:E], in1=Qb[:, 3:E - 4], op=ALU.add),
                              reads=[uk[2]], writes=[uk[1]])
                        Sb, skey = Pb, uk[1]
                    if w >= 16:
                        S.add("dve", lambda e, Pb=Pb, Qb=Qb: e.tensor_tensor(out=Qb[:, 15:E], in0=Pb[:, 15:E], in1=Pb[:, 7:E - 8], op=ALU.add),
                              reads=[uk[1]], writes=[uk[2]])
                        Sb, skey = Qb, uk[2]
                    S.add("dve", lambda e, Sb=Sb, U=U, pl=pl, w=w: e.scalar_tensor_tensor(
                        out=pl[:, 0:PWID], in0=Sb[:, 15:E], scalar=1.0 / w, in1=U[:, 15:E], op0=ALU.mult, op1=ALU.subtract),
                        reads=[skey, uk[0]], writes=[("pooled", pidx, "p")])
                    if first_prompt:
                        S.add("dve", lambda e, Sb=Sb, g=g: e.tensor_tensor(out=stmp[:, 2, :], in0=Sb[:, 15:31], in1=invc[:, g, :], op=ALU.mult),
                              reads=[skey, ("const",)], writes=[("stmp",)])
                        S.add("dve", lambda e, U=U, pl=pl: e.tensor_tensor(out=pl[:, 0:16], in0=stmp[:, 2, :], in1=U[:, 15:31], op=ALU.subtract),
                              reads=[("stmp",), uk[0]], writes=[("pooled", pidx, "p")])
                if has_s:
                    S.add("act", lambda e, b=b, c=c: e.activation(out=glus[:, c, :], in_=pb_s(b), func=AF.Copy), reads=[("PB", b)],
                          writes=[("usfm", c)])
                if c % 2 == 1 and g >= 2:
                    emit_m3(g - 2)
            if has_s:
                s_stage2(KC - 3)
                s_stage1(KC - 1)
                s_stage2(KC - 2)
                s_stage2(KC - 1)
            emit_m3(2)
            emit_m3(3)
            dbg_dump("mixs", mixs[:, :, :], [("mixs", c, pk) for c in range(KC) for pk in parts], is_bf16=True)
            S.fence(["merged", "m1h"], R2N)
            slk = [("siluln", c, pk) for c in range(KC) for pk in parts]
            mxk = [("mixs", c, pk) for c in range(KC) for pk in parts]
            for half in range(2):
                s_ga = wblock(lambda s, half=half: [(slot_k(s, KC, 512), wsrc(w_in, KC, 3 * D + half * 512, 512))])
                s_co = wblock(lambda s, half=half: [(slot_k(s, KC, 512), wsrc(w_conv_out, KC, half * 512, 512))])
                for cc in range(4):
                    oc = half * 4 + cc
                    sv_g, sv_c = slot_k(s_ga, KC, 512), slot_k(s_co, KC, 512)
                    mm_group(0, parts, lambda k, sv_g=sv_g, cc=cc: sv_g[:, k, cc * 128:(cc + 1) * 128], lambda k, c0, n: h[:, k, c0:c0 + n], KC,
                             reads=[("wslot", s_ga)] + hkeys)
                    mm_group(1, parts, lambda k, sv_c=sv_c, cc=cc: sv_c[:, k, cc * 128:(cc + 1) * 128],
                             lambda k, c0, n: siluln[:, k, c0:c0 + n], KC, reads=[("wslot", s_co)] + slk)
                    i = oc % 2
                    for pk in parts:
                        if pk == "p":
                            g_ap, y_ap, s_ap, o_ap = pb_p(0), pb_p(1), P3(scr[i][:, 0:PWID]), P3(m1h[:, cc, 0:PWID])
                        else:
                            g_ap, y_ap, s_ap, o_ap = pb_s(0), pb_s(1), sscr[:, i, :], m1h[:, cc, PWID:NTC]
                        S.add("act", lambda e, g_ap=g_ap, s_ap=s_ap: e.activation(out=s_ap, in_=g_ap, func=AF.Sigmoid),
                              reads=[("PB", 0)], writes=SK(i, pk))
                        S.add("dve", lambda e, y_ap=y_ap, s_ap=s_ap, o_ap=o_ap: e.tensor_tensor(out=o_ap, in0=y_ap, in1=s_ap, op=ALU.mult),
                              reads=[("PB", 1)] + SK(i, pk), writes=[("m1h", cc, pk)])
                s_gb = wblock(lambda s, half=half: [(slot_k(s, KC, 512), wsrc(w_in, KC, 4 * D + half * 512, 512))])
                s_po = wblock(lambda s, half=half: [(slot_k(s, KC, 512), wsrc(w_pool_out, KC, half * 512, 512))])
                for cc in range(4):
                    oc = half * 4 + cc
                    sv_g, sv_c = slot_k(s_gb, KC, 512), slot_k(s_po, KC, 512)
                    mm_group(0, parts, lambda k, sv_g=sv_g, cc=cc: sv_g[:, k, cc * 128:(cc + 1) * 128], lambda k, c0, n: h[:, k, c0:c0 + n], KC,
                             reads=[("wslot", s_gb)] + hkeys)
                    mm_group(1, parts, lambda k, sv_c=sv_c, cc=cc: sv_c[:, k, cc * 128:(cc + 1) * 128],
                             lambda k, c0, n: mixs[:, k, c0:c0 + n], KC, reads=[("wslot", s_po)] + mxk)
                    i = oc % 2
                    for pk in parts:
                        if pk == "p":
                            g_ap, y_ap, s_ap, m_ap, o_ap = pb_p(0), pb_p(1), P3(scr[i][:, 0:PWID]), P3(m1h[:, cc, 0:PWID]), P3(merged[:, oc, 0:PWID])
                        else:
                            g_ap, y_ap, s_ap, m_ap, o_ap = pb_s(0), pb_s(1), sscr[:, i, :], m1h[:, cc, PWID:NTC], merged[:, oc, PWID:NTC]
                        S.add("act", lambda e, g_ap=g_ap, s_ap=s_ap: e.activation(out=s_ap, in_=g_ap, func=AF.Sigmoid),
                              reads=[("PB", 0)], writes=SK(i, pk))
                        S.add("dve", lambda e, y_ap=y_ap, s_ap=s_ap: e.tensor_tensor(out=s_ap, in0=y_ap, in1=s_ap, op=ALU.mult),
                              reads=[("PB", 1)], writes=SK(i, pk))
                        S.add("dve", lambda e, s_ap=s_ap, m_ap=m_ap, o_ap=o_ap: e.tensor_tensor(out=o_ap, in0=s_ap, in1=m_ap, op=ALU.add),
                              reads=SK(i, pk) + [("m1h", cc, pk)], writes=[("merged", oc, pk)])
            dbg_dump("merged", merged[:, :, :], [("merged", c, pk) for c in range(KC) for pk in parts], is_bf16=True)
            mgk = [("merged", c, pk) for c in range(KC) for pk in parts]
            S.fence(["vsq"], ["m1h", "upb", "tmb", "conv", "sstage", "scprod", "yst", "vsq"])
            for half in range(2):
                s_o = wblock(lambda s, half=half: [(slot_k(s, KC, 512), wsrc(w_out, KC, half * 512, 512))])
                for cc in range(4):
                    oc = half * 4 + cc
                    sv = slot_k(s_o, KC, 512)
                    b = oc % 2
                    mm_group(b, parts, lambda k, sv=sv, cc=cc: sv[:, k, cc * 128:(cc + 1) * 128], lambda k, c0, n: merged[:, k, c0:c0 + n], KC,
                             reads=[("wslot", s_o)] + mgk)
                    resid_add(parts, b, oc, 5)
            layer_norm(parts, lambda c: xT[:, c, :], xT[:, :, PWID:NTC], lambda c, pk: [("xT", c, pk)], EPS_DN, "ln2_g",
                       *ln_outs_resid("ln2_b", 6, True), next_af=AF.Silu, pre=("p" in parts))
            dbg_dump("x2", xT[:, :, :], [("xT", c, pk) for c in range(KC) for pk in parts])

        def phase_out(parts, row0):
            S.fence(["yst"], R2N)
            S.fence(["bank"], ["PB"])
            n = 0
            for pk in parts:
                if pk == "p":
                    for r in range(8):
                        bp = (n % 3) * 2
                        sl = n % 4
                        n += 1

                        def tr(e, r=r, bp=bp):
                            ins = None
                            for c in range(KC):
                                ins = e.transpose(ps[:, bp + c // 4, (c % 4) * 128:(c % 4 + 1) * 128], xT[:, c, r * 128:(r + 1) * 128], ident[:, :])
                            return ins
                        S.add("pe", tr, reads=[("xT", c, "p") for c in range(KC)] + [("const",)],
                              writes=[("bank", bp), ("bank", bp + 1)])
                        if r % 2 == 0:
                            S.add("act", lambda e, bp=bp, sl=sl: e.activation(out=P3(yst[:, sl, :]), in_=ps[:, bp:bp + 2, :], func=AF.Copy),
                                  reads=[("bank", bp), ("bank", bp + 1)], writes=[("yst", sl)])
                        else:
                            S.add("dve", lambda e, bp=bp, sl=sl: e.tensor_copy(out=P3(yst[:, sl, :]), in_=ps[:, bp:bp + 2, :]),
                                  reads=[("bank", bp), ("bank", bp + 1)], writes=[("yst", sl)])
                        dma_op("sp", "o_y%d" % sl, [(y_p[row0 + r * 128: row0 + (r + 1) * 128, :], yst[:, sl, :])], reads=[("yst", sl)], out=True)
                else:
                    def tr(e):
                        ins = None
                        for c in range(KC):
                            ins = e.matmul(ps[0:16, 6 + c // 4, (c % 4) * 128:(c % 4 + 1) * 128], xT[:, c, PWID:NTC], ident[:, :], start=True, stop=True)
                        return ins
                    S.add("pe", tr, reads=[("xT", c, "s") for c in range(KC)] + [("const",)], writes=[("bank", 6), ("bank", 7)])
                    S.add("act", lambda e: e.activation(out=P3(scr[1][0:16, 0:D]), in_=ps[0:16, 6:8, :], func=AF.Copy),
                          reads=[("bank", 6), ("bank", 7)], writes=SCR(1))
                    dma_op("sp", "o_ys", [(y_s[:, :], scr[1][0:16, 0:D])], reads=SCR(1), out=True)
            S.fence(["PB"], ["bank"])

        def phase_final_states():
            def tr(e):
                ins = None
                for c in range(KC):
                    ins = e.matmul(ps[0:30, 6 + c // 4, (c % 4) * 128:(c % 4 + 1) * 128], gl_halo[:, c, :], ident[:, :], start=True, stop=True)
                return ins
            S.add("pe", tr, reads=[("gl_halo",), ("const",)], writes=[("bank", 6), ("bank", 7)])
            S.add("act", lambda e: e.activation(out=P3(scr[0][0:30, 0:D]), in_=ps[0:30, 6:8, :], func=AF.Copy),
                  reads=[("bank", 6), ("bank", 7)], writes=SCR(0))
            dma_op("sp", "o_fs", [(ncp[:, :], scr[0][0:30, 0:D])], reads=SCR(0), out=True)

            def tr2(e):
                ins = None
                for c in range(KC):
                    ins = e.matmul(ps[0:15, 6 + c // 4, (c % 4) * 128:(c % 4 + 1) * 128], up_halo[:, c, :], ident[:, :], start=True, stop=True)
                return ins
            S.add("pe", tr2, reads=[("up_halo", c) for c in range(KC)] + [("const",)], writes=[("bank", 6), ("bank", 7)])
            S.add("act", lambda e: e.activation(out=P3(scr[1][0:15, 0:D]), in_=ps[0:15, 6:8, :], func=AF.Copy),
                  reads=[("bank", 6), ("bank", 7)], writes=SCR(1))
            dma_op("sp", "o_fs", [(npp[:, :], scr[1][0:15, 0:D])], reads=SCR(1), out=True)

        stop = None
        stop_g = 0
        if debug and ":" in debug:
            parts_ = debug.split(":")
            debug, stop = parts_[0], parts_[1]
            if len(parts_) > 2:
                stop_g = int(parts_[2])
        phase_setup()
        prow = 0
        pg = 0
        x_load(groups_cfg[0][1], 0)
        for gi, (gname, parts) in enumerate(groups_cfg):
            has_p = "p" in parts
            phase_x(parts, prow)
            if stop == "x" and gi == stop_g:
                break
            phase_ffn(parts, 0, 2, "ln1_g", "ln1_b", 3, True, "1")
            if stop == "ffn1" and gi == stop_g:
                break
            phase_mixer(parts, first_prompt=(pg == 0))
            if stop == "mixer" and gi == stop_g:
                break
            phase_ffn(parts, 1, 8, "ln3_g", "ln3_b", 0, False, "3")
            if stop == "ffn3" and gi == stop_g:
                break
            if gi + 1 < len(groups_cfg):
                x_load(groups_cfg[gi + 1][1], prow + (PWID if has_p else 0))
            phase_out(parts, prow)
            if stop == "out" and gi == stop_g:
                break
            if has_p:
                prow += PWID
                pg += 1
        if stop is None:
            phase_final_states()

        final = S.add("sp", lambda e: None, extra=OUT_OPS)
        S.finalize()
        prog_sems = {k: es.enter_context(nc.semaphore("prog_" + k)) for k in Sched.ENGS}
        for n in sem_names:
            SEMS[n] = es.enter_context(nc.semaphore("d_" + n))
        with nc.Block() as block:
            @block.tensor
            def _(e):
                S.emit("pe", e, prog_sems, SEMS)

            @block.scalar
            def _(e):
                S.emit("act", e, prog_sems, SEMS)

            @block.vector
            def _(e):
                S.emit("dve", e, prog_sems, SEMS)

            @block.gpsimd
            def _(e):
                S.emit("pool", e, prog_sems, SEMS)

            @block.sync
            def _(e):
                S.ops["sp"].remove(final)
                S.emit("sp", e, prog_sems, SEMS)
                for d in final.deps:
                    e.wait_ge(SEMS[d.dma_sem], S.dma_counts[d.dma_sem])
    return nc


_NC_CACHE = {}


def _consts():
    ident = np.eye(128, dtype=np.float32)
    selc = np.zeros((120, 4, 16), np.float32)
    for q in range(4):
        for s in range(4):
            selc[s * 30:(s + 1) * 30, q, 4 * q + s] = 1.0
    selp = np.zeros((120, 2, 4, 16), np.float32)
    for q in range(2):
        for g, w in enumerate(POOL_W):
            for s in range(8):
                for k in range(15):
                    if k >= 16 - w:
                        selp[s * 15 + k, q, g, 8 * q + s] = 1.0 / w
    invc = np.zeros((128, 4, 16), np.float32)
    for g, w in enumerate(POOL_W):
        for t in range(16):
            invc[:, g, t] = 1.0 / min(w, t + 1)
    return ident, selc, selp, invc


def make_in_maps(x_prompt, x_sample, state_conv, state_pool, c_prompt, c_sample,
                 w_ada, b_ada, ffn1_w_in, ffn1_w_out, ln1_g, ln1_b,
                 w_in, conv_w, conv_b, conv_ln_g, conv_ln_b, w_conv_out,
                 pool_w, pool_scale, w_pool_out, w_out, ln2_g, ln2_b,
                 ffn2_w_in, ffn2_w_out, ln3_g, ln3_b):
    f = lambda a: np.ascontiguousarray(np.asarray(a, dtype=np.float32))
    ident, selc, selp, invc = _consts()
    pvec = np.concatenate([f(v)[0].reshape(8, 128) for v in
                           (ln1_g, ln1_b, ln2_g, ln2_b, ln3_g, ln3_b, conv_b, conv_ln_g, conv_ln_b, pool_scale)], axis=0)
    cb_row = np.stack([f(conv_w)[0, 30], f(conv_b)[0]], axis=0)
    shared = {
        "w_ada": f(w_ada)[0], "b_ada": f(b_ada)[0].reshape(72, 128),
        "ffn1_w_in": f(ffn1_w_in)[0], "ffn2_w_in": f(ffn2_w_in)[0],
        "ffn1_w_out": f(ffn1_w_out)[0], "ffn2_w_out": f(ffn2_w_out)[0],
        "w_in": f(w_in)[0], "conv_w": f(conv_w)[0], "w_conv_out": f(w_conv_out)[0],
        "pool_w": f(pool_w)[0], "w_pool_out": f(w_pool_out)[0], "w_out": f(w_out)[0],
        "pvec": np.ascontiguousarray(pvec), "cb_row": np.ascontiguousarray(cb_row),
        "ident": ident, "selc": selc, "selp": selp, "invc": invc,
    }
    xp, xs = f(x_prompt), f(x_sample)
    sc, sp = f(state_conv)[0], f(state_pool)[0]
    cp, cs = f(c_prompt), f(c_sample)
    in_maps = []
    for i in range(8):
        m = dict(shared)
        m["x_p"] = xp[i]
        m["x_s"] = np.ascontiguousarray(xs[16 * i:16 * i + 16, 0, :])
        m["sconv"] = np.ascontiguousarray(sc[16 * i:16 * i + 16])
        m["spool"] = np.ascontiguousarray(sp[16 * i:16 * i + 16])
        m["c_all"] = np.ascontiguousarray(np.concatenate([cp[i:i + 1], cs[16 * i:16 * i + 16]], axis=0))
        in_maps.append(m)
    return in_maps


def kernel(**inputs):
    if "nc" not in _NC_CACHE:
        _NC_CACHE["nc"] = build_program()
    nc = _NC_CACHE["nc"]
    in_maps = make_in_maps(**inputs)
    res = run_bass_kernel_spmd(nc, in_maps, core_ids=list(range(8)))
    R = res.results
    y_prompt = np.stack([R[i]["y_p"] for i in range(8)], axis=0)
    y_sample = np.concatenate([R[i]["y_s"] for i in range(8)], axis=0)[:, None, :]
    ncp = np.stack([R[i]["ncp"] for i in range(8)], axis=0)[None]
    npp = np.stack([R[i]["npp"] for i in range(8)], axis=0)[None]
    ncs = np.concatenate([R[i]["ncs"] for i in range(8)], axis=0)[None]
    nps = np.concatenate([R[i]["nps"] for i in range(8)], axis=0)[None]
    return (y_prompt.astype(np.float32), y_sample.astype(np.float32), ncp.astype(np.float32),
            npp.astype(np.float32), ncs.astype(np.float32), nps.astype(np.float32))
```

```python
import numpy as np
from contextlib import ExitStack
import concourse.bass as bass
import concourse.mybir as mybir
from concourse.bass_utils import run_bass_kernel_spmd

F32 = mybir.dt.float32
BF16 = mybir.dt.bfloat16
AF = mybir.ActivationFunctionType
ALU = mybir.AluOpType
AX = mybir.AxisListType

D = 1024
KC = 8
DFF = 2816
FC = 22
PWID = 1024
SWID = 16
NTC = PWID + SWID
EPS = 1e-5
ALPHA = 2.0 ** 0.25
EPS_DN = EPS / (ALPHA * ALPHA)
POOL_W = (2, 4, 8, 16)
NSLOT = 4
WARM_C = 5
WARM_N = 20
SLOT_ELEMS = 4096


class Op:
    __slots__ = ("eng", "fn", "deps", "marked", "count", "dma_sem", "dma_val", "idx")

    def __init__(self, eng, fn, deps):
        self.eng = eng
        self.fn = fn
        self.deps = deps
        self.marked = False
        self.count = 0
        self.dma_sem = None
        self.dma_val = 0


class Sched:
    ENGS = ("pe", "act", "dve", "pool", "sp")

    def __init__(self):
        self.ops = {e: [] for e in self.ENGS}
        self.last_w = {}
        self.readers = {}
        self.dma_counts = {}
        self.all_ops = []
        self.known = set()
        self.pending = {}

    def add(self, eng, fn, reads=(), writes=(), dma_sem=None, n_dma=0, extra=()):
        deps = []
        seen = set()

        def push(o):
            if o is not None and id(o) not in seen:
                seen.add(id(o))
                deps.append(o)

        for k in list(reads) + list(writes):
            if k not in self.known:
                self.known.add(k)
                if k[0] in self.pending:
                    self.readers.setdefault(k, []).extend(self.pending[k[0]])
        for k in reads:
            push(self.last_w.get(k))
        for k in writes:
            push(self.last_w.get(k))
            for r in self.readers.get(k, ()):
                push(r)
        for o in extra:
            push(o)
        op = Op(eng, fn, deps)
        if dma_sem is not None:
            c = self.dma_counts.get(dma_sem, 0) + 16 * n_dma
            self.dma_counts[dma_sem] = c
            op.dma_sem = dma_sem
            op.dma_val = c
        for k in reads:
            self.readers.setdefault(k, []).append(op)
        for k in writes:
            self.last_w[k] = op
            self.readers[k] = []
        self.ops[eng].append(op)
        self.all_ops.append(op)
        return op

    def fence(self, new_names, old_names):
        olds = []
        seen = set()
        for k in list(self.known):
            if k[0] in old_names:
                for o in [self.last_w.get(k)] + list(self.readers.get(k, ())):
                    if o is not None and id(o) not in seen:
                        seen.add(id(o))
                        olds.append(o)
        for n in new_names:
            self.pending[n] = list(olds)
        for k in list(self.known):
            if k[0] in new_names:
                self.readers.setdefault(k, []).extend(olds)

    def finalize(self):
        for op in self.all_ops:
            for d in op.deps:
                if d.dma_sem is None:
                    if d.eng == "pe" and op.eng == "pe":
                        continue
                    d.marked = True
        for e in self.ENGS:
            c = 0
            for op in self.ops[e]:
                if op.marked:
                    c += 1
                    op.count = c

    def emit(self, eng, handle, prog_sems, dma_sems):
        waited = {}
        for op in self.ops[eng]:
            for d in op.deps:
                if d.dma_sem is not None:
                    key, val, sem = ("dma", d.dma_sem), d.dma_val, dma_sems[d.dma_sem]
                else:
                    if d.eng == "pe" and eng == "pe":
                        continue
                    key, val, sem = ("eng", d.eng), d.count, prog_sems[d.eng]
                if waited.get(key, 0) < val:
                    handle.wait_ge(sem, val)
                    waited[key] = val
            ins = op.fn(handle)
            if op.marked:
                ins.then_inc(prog_sems[eng], 1)


def build_program(groups_cfg=None, debug=None):
    nc = bass.Bass("TRN2", target_bir_lowering=False)
    S = Sched()

    def din(name, shape):
        return nc.dram_tensor(name, list(shape), F32, kind="ExternalInput").ap()

    def dout(name, shape):
        return nc.dram_tensor(name, list(shape), F32, kind="ExternalOutput").ap()

    x_p = din("x_p", [2048, D])
    x_s = din("x_s", [SWID, D])
    sconv = din("sconv", [SWID, 30, D])
    spool = din("spool", [SWID, 15, D])
    c_all = din("c_all", [17, D])
    w_ada = din("w_ada", [D, 9 * D])
    b_ada = din("b_ada", [72, 128])
    ffn_w_in = [din("ffn1_w_in", [D, 2 * DFF]), din("ffn2_w_in", [D, 2 * DFF])]
    ffn_w_out = [din("ffn1_w_out", [DFF, D]), din("ffn2_w_out", [DFF, D])]
    w_in = din("w_in", [D, 5 * D])
    conv_w = din("conv_w", [31, D])
    w_conv_out = din("w_conv_out", [D, D])
    pool_w = din("pool_w", [4, 256, 256])
    w_pool_out = din("w_pool_out", [D, D])
    w_out = din("w_out", [D, D])
    pvec = din("pvec", [80, 128])
    cb_row = din("cb_row", [2, D])
    ident_d = din("ident", [128, 128])
    selc_d = din("selc", [120, 4, 16])
    selp_d = din("selp", [120, 2, 4, 16])
    invc_d = din("invc", [128, 4, 16])

    y_p = dout("y_p", [2048, D])
    y_s = dout("y_s", [SWID, D])
    ncp = dout("ncp", [30, D])
    npp = dout("npp", [15, D])
    ncs = dout("ncs", [SWID, 30, D])
    nps = dout("nps", [SWID, 15, D])
    dbg = dout("dbg", [128, KC * NTC]) if debug else None

    PV = {n: i for i, n in enumerate(
        ["ln1_g", "ln1_b", "ln2_g", "ln2_b", "ln3_g", "ln3_b", "conv_b", "cln_g", "cln_b", "pscale"])}
    R1N = ["hid", "xin", "xs_in", "glu", "gluhalo", "siluln", "mixs", "pooled"]
    R2N = ["conv", "merged", "m1h", "upb", "tmb", "sstage", "scprod", "setupR2", "yst", "vsq"]

    es = ExitStack()
    with es:
        def sb(name, shape, dt=F32):
            return es.enter_context(nc.sbuf_tensor(name, list(shape), dt))

        ident = sb("ident_sb", [128, 128])
        ones_bf = sb("ones_bf", [128, 128], BF16)
        pT = sb("pT", [128, 80])
        baT = sb("baT", [128, 72])
        cwT = sb("cwT", [128, 248])
        modT = sb("modT", [128, 72, 17])
        cT = sb("cT", [128, KC, 17], BF16)
        invc = sb("invc_sb", [128, 4, 16])
        selc = sb("selc_sb", [120, 4, 16])
        selp = sb("selp_sb", [120, 2, 4, 16])
        cbrow = sb("cbrow", [16, 2, D])
        xT = sb("xT", [128, KC, NTC])
        h = sb("h", [128, KC, NTC], BF16)
        R1 = sb("R1", [128, 11440])
        R2 = sb("R2", [128, KC * NTC])
        scr = [sb("scr0", [128, NTC]), sb("scr1", [128, NTC])]
        ring = sb("ring", [128, NSLOT, SLOT_ELEMS], BF16)
        poolw = sb("poolw", [128, 4, 2, 256], BF16)
        gl_halo = sb("gl_halo", [128, KC, 30])
        up_halo = sb("up_halo", [128, KC, 15])
        stmp = sb("stmp", [128, 8, 16])
        rstd_sb = sb("rstd_sb", [128, NTC])
        glus = sb("glus", [128, KC, SWID])
        lns = sb("lns", [128, 2, KC, SWID], BF16)
        epsT = sb("epsT", [128, 2])
        sscr = sb("sscr", [128, 2, SWID])
        jnk = sb("jnk", [128, 2])
        warm_src = sb("warm_src", [128, 512], BF16)
        diag = sb("diag", [128, 8, 128], BF16)
        tmA = sb("tmA", [16, D])
        mprev = sb("mprev_sb", [16, D])
        ps = es.enter_context(nc.psum_tensor("ps", [128, 8, 512], F32))

        hid = R1[:, :].bitcast(BF16).rearrange("p (j n) -> p j n", n=NTC)
        xin = R1[:, 0:8192].rearrange("p (s r d) -> p s r d", s=2, r=4)
        xs_in = R1[:, 8192:9216]
        yst = R2[:, 0:4096].rearrange("p (s d) -> p s d", s=4)
        glu = R1[:, 0:KC * 1070].rearrange("p (c n) -> p c n", n=1070)
        glub = R1[:, 0:KC * 535].bitcast(BF16).rearrange("p (c n) -> p c n", n=1070)
        siluln = R1[:, 0:4160].bitcast(BF16).rearrange("p (c n) -> p c n", n=NTC)
        mixs = R1[:, 4160:8320].bitcast(BF16).rearrange("p (c n) -> p c n", n=NTC)
        pooled = R1[:, 8320:11440].bitcast(BF16).rearrange("p (c n) -> p c n", n=NTC)
        conv = R2[:, :].rearrange("p (c n) -> p c n", n=NTC)
        merged = R2[:, 0:4160].bitcast(BF16).rearrange("p (c n) -> p c n", n=NTC)
        m1h = R2[:, 4160:8320].rearrange("p (c n) -> p c n", n=NTC)
        vsqf = R2[:, 4160:8320].bitcast(BF16).rearrange("p (c n) -> p c n", n=NTC)
        upb = R2[:, 0:6 * 1056].rearrange("p (b n) -> p b n", n=1056)
        tmbuf = R2[:, 6400:8320].rearrange("p (b n) -> p b n", n=384)
        sc_st = R2[:, 0:4096].rearrange("p (q d) -> p q d", q=4)
        wrep = R2[:, 4096:5120]
        sp_st = R2[:, 5120:7168].rearrange("p (q d) -> p q d", q=2)

        def pb_p(b):
            return ps[:, 3 * b:3 * b + 2, :]

        def pb_s(b):
            return ps[:, 3 * b + 2, 0:SWID]

        sem_names = []
        SEMS = {}
        OUT_OPS = []

        def semname(n):
            if n not in sem_names:
                sem_names.append(n)
            return n

        if groups_cfg is None:
            groups_cfg = [("A", ["p"]), ("B", ["p", "s"])]

        def P3(ap2):
            return ap2.rearrange("p (t n) -> p t n", n=512)

        def SCR(i):
            return [("scr", i, "a"), ("scr", i, "b")]

        def SK(i, pk):
            return SCR(i) if pk == "p" else [("sscr", i)]

        def dma_op(eng, sem, pairs, reads=(), writes=(), out=False):
            sem = semname(sem)

            def fn(e):
                ins = None
                for (dst, src) in pairs:
                    ins = e.dma_start(out=dst, in_=src)
                    ins.then_inc(SEMS[sem], 16)
                return ins
            op = S.add(eng, fn, reads=reads, writes=writes, dma_sem=sem, n_dma=len(pairs))
            if out:
                OUT_OPS.append(op)
            return op

        def dbg_dump(name, ap, keys, is_bf16=False):
            if debug != name:
                return
            n = ap.shape[-1] if len(ap.shape) == 2 else None
            if len(ap.shape) == 3:
                dst = dbg[:, 0:ap.shape[1] * ap.shape[2]].rearrange("p (c n) -> p c n", n=ap.shape[2])
            else:
                dst = dbg[:, 0:n]
            dma_op("pool" if is_bf16 else "sp", "dbg", [(dst, ap)], reads=keys, out=True)

        def mm_group(pbuf, parts, lhs_fn, rhs_fn, nk, reads):
            tiles = []
            if "s" in parts:
                tiles.append((ps[:, 3 * pbuf + 2, 0:SWID], PWID, SWID))
            if "p" in parts:
                tiles.append((ps[:, 3 * pbuf, :], 0, 512))
                tiles.append((ps[:, 3 * pbuf + 1, :], 512, 512))

            def fn(e):
                ins = None
                for k in range(nk):
                    lt = lhs_fn(k)
                    for (o, c0, n) in tiles:
                        ins = e.matmul(o, lt, rhs_fn(k, c0, n), start=(k == 0), stop=(k == nk - 1))
                return ins
            return S.add("pe", fn, reads=reads, writes=[("PB", pbuf)])

        ring_ctr = [0]

        def slot_k(s, kcn, cw):
            return ring[:, s, 0:kcn * cw].rearrange("p (k n) -> p k n", n=cw)

        def wsrc(w, kcn, c0, cw):
            return w.rearrange("(k p) n -> p k n", p=128)[:, 0:kcn, c0:c0 + cw]

        def wblock(dmas_fn):
            s = ring_ctr[0] % NSLOT
            ring_ctr[0] += 1
            dma_op("pool", "w%d" % s, dmas_fn(s), writes=[("wslot", s)])
            return s

        def modp(m, c):
            return modT[:, m * 8 + c, 0:1]

        def mods(m, c):
            return modT[:, m * 8 + c, 1:17]

        def pv(name, c):
            i = PV[name] * 8 + c
            return pT[:, i:i + 1]

        def pkeys(parts, name, c):
            return [(name, c, pk) for pk in parts]

        ada_pending = list(range(4, 18))
        DERIVE = {1: ("add", 1.0), 4: ("add", 1.0), 7: ("add", 1.0), 2: ("mul", 0.5 / ALPHA), 5: ("mul", 1.0 / ALPHA), 8: ("mul", 0.5 / ALPHA)}

        def ada_block(blk):
            s = wblock(lambda s, blk=blk: [(slot_k(s, KC, 512), wsrc(w_ada, KC, blk * 512, 512))])
            bank = 6 + (blk % 2)

            def fn(e, s=s, bank=bank):
                ins = None
                for m in range(4):
                    for k in range(KC):
                        ins = e.matmul(ps[:, bank, m * 17:(m + 1) * 17], slot_k(s, KC, 512)[:, k, m * 128:(m + 1) * 128],
                                       cT[:, k, :], start=(k == 0), stop=(k == KC - 1))
                return ins
            S.add("pe", fn, reads=[("wslot", s), ("cT",)], writes=[("bank", bank)])

            def ev(e, blk=blk, bank=bank):
                return e.tensor_tensor(out=modT[:, blk * 4:(blk + 1) * 4, :],
                                       in0=ps[:, bank, 0:68].rearrange("p (m t) -> p m t", t=17),
                                       in1=baT[:, blk * 4:(blk + 1) * 4].unsqueeze(2).to_broadcast([128, 4, 17]),
                                       op=ALU.add)
            S.add("dve", ev, reads=[("bank", bank), ("params",)], writes=[("mod",)])
            if blk % 2 == 1 and (blk // 2) in DERIVE:
                m = blk // 2
                kind, f = DERIVE[m]
                if kind == "add":
                    S.add("dve", lambda e, m=m, f=f: e.tensor_scalar_add(out=modT[:, m * 8:(m + 1) * 8, :], in0=modT[:, m * 8:(m + 1) * 8, :],
                                                                         scalar1=f), writes=[("mod",)])
                else:
                    S.add("dve", lambda e, m=m, f=f: e.tensor_scalar_mul(out=modT[:, m * 8:(m + 1) * 8, :], in0=modT[:, m * 8:(m + 1) * 8, :],
                                                                         scalar1=f), writes=[("mod",)])

        def ada_more(n=1):
            for _ in range(n):
                if ada_pending:
                    ada_block(ada_pending.pop(0))

        def phase_setup():
            cwv = conv_w.rearrange("k (c p) -> k c p", p=128)
            pairs = [
                (ident[:, :], ident_d[:, :]),
                (invc[:, :, :], invc_d[:, :, :]),
                (selc[:, :, :], selc_d[:, :, :]),
                (selp[:, :, :, :], selp_d[:, :, :, :]),
                (scr[0][0:80, 0:128], pvec[:, :]),
                (scr[0][0:72, 128:256], b_ada[:, :]),
                (scr[1][0:17, 0:D], c_all[:, :]),
            ]
            for k in range(31):
                half, r0 = (0, k * 8) if k < 16 else (1, (k - 16) * 8)
                pairs.append((R2[r0:r0 + 8, half * 128:(half + 1) * 128], cwv[k, :, :]))
            for r in range(2):
                pairs.append((cbrow[:, r, :], cb_row[r:r + 1, :].partition_broadcast(16)))
            dma_op("sp", "setup", pairs, writes=SCR(0) + SCR(1) + [("setupR2",), ("const",)])
            S.add("dve", lambda e: e.memset(ones_bf[:, :], 1.0 / D), writes=[("ones",)])
            S.add("dve", lambda e: e.memset(warm_src[:, :], 1.0), writes=[("ones",)])
            S.add("dve", lambda e: e.memset(epsT[:, 0:1], EPS), writes=[("ones",)])
            S.add("dve", lambda e: e.memset(epsT[:, 1:2], EPS_DN), writes=[("ones",)])

            def t_params(e):
                e.transpose(ps[:, 6, 0:80], scr[0][0:80, 0:128], ident[0:80, 0:80])
                e.transpose(ps[:, 6, 80:152], scr[0][0:72, 128:256], ident[0:72, 0:72])
                e.transpose(ps[:, 6, 152:280], R2[0:128, 0:128], ident[:, :])
                ins = e.transpose(ps[:, 6, 280:400], R2[0:120, 128:256], ident[0:120, 0:120])
                for c in range(KC):
                    ins = e.transpose(ps[:, 7, c * 17:(c + 1) * 17], scr[1][0:17, c * 128:(c + 1) * 128], ident[0:17, 0:17])
                return ins
            S.add("pe", t_params, reads=SCR(0) + SCR(1) + [("setupR2",), ("const",)], writes=[("bank", 6), ("bank", 7)])
            S.add("dve", lambda e: e.tensor_copy(out=pT[:, :], in_=ps[:, 6, 0:80]), reads=[("bank", 6)], writes=[("params",)])
            S.add("dve", lambda e: e.tensor_copy(out=baT[:, :], in_=ps[:, 6, 80:152]), reads=[("bank", 6)], writes=[("params",)])
            S.add("dve", lambda e: e.tensor_copy(out=cwT[:, :], in_=ps[:, 6, 152:400]), reads=[("bank", 6)], writes=[("params",)])
            S.add("act", lambda e: e.activation(out=cT[:, :, :], in_=ps[:, 7, 0:KC * 17].rearrange("p (c t) -> p c t", t=17),
                                                func=AF.Silu), reads=[("bank", 7)], writes=[("cT",)])
            for blk in range(4):
                ada_block(blk)
            dma_op("pool", "poolw", [(poolw[:, g, :, :], pool_w[g].rearrange("(i p) j -> p i j", p=128)) for g in range(4)],
                   writes=[("poolw",)])
            dbg_dump("modT", modT[:, :, :].rearrange("p m t -> p (m t)"), [("mod",)])

        def x_load(parts, row0):
            S.fence(["xin", "xs_in"], R1N)
            for pk in parts:
                if pk == "p":
                    for hf in range(2):
                        src = x_p[row0 + hf * 512: row0 + (hf + 1) * 512, :].rearrange("(r p) d -> p r d", p=128)
                        dma_op("sp", "xin%d" % hf, [(xin[:, hf, :, :], src)], writes=[("xin", hf)])
                else:
                    dma_op("sp", "xsin", [(xs_in[0:SWID, :], x_s[:, :])], writes=[("xs_in",)])

        def phase_x(parts, row0):
            S.fence(["bank"], ["PB"])
            bank_ctr = [0]
            for pk in parts:
                if pk == "p":
                    for hf in range(2):
                        for c in range(KC):
                            bank = bank_ctr[0] % 6
                            bank_ctr[0] += 1

                            def tr(e, hf=hf, c=c, bank=bank):
                                ins = None
                                for r in range(4):
                                    ins = e.transpose(ps[:, bank, r * 128:(r + 1) * 128], xin[:, hf, r, c * 128:(c + 1) * 128], ident[:, :])
                                return ins
                            S.add("pe", tr, reads=[("xin", hf), ("const",)], writes=[("bank", bank)])
                            cs = slice(hf * 512, (hf + 1) * 512)
                            S.add("act", lambda e, c=c, bank=bank, cs=cs: e.activation(out=xT[:, c, cs], in_=ps[:, bank, :], func=AF.Copy),
                                  reads=[("bank", bank)], writes=[("xT", c, "p")])
                            S.add("dve", lambda e, c=c, cs=cs: e.tensor_scalar(
                                out=h[:, c, cs], in0=xT[:, c, cs], scalar1=modp(1, c), scalar2=modp(0, c), op0=ALU.mult, op1=ALU.add),
                                reads=[("xT", c, "p"), ("mod",)], writes=[("h", c, "p")])
                else:
                    bank = bank_ctr[0] % 6
                    bank_ctr[0] += 1

                    def tr(e, bank=bank):
                        ins = None
                        for c in range(KC):
                            ins = e.transpose(ps[:, bank, c * SWID:(c + 1) * SWID], xs_in[0:SWID, c * 128:(c + 1) * 128], ident[0:SWID, 0:SWID])
                        return ins
                    S.add("pe", tr, reads=[("xs_in",), ("const",)], writes=[("bank", bank)])
                    pv3 = ps[:, bank, 0:KC * SWID].rearrange("p (c t) -> p c t", t=SWID)
                    S.add("act", lambda e, pv3=pv3: e.activation(out=xT[:, :, PWID:NTC], in_=pv3, func=AF.Copy),
                          reads=[("bank", bank)], writes=[("xT", c, "s") for c in range(KC)])
                    S.add("dve", lambda e: e.tensor_tensor(out=stmp[:, :, :], in0=xT[:, :, PWID:NTC], in1=modT[:, 8:16, 1:17], op=ALU.mult),
                          reads=[("xT", c, "s") for c in range(KC)] + [("mod",)], writes=[("stmp",)])
                    S.add("dve", lambda e: e.tensor_tensor(out=h[:, :, PWID:NTC], in0=stmp[:, :, :], in1=modT[:, 0:8, 1:17], op=ALU.add),
                          reads=[("stmp",), ("mod",)], writes=[("h", c, "s") for c in range(KC)])
            S.fence(["PB"], ["bank"])
            dbg_dump("xT0", xT[:, :, :], [("xT", c, pk) for c in range(KC) for pk in parts])
            dbg_dump("h0", h[:, :, :], [("h", c, pk) for c in range(KC) for pk in parts], is_bf16=True)

        def layer_norm(parts, src_fn, src3_s, src_keys_fn, eps, gname, outs_p, outs_s, next_af=None, pre=False):
            has_p, has_s = "p" in parts, "s" in parts
            gi0 = PV[gname] * 8
            MB = [("bank", 6), ("bank", 7)]
            if has_s:
                skeys = [k for c in range(KC) for k in src_keys_fn(c, "s")]
                S.add("act", lambda e: e.activation(out=lns[:, 0, :, :], in_=src3_s, func=AF.Copy), reads=skeys, writes=[("lns", 0)])
                S.add("dve", lambda e: e.tensor_tensor(out=lns[:, 1, :, :], in0=src3_s, in1=src3_s, op=ALU.mult), reads=skeys, writes=[("lns", 1)])

                def st_s(e):
                    ins = None
                    for c in range(KC):
                        ins = e.matmul(ps[:, 2, 0:SWID], ones_bf[:, :], lns[:, 0, c, :], start=(c == 0), stop=(c == KC - 1))
                    for c in range(KC):
                        ins = e.matmul(ps[:, 2, SWID:2 * SWID], ones_bf[:, :], lns[:, 1, c, :], start=(c == 0), stop=(c == KC - 1))
                    return ins
                if not has_p:
                    S.add("pe", st_s, reads=[("lns", 0), ("lns", 1), ("ones",)], writes=[("PB", 0)])
            if has_p:
                for c in range(KC):
                    i = c % 2
                    if pre:
                        vbf, vsq = h[:, c, :], vsqf[:, c, :]
                        rk = [("h", c, "p"), ("vsq", c)]
                    else:
                        vbf = scr[i][:, 0:520].bitcast(BF16)
                        vsq = scr[i][:, 520:1040].bitcast(BF16)
                        rk = SCR(i)
                        S.add("act", lambda e, c=c, vbf=vbf: e.activation(out=vbf[:, 0:PWID], in_=src_fn(c)[:, 0:PWID], func=AF.Copy),
                              reads=src_keys_fn(c, "p"), writes=[("scr", i, "a")])
                        S.add("dve", lambda e, c=c, vsq=vsq: e.tensor_tensor(out=vsq[:, 0:PWID], in0=src_fn(c)[:, 0:PWID], in1=src_fn(c)[:, 0:PWID],
                                                                            op=ALU.mult),
                              reads=src_keys_fn(c, "p"), writes=[("scr", i, "b")])

                    def st(e, c=c, vbf=vbf, vsq=vsq):
                        ins = None
                        for (bo, c0) in ((0, 0), (1, 512)):
                            e.matmul(ps[:, 6 + bo, :], ones_bf[:, :], vbf[:, c0:c0 + 512], start=(c == 0), stop=(c == KC - 1))
                            ins = e.matmul(ps[:, bo, :], ones_bf[:, :], vsq[:, c0:c0 + 512], start=(c == 0), stop=(c == KC - 1))
                        return ins
                    S.add("pe", st, reads=rk + [("ones",)], writes=MB + [("PB", 0)])
                if has_s:
                    S.add("pe", st_s, reads=[("lns", 0), ("lns", 1), ("ones",)], writes=[("PB", 0)])
                S.add("act", lambda e: e.activation(out=jnk[:, 0:1], in_=epsT[:, 0:1], func=AF.Ln), reads=[("ones",)], writes=[("jnk",)])
            for pk in parts:
                if pk == "p":
                    m_ap, r_ap, t_ap, rs_ap = ps[:, 6:8, :], ps[:, 0:2, :], P3(scr[0][:, 0:PWID]), P3(rstd_sb[:, 0:PWID])
                    mk = MB
                else:
                    m_ap, r_ap, t_ap, rs_ap = ps[:, 2, 0:SWID], ps[:, 2, SWID:2 * SWID], sscr[:, 0, :], rstd_sb[:, PWID:NTC]
                    mk = [("PB", 0)]
                S.add("act", lambda e, m_ap=m_ap, t_ap=t_ap: e.activation(out=t_ap, in_=m_ap, func=AF.Square),
                      reads=mk, writes=SK(0, pk))
                S.add("dve", lambda e, r_ap=r_ap, t_ap=t_ap, rs_ap=rs_ap: e.tensor_tensor(out=rs_ap, in0=r_ap, in1=t_ap, op=ALU.subtract),
                      reads=SK(0, pk) + [("PB", 0)], writes=[("rstd", pk)])
            for pk in parts:
                rs_ap = P3(rstd_sb[:, 0:PWID]) if pk == "p" else rstd_sb[:, PWID:NTC]
                S.add("act", lambda e, rs_ap=rs_ap: e.activation(out=rs_ap, in_=rs_ap, func=AF.Ln,
                                                                 bias=epsT[:, (0 if eps == EPS else 1):(1 if eps == EPS else 2)], scale=1.0),
                      reads=[("ones",)], writes=[("rstd", pk)])
                S.add("act", lambda e, rs_ap=rs_ap: e.activation(out=rs_ap, in_=rs_ap, func=AF.Exp, scale=-0.5),
                      writes=[("rstd", pk)])
            if has_s:
                t3 = stmp[:, :, :]
                S.add("dve", lambda e: e.tensor_tensor(out=t3, in0=src3_s, in1=ps[:, 2, 0:SWID].unsqueeze(1).to_broadcast([128, KC, SWID]),
                                                       op=ALU.subtract),
                      reads=[k for c in range(KC) for k in src_keys_fn(c, "s")] + [("PB", 0)], writes=[("stmp",)])
                S.add("dve", lambda e: e.tensor_tensor(out=t3, in0=t3, in1=rstd_sb[:, PWID:NTC].unsqueeze(1).to_broadcast([128, KC, SWID]),
                                                       op=ALU.mult),
                      reads=[("rstd", "s")], writes=[("stmp",)])
                S.add("dve", lambda e: e.tensor_tensor(out=t3, in0=t3, in1=pT[:, gi0:gi0 + 8].unsqueeze(2).to_broadcast([128, KC, SWID]),
                                                       op=ALU.mult),
                      reads=[("params",)], writes=[("stmp",)])
                outs_s(t3, [("stmp",)])
            if has_p:
                for c in range(KC):
                    i = c % 2
                    cs = slice(0, PWID)
                    m_ap, r_ap = ps[:, 6:8, :], P3(rstd_sb[:, 0:PWID])
                    v_ap, t_ap = P3(src_fn(c)[:, cs]), P3(scr[i][:, cs])
                    S.add("dve", lambda e, v_ap=v_ap, t_ap=t_ap, m_ap=m_ap: e.tensor_tensor(out=t_ap, in0=v_ap, in1=m_ap, op=ALU.subtract),
                          reads=src_keys_fn(c, "p") + MB, writes=SCR(i))
                    bop = S.add("dve", lambda e, t_ap=t_ap, r_ap=r_ap, c=c: e.scalar_tensor_tensor(
                        out=t_ap, in0=t_ap, scalar=pv(gname, c), in1=r_ap, op0=ALU.mult, op1=ALU.mult),
                        reads=[("rstd", "p"), ("params",)], writes=SCR(i))
                    if c == WARM_C:
                        def warm(e):
                            ins = None
                            for _ in range(WARM_N):
                                ins = e.matmul(ps[:, 5, :], ones_bf[:, :], warm_src[:, :], start=True, stop=True)
                            return ins
                        S.add("pe", warm, reads=[("ones",)], writes=[("PB", 1)], extra=[bop])
                    outs_p(c, scr[i][:, cs], SCR(i))
            if next_af is not None:
                S.add("act", lambda e: e.activation(out=jnk[:, 1:2], in_=epsT[:, 0:1], func=next_af), reads=[("ones",)], writes=[("jnk",)])

        def ln_outs_resid(bname, mod_base, with_h):
            bi0 = PV[bname] * 8

            def outs_p(c, tB, tkeys):
                cs = slice(0, PWID)
                xk = [("xT", c, "p")]
                S.add("act", lambda e, c=c, tB=tB, cs=cs: e.activation(out=xT[:, c, cs], in_=tB, func=AF.Identity, bias=pv(bname, c), scale=1.0),
                      reads=tkeys + [("params",)], writes=xk)
                if with_h:
                    S.add("act", lambda e, c=c, cs=cs: e.activation(out=h[:, c, cs], in_=xT[:, c, cs], func=AF.Identity,
                                                                    bias=modp(mod_base, c), scale=modp(mod_base + 1, c)),
                          reads=xk + [("mod",)], writes=[("h", c, "p")])

            def outs_s(t3, tkeys):
                xk = [("xT", c, "s") for c in range(KC)]
                S.add("dve", lambda e: e.tensor_tensor(out=xT[:, :, PWID:NTC], in0=t3,
                                                       in1=pT[:, bi0:bi0 + 8].unsqueeze(2).to_broadcast([128, KC, SWID]), op=ALU.add),
                      reads=tkeys + [("params",)], writes=xk)
                if with_h:
                    mb = mod_base
                    S.add("dve", lambda e: e.tensor_tensor(out=t3, in0=xT[:, :, PWID:NTC], in1=modT[:, (mb + 1) * 8:(mb + 2) * 8, 1:17], op=ALU.mult),
                          reads=xk + [("mod",)], writes=[("stmp",)])
                    S.add("dve", lambda e: e.tensor_tensor(out=h[:, :, PWID:NTC], in0=t3, in1=modT[:, mb * 8:(mb + 1) * 8, 1:17], op=ALU.add),
                          reads=[("stmp",), ("mod",)], writes=[("h", c, "s") for c in range(KC)])
            return outs_p, outs_s

        def resid_add(parts, b, oc, gate_mod):
            for pk in parts:
                if pk == "p":
                    S.add("dve", lambda e, b=b, oc=oc: e.scalar_tensor_tensor(
                        out=P3(xT[:, oc, 0:PWID]), in0=pb_p(b), scalar=modp(gate_mod, oc), in1=P3(xT[:, oc, 0:PWID]),
                        op0=ALU.mult, op1=ALU.add),
                        reads=[("PB", b), ("mod",)], writes=[("xT", oc, "p")])
                    S.add("act", lambda e, oc=oc: e.activation(out=h[:, oc, 0:PWID], in_=xT[:, oc, 0:PWID], func=AF.Copy),
                          reads=[("xT", oc, "p")], writes=[("h", oc, "p")])
                    S.add("dve", lambda e, oc=oc: e.tensor_tensor(out=vsqf[:, oc, 0:PWID], in0=xT[:, oc, 0:PWID], in1=xT[:, oc, 0:PWID], op=ALU.mult),
                          reads=[("xT", oc, "p")], writes=[("vsq", oc)])
                else:
                    S.add("dve", lambda e, b=b, oc=oc: e.tensor_tensor(out=stmp[:, 1, :], in0=pb_s(b), in1=mods(gate_mod, oc), op=ALU.mult),
                          reads=[("PB", b), ("mod",)], writes=[("stmp",)])
                    S.add("dve", lambda e, oc=oc: e.tensor_tensor(out=xT[:, oc, PWID:NTC], in0=stmp[:, 1, :], in1=xT[:, oc, PWID:NTC], op=ALU.add),
                          reads=[("stmp",)], writes=[("xT", oc, "s")])

        def phase_ffn(parts, f, gate_mod, gname, bname, next_mod, with_h, tag):
            win, wout = ffn_w_in[f], ffn_w_out[f]
            S.fence(["hid"], R1N)
            hkeys = [("h", c, pk) for c in range(KC) for pk in parts]
            for blk in range(FC // 2):
                def dm(s, blk=blk):
                    v = slot_k(s, KC, 512)
                    return [(v[:, :, 0:256], wsrc(win, KC, blk * 256, 256)),
                            (v[:, :, 256:512], wsrc(win, KC, DFF + blk * 256, 256))]
                s = wblock(dm)
                for jj in range(2):
                    j = 2 * blk + jj
                    sv = slot_k(s, KC, 512)
                    mm_group(0, parts, lambda k, sv=sv, jj=jj: sv[:, k, jj * 128:(jj + 1) * 128],
                             lambda k, c0, n: h[:, k, c0:c0 + n], KC, reads=[("wslot", s)] + hkeys)
                    mm_group(1, parts, lambda k, sv=sv, jj=jj: sv[:, k, 256 + jj * 128:256 + (jj + 1) * 128],
                             lambda k, c0, n: h[:, k, c0:c0 + n], KC, reads=[("wslot", s)] + hkeys)
                    i = j % 2
                    for pk in parts:
                        if pk == "p":
                            g_ap, u_ap, s_ap, o_ap = pb_p(0), pb_p(1), P3(scr[i][:, 0:PWID]), P3(hid[:, j, 0:PWID])
                        else:
                            g_ap, u_ap, s_ap, o_ap = pb_s(0), pb_s(1), sscr[:, i, :], hid[:, j, PWID:NTC]
                        S.add("act", lambda e, g_ap=g_ap, s_ap=s_ap: e.activation(out=s_ap, in_=g_ap, func=AF.Silu),
                              reads=[("PB", 0)], writes=SK(i, pk))
                        S.add("dve", lambda e, u_ap=u_ap, s_ap=s_ap, o_ap=o_ap: e.tensor_tensor(out=o_ap, in0=u_ap, in1=s_ap, op=ALU.mult),
                              reads=[("PB", 1)] + SK(i, pk), writes=[("hid", j, pk)])
                if len(ada_pending) > 8:
                    ada_more(1)
            dbg_dump("hid" + tag, hid[:, 0:8, :], [("hid", j, pk) for j in range(FC) for pk in parts], is_bf16=True)
            hidkeys = [("hid", j, pk) for j in range(FC) for pk in parts]
            S.fence(["vsq"], R2N)
            for oc in range(KC):
                s = wblock(lambda s, oc=oc: [(slot_k(s, FC, 128), wsrc(wout, FC, oc * 128, 128))])
                sv = slot_k(s, FC, 128)
                b = oc % 2
                mm_group(b, parts, lambda k, sv=sv: sv[:, k, :], lambda k, c0, n: hid[:, k, c0:c0 + n], FC,
                         reads=[("wslot", s)] + hidkeys)
                resid_add(parts, b, oc, gate_mod)
            dbg_dump("v" + tag, xT[:, :, :], [("xT", c, pk) for c in range(KC) for pk in parts])
            layer_norm(parts, lambda c: xT[:, c, :], xT[:, :, PWID:NTC], lambda c, pk: [("xT", c, pk)], EPS_DN, gname,
                       *ln_outs_resid(bname, next_mod, with_h), next_af=(AF.Sigmoid if tag == "1" else None), pre=("p" in parts))
            dbg_dump("x" + tag, xT[:, :, :], [("xT", c, pk) for c in range(KC) for pk in parts])

        def phase_mixer(parts, first_prompt):
            hkeys = [("h", c, pk) for c in range(KC) for pk in parts]
            has_p = "p" in parts
            has_s = "s" in parts
            S.fence(["glu", "gluhalo"], R1N)
            S.fence(["sstage", "scprod"], R2N)
            if has_s:
                scv = sconv.rearrange("(q s) k d -> q (s k) d", s=4)
                spv = spool.rearrange("(q s) k d -> q (s k) d", s=8)
                pairs = [(sc_st[0:120, q, :], scv[q]) for q in range(4)]
                pairs += [(wrep[s4 * 30:(s4 + 1) * 30, :], conv_w[0:30, :]) for s4 in range(4)]
                pairs += [(sp_st[0:120, q, :], spv[q]) for q in range(2)]
                dma_op("sp", "sstate", pairs, writes=[("sstage",)])
                dma_op("sp", "sshift", [(ncs[:, 0:29, :], sconv[:, 1:30, :]), (nps[:, 0:14, :], spool[:, 1:15, :])], out=True)
                for q in range(4):
                    S.add("dve", lambda e, q=q: e.tensor_tensor(out=sc_st[0:120, q, :], in0=sc_st[0:120, q, :], in1=wrep[0:120, :], op=ALU.mult),
                          reads=[("sstage",)], writes=[("scprod", q)])

                def selmm(e):
                    ins = None
                    for hh in range(2):
                        for q in range(4):
                            ins = e.matmul(ps[0:16, 6 + hh, :], selc[0:120, q, :], sc_st[0:120, q, hh * 512:(hh + 1) * 512],
                                           start=(q == 0), stop=(q == 3))
                    return ins
                S.add("pe", selmm, reads=[("scprod", q) for q in range(4)] + [("const",)], writes=[("bank", 6), ("bank", 7)])
                S.add("dve", lambda e: e.tensor_tensor(out=P3(tmA[0:16, :]), in0=ps[0:16, 6:8, :], in1=P3(cbrow[:, 1, :]), op=ALU.add),
                      reads=[("bank", 6), ("bank", 7), ("const",)], writes=[("tmA",)])

                def selpm(e):
                    ins = None
                    for g in range(4):
                        for q in range(2):
                            ins = e.matmul(ps[0:16, 6 + g // 2, (g % 2) * 256:(g % 2) * 256 + 256], selp[0:120, q, g, :],
                                           sp_st[0:120, q, g * 256:(g + 1) * 256], start=(q == 0), stop=(q == 1))
                    return ins
                S.add("pe", selpm, reads=[("sstage",), ("const",)], writes=[("bank", 6), ("bank", 7)])
                S.add("act", lambda e: e.activation(out=P3(mprev[0:16, :]), in_=ps[0:16, 6:8, :], func=AF.Copy),
                      reads=[("bank", 6), ("bank", 7)], writes=[("mprev",)])
            if has_p:
                if first_prompt:
                    S.add("dve", lambda e: e.memset(glub[:, :, 0:30], 0.0), writes=[("gluhalo",)])
                else:
                    S.add("dve", lambda e: e.tensor_copy(out=glub[:, :, 0:30], in_=gl_halo[:, :, :]), reads=[("gl_halo",)],
                          writes=[("gluhalo",)])
            slots = {}
            for c in range(KC):
                blk, cc = c // 4, c % 4
                if cc == 0:
                    slots["bg"] = wblock(lambda s, blk=blk: [(slot_k(s, KC, 512), wsrc(w_in, KC, D + blk * 512, 512))])
                    slots["a"] = wblock(lambda s, blk=blk: [(slot_k(s, KC, 512), wsrc(w_in, KC, blk * 512, 512))])
                sa, sg = slot_k(slots["a"], KC, 512), slot_k(slots["bg"], KC, 512)
                mm_group(1, parts, lambda k, sg=sg, cc=cc: sg[:, k, cc * 128:(cc + 1) * 128], lambda k, c0, n: h[:, k, c0:c0 + n], KC,
                         reads=[("wslot", slots["bg"])] + hkeys)
                mm_group(0, parts, lambda k, sa=sa, cc=cc: sa[:, k, cc * 128:(cc + 1) * 128], lambda k, c0, n: h[:, k, c0:c0 + n], KC,
                         reads=[("wslot", slots["a"])] + hkeys)
                i = c % 2
                for pk in parts:
                    if pk == "p":
                        a_ap, g_ap, s_ap, o_ap = pb_p(0), pb_p(1), P3(scr[i][:, 0:PWID]), P3(glub[:, c, 30:30 + PWID])
                    else:
                        a_ap, g_ap, s_ap, o_ap = pb_s(0), pb_s(1), sscr[:, i, :], glus[:, c, :]
                    S.add("act", lambda e, g_ap=g_ap, s_ap=s_ap: e.activation(out=s_ap, in_=g_ap, func=AF.Sigmoid),
                          reads=[("PB", 1)], writes=SK(i, pk))
                    S.add("dve", lambda e, a_ap=a_ap, s_ap=s_ap, o_ap=o_ap: e.tensor_tensor(out=o_ap, in0=a_ap, in1=s_ap, op=ALU.mult),
                          reads=[("PB", 0)] + SK(i, pk), writes=[("glu", c, pk)])
                    if pk == "p":
                        S.add("dve", lambda e, c=c, i=i: e.tensor_tensor(out=gl_halo[:, c, :], in0=ps[:, 1, 482:512],
                                                                        in1=scr[i][:, PWID - 30:PWID], op=ALU.mult),
                              reads=[("PB", 0), ("gluhalo",)] + SCR(i), writes=[("gl_halo",)])
            S.fence(["conv"], R2N)
            if has_p:
                PT = 24
                dctr = [0]
                for c in range(KC):
                    b = c % 2
                    rk = [("glu", c, "p"), ("gluhalo",), ("params",)]
                    for k in range(PT, 31):
                        wk = cwT[:, k * 8 + c:k * 8 + c + 1]
                        if k == PT:
                            S.add("dve", lambda e, c=c, k=k, wk=wk: e.tensor_scalar(
                                out=conv[:, c, 0:PWID], in0=glub[:, c, k:k + PWID], scalar1=wk, scalar2=pv("conv_b", c),
                                op0=ALU.mult, op1=ALU.add), reads=rk, writes=[("conv", c, "p")])
                        else:
                            S.add("dve", lambda e, c=c, k=k, wk=wk: e.scalar_tensor_tensor(
                                out=conv[:, c, 0:PWID], in0=glub[:, c, k:k + PWID], scalar=wk, in1=conv[:, c, 0:PWID],
                                op0=ALU.mult, op1=ALU.add), reads=rk, writes=[("conv", c, "p")])
                    for k in range(PT):
                        di = dctr[0] % 8
                        dctr[0] += 1
                        wk = cwT[:, k * 8 + c:k * 8 + c + 1]
                        S.add("act", lambda e, di=di, wk=wk: e.activation(out=diag[:, di, :], in_=ident[:, :], func=AF.Copy, scale=wk),
                              reads=[("params",), ("const",)], writes=[("diag", di)])

                        def cmm(e, c=c, k=k, di=di, b=b):
                            e.matmul(ps[:, 3 * b, :], diag[:, di, :], glub[:, c, k:k + 512], start=(k == 0), stop=(k == PT - 1))
                            return e.matmul(ps[:, 3 * b + 1, :], diag[:, di, :], glub[:, c, k + 512:k + 1024], start=(k == 0), stop=(k == PT - 1))
                        S.add("pe", cmm, reads=[("diag", di), ("glu", c, "p"), ("gluhalo",)], writes=[("PB", b)])
                    S.add("dve", lambda e, c=c, b=b: e.tensor_tensor(out=P3(conv[:, c, 0:PWID]), in0=pb_p(b), in1=P3(conv[:, c, 0:PWID]), op=ALU.add),
                          reads=[("PB", b)], writes=[("conv", c, "p")])
                    ada_more(1)
            if has_s:
                def trg(e):
                    ins = None
                    for c in range(KC):
                        ins = e.matmul(ps[0:16, 6 + c // 4, (c % 4) * 128:(c % 4 + 1) * 128], glus[:, c, :], ident[:, :], start=True, stop=True)
                    return ins
                S.add("pe", trg, reads=[("glu", c, "s") for c in range(KC)] + [("const",)], writes=[("bank", 6), ("bank", 7)])
                S.add("act", lambda e: e.activation(out=P3(scr[1][0:16, 0:D]), in_=ps[0:16, 6:8, :], func=AF.Copy),
                      reads=[("bank", 6), ("bank", 7)], writes=SCR(1))
                dma_op("sp", "o_ncs", [(ncs[:, 29, :], scr[1][0:16, 0:D])], reads=SCR(1), out=True)
                S.add("dve", lambda e: e.tensor_tensor(out=scr[0][0:16, 0:D], in0=scr[1][0:16, 0:D], in1=cbrow[:, 0, :], op=ALU.mult),
                      reads=SCR(1) + [("const",)], writes=SCR(0))
                S.add("dve", lambda e: e.tensor_tensor(out=scr[0][0:16, 0:D], in0=scr[0][0:16, 0:D], in1=tmA[0:16, :], op=ALU.add),
                      reads=[("tmA",)], writes=SCR(0))

                def trb(e):
                    ins = None
                    for c in range(KC):
                        ins = e.transpose(ps[:, 6, c * SWID:(c + 1) * SWID], scr[0][0:16, c * 128:(c + 1) * 128], ident[0:16, 0:16])
                    return ins
                S.add("pe", trb, reads=SCR(0) + [("const",)], writes=[("bank", 6)])
                S.add("act", lambda e: e.activation(out=conv[:, :, PWID:NTC], in_=ps[:, 6, 0:KC * SWID].rearrange("p (c t) -> p c t", t=SWID),
                                                    func=AF.Copy),
                      reads=[("bank", 6)], writes=[("conv", c, "s") for c in range(KC)])
            dbg_dump("conv", conv[:, :, :], [("conv", c, pk) for c in range(KC) for pk in parts])
            S.fence(["siluln"], R1N)

            def cl_outs_p(c, tB, tkeys):
                S.add("act", lambda e, c=c, tB=tB: e.activation(out=siluln[:, c, 0:PWID], in_=tB, func=AF.Silu, bias=pv("cln_b", c), scale=1.0),
                      reads=tkeys + [("params",)], writes=[("siluln", c, "p")])

            def cl_outs_s(t3, tkeys):
                bi0 = PV["cln_b"] * 8
                S.add("dve", lambda e: e.tensor_tensor(out=t3, in0=t3, in1=pT[:, bi0:bi0 + 8].unsqueeze(2).to_broadcast([128, KC, SWID]), op=ALU.add),
                      reads=tkeys + [("params",)], writes=[("stmp",)])
                S.add("act", lambda e: e.activation(out=siluln[:, :, PWID:NTC], in_=t3, func=AF.Silu),
                      reads=[("stmp",)], writes=[("siluln", c, "s") for c in range(KC)])
            layer_norm(parts, lambda c: conv[:, c, :], conv[:, :, PWID:NTC], lambda c, pk: [("conv", c, pk)], EPS, "cln_g", cl_outs_p, cl_outs_s, next_af=AF.Sigmoid)
            dbg_dump("siluln", siluln[:, :, :], [("siluln", c, pk) for c in range(KC) for pk in parts], is_bf16=True)
            S.fence(["upb", "tmb"], R2N)
            S.fence(["mixs", "pooled"], ["hid", "xin", "xs_in", "yst", "glu", "gluhalo", "mixs", "pooled"])
            def emit_m3(g):
                for jh in range(2):
                    oc = 2 * g + jh
                    bb = jh
                    pk_g = [("pooled", 2 * (g % 3) + t, pk) for t in range(2) for pk in parts]
                    mm_group(bb, parts, lambda k, g=g, jh=jh: poolw[:, g, k, jh * 128:(jh + 1) * 128],
                             lambda k, c0, n, g=g: pooled[:, 2 * (g % 3) + k, c0:c0 + n], 2, reads=pk_g + [("poolw",)])
                    for pk in parts:
                        if pk == "p":
                            i_ap, o_ap = pb_p(bb), P3(mixs[:, oc, 0:PWID])
                        else:
                            i_ap, o_ap = pb_s(bb), mixs[:, oc, PWID:NTC]
                        S.add("act", lambda e, i_ap=i_ap, o_ap=o_ap, oc=oc: e.activation(out=o_ap, in_=i_ap, func=AF.Copy,
                                                                                     scale=pv("pscale", oc)),
                              reads=[("PB", bb), ("params",)], writes=[("mixs", oc, pk)])

            def s_stage1(c):
                w = POOL_W[c // 2]
                ut = tmbuf[0:16, 2 * (c % 2), 0:128]
                pt = tmbuf[0:16, 2 * (c % 2) + 1, 0:128]
                S.add("pe", lambda e, c=c: e.matmul(ps[0:16, 6, 0:128], glus[:, c, :], ident[:, :], start=True, stop=True),
                      reads=[("usfm", c), ("const",)], writes=[("bank", 6)])
                S.add("act", lambda e, ut=ut: e.activation(out=ut, in_=ps[0:16, 6, 0:128], func=AF.Copy), reads=[("bank", 6)],
                      writes=[("tmb", 2 * (c % 2))])
                dma_op("sp", "o_nps%d" % (c % 2), [(nps[:, 14, c * 128:(c + 1) * 128], ut)], reads=[("tmb", 2 * (c % 2))], out=True)
                S.add("dve", lambda e, ut=ut, pt=pt, c=c, w=w: e.scalar_tensor_tensor(
                    out=pt, in0=ut, scalar=(1.0 / w - 1.0), in1=mprev[0:16, c * 128:(c + 1) * 128], op0=ALU.mult, op1=ALU.add),
                    reads=[("tmb", 2 * (c % 2)), ("mprev",)], writes=[("tmb", 2 * (c % 2) + 1)])

            def s_stage2(c):
                g = c // 2
                pidx = 2 * (g % 3) + (c % 2)
                pt = tmbuf[0:16, 2 * (c % 2) + 1, 0:128]
                S.add("pe", lambda e, pt=pt: e.transpose(ps[:, 7, 0:SWID], pt, ident[0:16, 0:16]), reads=[("tmb", 2 * (c % 2) + 1), ("const",)],
                      writes=[("bank", 7)])
                S.add("act", lambda e, pidx=pidx: e.activation(out=pooled[:, pidx, PWID:NTC], in_=ps[:, 7, 0:SWID], func=AF.Copy),
                      reads=[("bank", 7)], writes=[("pooled", pidx, "s")])

            uslot = None
            for c in range(KC):
                g = c // 2
                w = POOL_W[g]
                if c % 4 == 0:
                    uslot = wblock(lambda s, c=c: [(slot_k(s, KC, 512), wsrc(w_in, KC, 2 * D + (c // 4) * 512, 512))])
                su = slot_k(uslot, KC, 512)
                b = c % 2
                mm_group(b, parts, lambda k, su=su, c=c: su[:, k, (c % 4) * 128:(c % 4 + 1) * 128], lambda k, c0, n: h[:, k, c0:c0 + n], KC,
                         reads=[("wslot", uslot)] + hkeys)
                if has_s:
                    if c >= 3:
                        s_stage2(c - 3)
                    if c >= 1:
                        s_stage1(c - 1)
                U, Pb, Qb = upb[:, 3 * b + 0, :], upb[:, 3 * b + 1, :], upb[:, 3 * b + 2, :]
                uk = [("upb", 3 * b + t) for t in range(3)]
                pidx = 2 * (g % 3) + (c % 2)
                pl = pooled[:, pidx, :]
                if has_p:
                    E = 15 + PWID
                    if first_prompt:
                        S.add("dve", lambda e, U=U: e.memset(U[:, 0:15], 0.0), writes=[uk[0]])
                    else:
                        S.add("dve", lambda e, U=U, c=c: e.tensor_copy(out=U[:, 0:15], in_=up_halo[:, c, :]), reads=[("up_halo", c)],
                              writes=[uk[0]])
                    S.add("act", lambda e, U=U, b=b: e.activation(out=P3(U[:, 15:E]), in_=pb_p(b), func=AF.Copy),
                          reads=[("PB", b)], writes=[uk[0]])
                    S.add("act", lambda e, U=U, c=c: e.activation(out=up_halo[:, c, :], in_=U[:, PWID:PWID + 15], func=AF.Copy), reads=[uk[0]],
                          writes=[("up_halo", c)])
                    S.add("dve", lambda e, U=U, Pb=Pb: e.tensor_tensor(out=Pb[:, 1:E], in0=U[:, 1:E], in1=U[:, 0:E - 1], op=ALU.add),
                          reads=[uk[0]], writes=[uk[1]])
                    Sb, skey = Pb, uk[1]
                    if w >= 4:
                        S.add("dve", lambda e, Pb=Pb, Qb=Qb: e.tensor_tensor(out=Qb[:, 3:E], in0=Pb[:, 3:E], in1=Pb[:, 1:E - 2], op=ALU.add),
                              reads=[uk[1]], writes=[uk[2]])
                        Sb, skey = Qb, uk[2]
                    if w >= 8:
                        S.add("dve", lambda e, Pb=Pb, Qb=Qb: e.tensor_tensor(out=Pb[:, 7:E], in0=Qb[:, 7:E], in1=Qb[:, 3:E - 4], op=ALU.add),
                              reads=[uk[2]], writes=[uk[1]])
                        Sb, skey = Pb, uk[1]
                    if w >= 16:
                        S.add("dve", lambda e, Pb=Pb, Qb=Qb: e.tensor_tensor(out=Qb[:, 15:E], in0=Pb[:, 15:E], in1=Pb[:, 7:E - 8], op=ALU.add),
                              reads=[uk[1]], writes=[uk[2]])
                        Sb, skey = Qb, uk[2]
                    S.add("dve", lambda e, Sb=Sb, U=U, pl=pl, w=w: e.scalar_tensor_tensor(
                        out=pl[:, 0:PWID], in0=Sb[:, 15:E], scalar=1.0 / w, in1=U[:, 15:E], op0=ALU.mult, op1=ALU.subtract),
                        reads=[skey, uk[0]], writes=[("pooled", pidx, "p")])
                    if first_prompt:
                        S.add("dve", lambda e, Sb=Sb, g=g: e.tensor_tensor(out=stmp[:, 2, :], in0=Sb[:, 15:31], in1=invc[:, g, :], op=ALU.mult),
                              reads=[skey, ("const",)], writes=[("stmp",)])
                        S.add("dve", lambda e, U=U, pl=pl: e.tensor_tensor(out=pl[:, 0:16], in0=stmp[:, 2, :], in1=U[:, 15:31], op=ALU.subtract),
                              reads=[("stmp",), uk[0]], writes=[("pooled", pidx, "p")])
                if has_s:
                    S.add("act", lambda e, b=b, c=c: e.activation(out=glus[:, c, :], in_=pb_s(b), func=AF.Copy), reads=[("PB", b)],
                          writes=[("usfm", c)])
                if c % 2 == 1 and g >= 2:
                    emit_m3(g - 2)
            if has_s:
                s_stage2(KC - 3)
                s_stage1(KC - 1)
                s_stage2(KC - 2)
                s_stage2(KC - 1)
            emit_m3(2)
            emit_m3(3)
            dbg_dump("mixs", mixs[:, :, :], [("mixs", c, pk) for c in range(KC) for pk in parts], is_bf16=True)
            S.fence(["merged", "m1h"], R2N)
            slk = [("siluln", c, pk) for c in range(KC) for pk in parts]
            mxk = [("mixs", c, pk) for c in range(KC) for pk in parts]
            for half in range(2):
                s_ga = wblock(lambda s, half=half: [(slot_k(s, KC, 512), wsrc(w_in, KC, 3 * D + half * 512, 512))])
                s_co = wblock(lambda s, half=half: [(slot_k(s, KC, 512), wsrc(w_conv_out, KC, half * 512, 512))])
                for cc in range(4):
                    oc = half * 4 + cc
                    sv_g, sv_c = slot_k(s_ga, KC, 512), slot_k(s_co, KC, 512)
                    mm_group(0, parts, lambda k, sv_g=sv_g, cc=cc: sv_g[:, k, cc * 128:(cc + 1) * 128], lambda k, c0, n: h[:, k, c0:c0 + n], KC,
                             reads=[("wslot", s_ga)] + hkeys)
                    mm_group(1, parts, lambda k, sv_c=sv_c, cc=cc: sv_c[:, k, cc * 128:(cc + 1) * 128],
                             lambda k, c0, n: siluln[:, k, c0:c0 + n], KC, reads=[("wslot", s_co)] + slk)
                    i = oc % 2
                    for pk in parts:
                        if pk == "p":
                            g_ap, y_ap, s_ap, o_ap = pb_p(0), pb_p(1), P3(scr[i][:, 0:PWID]), P3(m1h[:, cc, 0:PWID])
                        else:
                            g_ap, y_ap, s_ap, o_ap = pb_s(0), pb_s(1), sscr[:, i, :], m1h[:, cc, PWID:NTC]
                        S.add("act", lambda e, g_ap=g_ap, s_ap=s_ap: e.activation(out=s_ap, in_=g_ap, func=AF.Sigmoid),
                              reads=[("PB", 0)], writes=SK(i, pk))
                        S.add("dve", lambda e, y_ap=y_ap, s_ap=s_ap, o_ap=o_ap: e.tensor_tensor(out=o_ap, in0=y_ap, in1=s_ap, op=ALU.mult),
                              reads=[("PB", 1)] + SK(i, pk), writes=[("m1h", cc, pk)])
                s_gb = wblock(lambda s, half=half: [(slot_k(s, KC, 512), wsrc(w_in, KC, 4 * D + half * 512, 512))])
                s_po = wblock(lambda s, half=half: [(slot_k(s, KC, 512), wsrc(w_pool_out, KC, half * 512, 512))])
                for cc in range(4):
                    oc = half * 4 + cc
                    sv_g, sv_c = slot_k(s_gb, KC, 512), slot_k(s_po, KC, 512)
                    mm_group(0, parts, lambda k, sv_g=sv_g, cc=cc: sv_g[:, k, cc * 128:(cc + 1) * 128], lambda k, c0, n: h[:, k, c0:c0 + n], KC,
                             reads=[("wslot", s_gb)] + hkeys)
                    mm_group(1, parts, lambda k, sv_c=sv_c, cc=cc: sv_c[:, k, cc * 128:(cc + 1) * 128],
                             lambda k, c0, n: mixs[:, k, c0:c0 + n], KC, reads=[("wslot", s_po)] + mxk)
                    i = oc % 2
                    for pk in parts:
                        if pk == "p":
                            g_ap, y_ap, s_ap, m_ap, o_ap = pb_p(0), pb_p(1), P3(scr[i][:, 0:PWID]), P3(m1h[:, cc, 0:PWID]), P3(merged[:, oc, 0:PWID])
                        else:
                            g_ap, y_ap, s_ap, m_ap, o_ap = pb_s(0), pb_s(1), sscr[:, i, :], m1h[:, cc, PWID:NTC], merged[:, oc, PWID:NTC]
                        S.add("act", lambda e, g_ap=g_ap, s_ap=s_ap: e.activation(out=s_ap, in_=g_ap, func=AF.Sigmoid),
                              reads=[("PB", 0)], writes=SK(i, pk))
                        S.add("dve", lambda e, y_ap=y_ap, s_ap=s_ap: e.tensor_tensor(out=s_ap, in0=y_ap, in1=s_ap, op=ALU.mult),
                              reads=[("PB", 1)], writes=SK(i, pk))
                        S.add("dve", lambda e, s_ap=s_ap, m_ap=m_ap, o_ap=o_ap: e.tensor_tensor(out=o_ap, in0=s_ap, in1=m_ap, op=ALU.add),
                              reads=SK(i, pk) + [("m1h", cc, pk)], writes=[("merged", oc, pk)])
            dbg_dump("merged", merged[:, :, :], [("merged", c, pk) for c in range(KC) for pk in parts], is_bf16=True)
            mgk = [("merged", c, pk) for c in range(KC) for pk in parts]
            S.fence(["vsq"], ["m1h", "upb", "tmb", "conv", "sstage", "scprod", "yst", "vsq"])
            for half in range(2):
                s_o = wblock(lambda s, half=half: [(slot_k(s, KC, 512), wsrc(w_out, KC, half * 512, 512))])
                for cc in range(4):
                    oc = half * 4 + cc
                    sv = slot_k(s_o, KC, 512)
                    b = oc % 2
                    mm_group(b, parts, lambda k, sv=sv, cc=cc: sv[:, k, cc * 128:(cc + 1) * 128], lambda k, c0, n: merged[:, k, c0:c0 + n], KC,
                             reads=[("wslot", s_o)] + mgk)
                    resid_add(parts, b, oc, 5)
            layer_norm(parts, lambda c: xT[:, c, :], xT[:, :, PWID:NTC], lambda c, pk: [("xT", c, pk)], EPS_DN, "ln2_g",
                       *ln_outs_resid("ln2_b", 6, True), next_af=AF.Silu, pre=("p" in parts))
            dbg_dump("x2", xT[:, :, :], [("xT", c, pk) for c in range(KC) for pk in parts])

        def phase_out(parts, row0):
            S.fence(["yst"], R2N)
            S.fence(["bank"], ["PB"])
            n = 0
            for pk in parts:
                if pk == "p":
                    for r in range(8):
                        bp = (n % 3) * 2
                        sl = n % 4
                        n += 1

                        def tr(e, r=r, bp=bp):
                            ins = None
                            for c in range(KC):
                                ins = e.transpose(ps[:, bp + c // 4, (c % 4) * 128:(c % 4 + 1) * 128], xT[:, c, r * 128:(r + 1) * 128], ident[:, :])
                            return ins
                        S.add("pe", tr, reads=[("xT", c, "p") for c in range(KC)] + [("const",)],
                              writes=[("bank", bp), ("bank", bp + 1)])
                        if r % 2 == 0:
                            S.add("act", lambda e, bp=bp, sl=sl: e.activation(out=P3(yst[:, sl, :]), in_=ps[:, bp:bp + 2, :], func=AF.Copy),
                                  reads=[("bank", bp), ("bank", bp + 1)], writes=[("yst", sl)])
                        else:
                            S.add("dve", lambda e, bp=bp, sl=sl: e.tensor_copy(out=P3(yst[:, sl, :]), in_=ps[:, bp:bp + 2, :]),
                                  reads=[("bank", bp), ("bank", bp + 1)], writes=[("yst", sl)])
                        dma_op("sp", "o_y%d" % sl, [(y_p[row0 + r * 128: row0 + (r + 1) * 128, :], yst[:, sl, :])], reads=[("yst", sl)], out=True)
                else:
                    def tr(e):
                        ins = None
                        for c in range(KC):
                            ins = e.matmul(ps[0:16, 6 + c // 4, (c % 4) * 128:(c % 4 + 1) * 128], xT[:, c, PWID:NTC], ident[:, :], start=True, stop=True)
                        return ins
                    S.add("pe", tr, reads=[("xT", c, "s") for c in range(KC)] + [("const",)], writes=[("bank", 6), ("bank", 7)])
                    S.add("act", lambda e: e.activation(out=P3(scr[1][0:16, 0:D]), in_=ps[0:16, 6:8, :], func=AF.Copy),
                          reads=[("bank", 6), ("bank", 7)], writes=SCR(1))
                    dma_op("sp", "o_ys", [(y_s[:, :], scr[1][0:16, 0:D])], reads=SCR(1), out=True)
            S.fence(["PB"], ["bank"])

        def phase_final_states():
            def tr(e):
                ins = None
                for c in range(KC):
                    ins = e.matmul(ps[0:30, 6 + c // 4, (c % 4) * 128:(c % 4 + 1) * 128], gl_halo[:, c, :], ident[:, :], start=True, stop=True)
                return ins
            S.add("pe", tr, reads=[("gl_halo",), ("const",)], writes=[("bank", 6), ("bank", 7)])
            S.add("act", lambda e: e.activation(out=P3(scr[0][0:30, 0:D]), in_=ps[0:30, 6:8, :], func=AF.Copy),
                  reads=[("bank", 6), ("bank", 7)], writes=SCR(0))
            dma_op("sp", "o_fs", [(ncp[:, :], scr[0][0:30, 0:D])], reads=SCR(0), out=True)

            def tr2(e):
                ins = None
                for c in range(KC):
                    ins = e.matmul(ps[0:15, 6 + c // 4, (c % 4) * 128:(c % 4 + 1) * 128], up_halo[:, c, :], ident[:, :], start=True, stop=True)
                return ins
            S.add("pe", tr2, reads=[("up_halo", c) for c in range(KC)] + [("const",)], writes=[("bank", 6), ("bank", 7)])
            S.add("act", lambda e: e.activation(out=P3(scr[1][0:15, 0:D]), in_=ps[0:15, 6:8, :], func=AF.Copy),
                  reads=[("bank", 6), ("bank", 7)], writes=SCR(1))
            dma_op("sp", "o_fs", [(npp[:, :], scr[1][0:15, 0:D])], reads=SCR(1), out=True)

        stop = None
        stop_g = 0
        if debug and ":" in debug:
            parts_ = debug.split(":")
            debug, stop = parts_[0], parts_[1]
            if len(parts_) > 2:
                stop_g = int(parts_[2])
        phase_setup()
        prow = 0
        pg = 0
        x_load(groups_cfg[0][1], 0)
        for gi, (gname, parts) in enumerate(groups_cfg):
            has_p = "p" in parts
            phase_x(parts, prow)
            if stop == "x" and gi == stop_g:
                break
            phase_ffn(parts, 0, 2, "ln1_g", "ln1_b", 3, True, "1")
            if stop == "ffn1" and gi == stop_g:
                break
            phase_mixer(parts, first_prompt=(pg == 0))
            if stop == "mixer" and gi == stop_g:
                break
            phase_ffn(parts, 1, 8, "ln3_g", "ln3_b", 0, False, "3")
            if stop == "ffn3" and gi == stop_g:
                break
            if gi + 1 < len(groups_cfg):
                x_load(groups_cfg[gi + 1][1], prow + (PWID if has_p else 0))
            phase_out(parts, prow)
            if stop == "out" and gi == stop_g:
                break
            if has_p:
                prow += PWID
                pg += 1
        if stop is None:
            phase_final_states()

        final = S.add("sp", lambda e: None, extra=OUT_OPS)
        S.finalize()
        prog_sems = {k: es.enter_context(nc.semaphore("prog_" + k)) for k in Sched.ENGS}
        for n in sem_names:
            SEMS[n] = es.enter_context(nc.semaphore("d_" + n))
        with nc.Block() as block:
            @block.tensor
            def _(e):
                S.emit("pe", e, prog_sems, SEMS)

            @block.scalar
            def _(e):
                S.emit("act", e, prog_sems, SEMS)

            @block.vector
            def _(e):
                S.emit("dve", e, prog_sems, SEMS)

            @block.gpsimd
            def _(e):
                S.emit("pool", e, prog_sems, SEMS)

            @block.sync
            def _(e):
                S.ops["sp"].remove(final)
                S.emit("sp", e, prog_sems, SEMS)
                for d in final.deps:
                    e.wait_ge(SEMS[d.dma_sem], S.dma_counts[d.dma_sem])
    return nc


_NC_CACHE = {}


def _consts():
    ident = np.eye(128, dtype=np.float32)
    selc = np.zeros((120, 4, 16), np.float32)
    for q in range(4):
        for s in range(4):
            selc[s * 30:(s + 1) * 30, q, 4 * q + s] = 1.0
    selp = np.zeros((120, 2, 4, 16), np.float32)
    for q in range(2):
        for g, w in enumerate(POOL_W):
            for s in range(8):
                for k in range(15):
                    if k >= 16 - w:
                        selp[s * 15 + k, q, g, 8 * q + s] = 1.0 / w
    invc = np.zeros((128, 4, 16), np.float32)
    for g, w in enumerate(POOL_W):
        for t in range(16):
            invc[:, g, t] = 1.0 / min(w, t + 1)
    return ident, selc, selp, invc


def make_in_maps(x_prompt, x_sample, state_conv, state_pool, c_prompt, c_sample,
                 w_ada, b_ada, ffn1_w_in, ffn1_w_out, ln1_g, ln1_b,
                 w_in, conv_w, conv_b, conv_ln_g, conv_ln_b, w_conv_out,
                 pool_w, pool_scale, w_pool_out, w_out, ln2_g, ln2_b,
                 ffn2_w_in, ffn2_w_out, ln3_g, ln3_b):
    f = lambda a: np.ascontiguousarray(np.asarray(a, dtype=np.float32))
    ident, selc, selp, invc = _consts()
    pvec = np.concatenate([f(v)[0].reshape(8, 128) for v in
                           (ln1_g, ln1_b, ln2_g, ln2_b, ln3_g, ln3_b, conv_b, conv_ln_g, conv_ln_b, pool_scale)], axis=0)
    cb_row = np.stack([f(conv_w)[0, 30], f(conv_b)[0]], axis=0)
    shared = {
        "w_ada": f(w_ada)[0], "b_ada": f(b_ada)[0].reshape(72, 128),
        "ffn1_w_in": f(ffn1_w_in)[0], "ffn2_w_in": f(ffn2_w_in)[0],
        "ffn1_w_out": f(ffn1_w_out)[0], "ffn2_w_out": f(ffn2_w_out)[0],
        "w_in": f(w_in)[0], "conv_w": f(conv_w)[0], "w_conv_out": f(w_conv_out)[0],
        "pool_w": f(pool_w)[0], "w_pool_out": f(w_pool_out)[0], "w_out": f(w_out)[0],
        "pvec": np.ascontiguousarray(pvec), "cb_row": np.ascontiguousarray(cb_row),
        "ident": ident, "selc": selc, "selp": selp, "invc": invc,
    }
    xp, xs = f(x_prompt), f(x_sample)
    sc, sp = f(state_conv)[0], f(state_pool)[0]
    cp, cs = f(c_prompt), f(c_sample)
    in_maps = []
    for i in range(8):
        m = dict(shared)
        m["x_p"] = xp[i]
        m["x_s"] = np.ascontiguousarray(xs[16 * i:16 * i + 16, 0, :])
        m["sconv"] = np.ascontiguousarray(sc[16 * i:16 * i + 16])
        m["spool"] = np.ascontiguousarray(sp[16 * i:16 * i + 16])
        m["c_all"] = np.ascontiguousarray(np.concatenate([cp[i:i + 1], cs[16 * i:16 * i + 16]], axis=0))
        in_maps.append(m)
    return in_maps


def kernel(**inputs):
    if "nc" not in _NC_CACHE:
        _NC_CACHE["nc"] = build_program()
    nc = _NC_CACHE["nc"]
    in_maps = make_in_maps(**inputs)
    res = run_bass_kernel_spmd(nc, in_maps, core_ids=list(range(8)))
    R = res.results
    y_prompt = np.stack([R[i]["y_p"] for i in range(8)], axis=0)
    y_sample = np.concatenate([R[i]["y_s"] for i in range(8)], axis=0)[:, None, :]
    ncp = np.stack([R[i]["ncp"] for i in range(8)], axis=0)[None]
    npp = np.stack([R[i]["npp"] for i in range(8)], axis=0)[None]
    ncs = np.concatenate([R[i]["ncs"] for i in range(8)], axis=0)[None]
    nps = np.concatenate([R[i]["nps"] for i in range(8)], axis=0)[None]
    return (y_prompt.astype(np.float32), y_sample.astype(np.float32), ncp.astype(np.float32),
            npp.astype(np.float32), ncs.astype(np.float32), nps.astype(np.float32))
```

```python
import numpy as np
from contextlib import ExitStack
import concourse.bass as bass
import concourse.mybir as mybir
from concourse.bass_utils import run_bass_kernel_spmd

F32 = mybir.dt.float32
BF16 = mybir.dt.bfloat16
AF = mybir.ActivationFunctionType
ALU = mybir.AluOpType
AX = mybir.AxisListType

D = 1024
KC = 8
DFF = 2816
FC = 22
PWID = 1024
SWID = 16
NTC = PWID + SWID
EPS = 1e-5
ALPHA = 2.0 ** 0.25
EPS_DN = EPS / (ALPHA * ALPHA)
POOL_W = (2, 4, 8, 16)
NSLOT = 4
WARM_C = 5
WARM_N = 20
SLOT_ELEMS = 4096


class Op:
    __slots__ = ("eng", "fn", "deps", "marked", "count", "dma_sem", "dma_val", "idx")

    def __init__(self, eng, fn, deps):
        self.eng = eng
        self.fn = fn
        self.deps = deps
        self.marked = False
        self.count = 0
        self.dma_sem = None
        self.dma_val = 0


class Sched:
    ENGS = ("pe", "act", "dve", "pool", "sp")

    def __init__(self):
        self.ops = {e: [] for e in self.ENGS}
        self.last_w = {}
        self.readers = {}
        self.dma_counts = {}
        self.all_ops = []
        self.known = set()
        self.pending = {}

    def add(self, eng, fn, reads=(), writes=(), dma_sem=None, n_dma=0, extra=()):
        deps = []
        seen = set()

        def push(o):
            if o is not None and id(o) not in seen:
                seen.add(id(o))
                deps.append(o)

        for k in list(reads) + list(writes):
            if k not in self.known:
                self.known.add(k)
                if k[0] in self.pending:
                    self.readers.setdefault(k, []).extend(self.pending[k[0]])
        for k in reads:
            push(self.last_w.get(k))
        for k in writes:
            push(self.last_w.get(k))
            for r in self.readers.get(k, ()):
                push(r)
        for o in extra:
            push(o)
        op = Op(eng, fn, deps)
        if dma_sem is not None:
            c = self.dma_counts.get(dma_sem, 0) + 16 * n_dma
            self.dma_counts[dma_sem] = c
            op.dma_sem = dma_sem
            op.dma_val = c
        for k in reads:
            self.readers.setdefault(k, []).append(op)
        for k in writes:
            self.last_w[k] = op
            self.readers[k] = []
        self.ops[eng].append(op)
        self.all_ops.append(op)
        return op

    def fence(self, new_names, old_names):
        olds = []
        seen = set()
        for k in list(self.known):
            if k[0] in old_names:
                for o in [self.last_w.get(k)] + list(self.readers.get(k, ())):
                    if o is not None and id(o) not in seen:
                        seen.add(id(o))
                        olds.append(o)
        for n in new_names:
            self.pending[n] = list(olds)
        for k in list(self.known):
            if k[0] in new_names:
                self.readers.setdefault(k, []).extend(olds)

    def finalize(self):
        for op in self.all_ops:
            for d in op.deps:
                if d.dma_sem is None:
                    if d.eng == "pe" and op.eng == "pe":
                        continue
                    d.marked = True
        for e in self.ENGS:
            c = 0
            for op in self.ops[e]:
                if op.marked:
                    c += 1
                    op.count = c

    def emit(self, eng, handle, prog_sems, dma_sems):
        waited = {}
        for op in self.ops[eng]:
            for d in op.deps:
                if d.dma_sem is not None:
                    key, val, sem = ("dma", d.dma_sem), d.dma_val, dma_sems[d.dma_sem]
                else:
                    if d.eng == "pe" and eng == "pe":
                        continue
                    key, val, sem = ("eng", d.eng), d.count, prog_sems[d.eng]
                if waited.get(key, 0) < val:
                    handle.wait_ge(sem, val)
                    waited[key] = val
            ins = op.fn(handle)
            if op.marked:
                ins.then_inc(prog_sems[eng], 1)


def build_program(groups_cfg=None, debug=None):
    nc = bass.Bass("TRN2", target_bir_lowering=False)
    S = Sched()

    def din(name, shape):
        return nc.dram_tensor(name, list(shape), F32, kind="ExternalInput").ap()

    def dout(name, shape):
        return nc.dram_tensor(name, list(shape), F32, kind="ExternalOutput").ap()

    x_p = din("x_p", [2048, D])
    x_s = din("x_s", [SWID, D])
    sconv = din("sconv", [SWID, 30, D])
    spool = din("spool", [SWID, 15, D])
    c_all = din("c_all", [17, D])
    w_ada = din("w_ada", [D, 9 * D])
    b_ada = din("b_ada", [72, 128])
    ffn_w_in = [din("ffn1_w_in", [D, 2 * DFF]), din("ffn2_w_in", [D, 2 * DFF])]
    ffn_w_out = [din("ffn1_w_out", [DFF, D]), din("ffn2_w_out", [DFF, D])]
    w_in = din("w_in", [D, 5 * D])
    conv_w = din("conv_w", [31, D])
    w_conv_out = din("w_conv_out", [D, D])
    pool_w = din("pool_w", [4, 256, 256])
    w_pool_out = din("w_pool_out", [D, D])
    w_out = din("w_out", [D, D])
    pvec = din("pvec", [80, 128])
    cb_row = din("cb_row", [2, D])
    ident_d = din("ident", [128, 128])
    selc_d = din("selc", [120, 4, 16])
    selp_d = din("selp", [120, 2, 4, 16])
    invc_d = din("invc", [128, 4, 16])

    y_p = dout("y_p", [2048, D])
    y_s = dout("y_s", [SWID, D])
    ncp = dout("ncp", [30, D])
    npp = dout("npp", [15, D])
    ncs = dout("ncs", [SWID, 30, D])
    nps = dout("nps", [SWID, 15, D])
    dbg = dout("dbg", [128, KC * NTC]) if debug else None

    PV = {n: i for i, n in enumerate(
        ["ln1_g", "ln1_b", "ln2_g", "ln2_b", "ln3_g", "ln3_b", "conv_b", "cln_g", "cln_b", "pscale"])}
    R1N = ["hid", "xin", "xs_in", "glu", "gluhalo", "siluln", "mixs", "pooled"]
    R2N = ["conv", "merged", "m1h", "upb", "tmb", "sstage", "scprod", "setupR2", "yst", "vsq"]

    es = ExitStack()
    with es:
        def sb(name, shape, dt=F32):
            return es.enter_context(nc.sbuf_tensor(name, list(shape), dt))

        ident = sb("ident_sb", [128, 128])
        ones_bf = sb("ones_bf", [128, 128], BF16)
        pT = sb("pT", [128, 80])
        baT = sb("baT", [128, 72])
        cwT = sb("cwT", [128, 248])
        modT = sb("modT", [128, 72, 17])
        cT = sb("cT", [128, KC, 17], BF16)
        invc = sb("invc_sb", [128, 4, 16])
        selc = sb("selc_sb", [120, 4, 16])
        selp = sb("selp_sb", [120, 2, 4, 16])
        cbrow = sb("cbrow", [16, 2, D])
        xT = sb("xT", [128, KC, NTC])
        h = sb("h", [128, KC, NTC], BF16)
        R1 = sb("R1", [128, 11440])
        R2 = sb("R2", [128, KC * NTC])
        scr = [sb("scr0", [128, NTC]), sb("scr1", [128, NTC])]
        ring = sb("ring", [128, NSLOT, SLOT_ELEMS], BF16)
        poolw = sb("poolw", [128, 4, 2, 256], BF16)
        gl_halo = sb("gl_halo", [128, KC, 30])
        up_halo = sb("up_halo", [128, KC, 15])
        stmp = sb("stmp", [128, 8, 16])
        rstd_sb = sb("rstd_sb", [128, NTC])
        glus = sb("glus", [128, KC, SWID])
        lns = sb("lns", [128, 2, KC, SWID], BF16)
        epsT = sb("epsT", [128, 2])
        sscr = sb("sscr", [128, 2, SWID])
        jnk = sb("jnk", [128, 2])
        warm_src = sb("warm_src", [128, 512], BF16)
        diag = sb("diag", [128, 8, 128], BF16)
        tmA = sb("tmA", [16, D])
        mprev = sb("mprev_sb", [16, D])
        ps = es.enter_context(nc.psum_tensor("ps", [128, 8, 512], F32))

        hid = R1[:, :].bitcast(BF16).rearrange("p (j n) -> p j n", n=NTC)
        xin = R1[:, 0:8192].rearrange("p (s r d) -> p s r d", s=2, r=4)
        xs_in = R1[:, 8192:9216]
        yst = R2[:, 0:4096].rearrange("p (s d) -> p s d", s=4)
        glu = R1[:, 0:KC * 1070].rearrange("p (c n) -> p c n", n=1070)
        glub = R1[:, 0:KC * 535].bitcast(BF16).rearrange("p (c n) -> p c n", n=1070)
        siluln = R1[:, 0:4160].bitcast(BF16).rearrange("p (c n) -> p c n", n=NTC)
        mixs = R1[:, 4160:8320].bitcast(BF16).rearrange("p (c n) -> p c n", n=NTC)
        pooled = R1[:, 8320:11440].bitcast(BF16).rearrange("p (c n) -> p c n", n=NTC)
        conv = R2[:, :].rearrange("p (c n) -> p c n", n=NTC)
        merged = R2[:, 0:4160].bitcast(BF16).rearrange("p (c n) -> p c n", n=NTC)
        m1h = R2[:, 4160:8320].rearrange("p (c n) -> p c n", n=NTC)
        vsqf = R2[:, 4160:8320].bitcast(BF16).rearrange("p (c n) -> p c n", n=NTC)
        upb = R2[:, 0:6 * 1056].rearrange("p (b n) -> p b n", n=1056)
        tmbuf = R2[:, 6400:8320].rearrange("p (b n) -> p b n", n=384)
        sc_st = R2[:, 0:4096].rearrange("p (q d) -> p q d", q=4)
        wrep = R2[:, 4096:5120]
        sp_st = R2[:, 5120:7168].rearrange("p (q d) -> p q d", q=2)

        def pb_p(b):
            return ps[:, 3 * b:3 * b + 2, :]

        def pb_s(b):
            return ps[:, 3 * b + 2, 0:SWID]

        sem_names = []
        SEMS = {}
        OUT_OPS = []

        def semname(n):
            if n not in sem_names:
                sem_names.append(n)
            return n

        if groups_cfg is None:
            groups_cfg = [("A", ["p"]), ("B", ["p", "s"])]

        def P3(ap2):
            return ap2.rearrange("p (t n) -> p t n", n=512)

        def SCR(i):
            return [("scr", i, "a"), ("scr", i, "b")]

        def SK(i, pk):
            return SCR(i) if pk == "p" else [("sscr", i)]

        def dma_op(eng, sem, pairs, reads=(), writes=(), out=False):
            sem = semname(sem)

            def fn(e):
                ins = None
                for (dst, src) in pairs:
                    ins = e.dma_start(out=dst, in_=src)
                    ins.then_inc(SEMS[sem], 16)
                return ins
            op = S.add(eng, fn, reads=reads, writes=writes, dma_sem=sem, n_dma=len(pairs))
            if out:
                OUT_OPS.append(op)
            return op

        def dbg_dump(name, ap, keys, is_bf16=False):
            if debug != name:
                return
            n = ap.shape[-1] if len(ap.shape) == 2 else None
            if len(ap.shape) == 3:
                dst = dbg[:, 0:ap.shape[1] * ap.shape[2]].rearrange("p (c n) -> p c n", n=ap.shape[2])
            else:
                dst = dbg[:, 0:n]
            dma_op("pool" if is_bf16 else "sp", "dbg", [(dst, ap)], reads=keys, out=True)

        def mm_group(pbuf, parts, lhs_fn, rhs_fn, nk, reads):
            tiles = []
            if "s" in parts:
                tiles.append((ps[:, 3 * pbuf + 2, 0:SWID], PWID, SWID))
            if "p" in parts:
                tiles.append((ps[:, 3 * pbuf, :], 0, 512))
                tiles.append((ps[:, 3 * pbuf + 1, :], 512, 512))

            def fn(e):
                ins = None
                for k in range(nk):
                    lt = lhs_fn(k)
                    for (o, c0, n) in tiles:
                        ins = e.matmul(o, lt, rhs_fn(k, c0, n), start=(k == 0), stop=(k == nk - 1))
                return ins
            return S.add("pe", fn, reads=reads, writes=[("PB", pbuf)])

        ring_ctr = [0]

        def slot_k(s, kcn, cw):
            return ring[:, s, 0:kcn * cw].rearrange("p (k n) -> p k n", n=cw)

        def wsrc(w, kcn, c0, cw):
            return w.rearrange("(k p) n -> p k n", p=128)[:, 0:kcn, c0:c0 + cw]

        def wblock(dmas_fn):
            s = ring_ctr[0] % NSLOT
            ring_ctr[0] += 1
            dma_op("pool", "w%d" % s, dmas_fn(s), writes=[("wslot", s)])
            return s

        def modp(m, c):
            return modT[:, m * 8 + c, 0:1]

        def mods(m, c):
            return modT[:, m * 8 + c, 1:17]

        def pv(name, c):
            i = PV[name] * 8 + c
            return pT[:, i:i + 1]

        def pkeys(parts, name, c):
            return [(name, c, pk) for pk in parts]

        ada_pending = list(range(4, 18))
        DERIVE = {1: ("add", 1.0), 4: ("add", 1.0), 7: ("add", 1.0), 2: ("mul", 0.5 / ALPHA), 5: ("mul", 1.0 / ALPHA), 8: ("mul", 0.5 / ALPHA)}

        def ada_block(blk):
            s = wblock(lambda s, blk=blk: [(slot_k(s, KC, 512), wsrc(w_ada, KC, blk * 512, 512))])
            bank = 6 + (blk % 2)

            def fn(e, s=s, bank=bank):
                ins = None
                for m in range(4):
                    for k in range(KC):
                        ins = e.matmul(ps[:, bank, m * 17:(m + 1) * 17], slot_k(s, KC, 512)[:, k, m * 128:(m + 1) * 128],
                                       cT[:, k, :], start=(k == 0), stop=(k == KC - 1))
                return ins
            S.add("pe", fn, reads=[("wslot", s), ("cT",)], writes=[("bank", bank)])

            def ev(e, blk=blk, bank=bank):
                return e.tensor_tensor(out=modT[:, blk * 4:(blk + 1) * 4, :],
                                       in0=ps[:, bank, 0:68].rearrange("p (m t) -> p m t", t=17),
                                       in1=baT[:, blk * 4:(blk + 1) * 4].unsqueeze(2).to_broadcast([128, 4, 17]),
                                       op=ALU.add)
            S.add("dve", ev, reads=[("bank", bank), ("params",)], writes=[("mod",)])
            if blk % 2 == 1 and (blk // 2) in DERIVE:
                m = blk // 2
                kind, f = DERIVE[m]
                if kind == "add":
                    S.add("dve", lambda e, m=m, f=f: e.tensor_scalar_add(out=modT[:, m * 8:(m + 1) * 8, :], in0=modT[:, m * 8:(m + 1) * 8, :],
                                                                         scalar1=f), writes=[("mod",)])
                else:
                    S.add("dve", lambda e, m=m, f=f: e.tensor_scalar_mul(out=modT[:, m * 8:(m + 1) * 8, :], in0=modT[:, m * 8:(m + 1) * 8, :],
                                                                         scalar1=f), writes=[("mod",)])

        def ada_more(n=1):
            for _ in range(n):
                if ada_pending:
                    ada_block(ada_pending.pop(0))

        def phase_setup():
            cwv = conv_w.rearrange("k (c p) -> k c p", p=128)
            pairs = [
                (ident[:, :], ident_d[:, :]),
                (invc[:, :, :], invc_d[:, :, :]),
                (selc[:, :, :], selc_d[:, :, :]),
                (selp[:, :, :, :], selp_d[:, :, :, :]),
                (scr[0][0:80, 0:128], pvec[:, :]),
                (scr[0][0:72, 128:256], b_ada[:, :]),
                (scr[1][0:17, 0:D], c_all[:, :]),
            ]
            for k in range(31):
                half, r0 = (0, k * 8) if k < 16 else (1, (k - 16) * 8)
                pairs.append((R2[r0:r0 + 8, half * 128:(half + 1) * 128], cwv[k, :, :]))
            for r in range(2):
                pairs.append((cbrow[:, r, :], cb_row[r:r + 1, :].partition_broadcast(16)))
            dma_op("sp", "setup", pairs, writes=SCR(0) + SCR(1) + [("setupR2",), ("const",)])
            S.add("dve", lambda e: e.memset(ones_bf[:, :], 1.0 / D), writes=[("ones",)])
            S.add("dve", lambda e: e.memset(warm_src[:, :], 1.0), writes=[("ones",)])
            S.add("dve", lambda e: e.memset(epsT[:, 0:1], EPS), writes=[("ones",)])
            S.add("dve", lambda e: e.memset(epsT[:, 1:2], EPS_DN), writes=[("ones",)])

            def t_params(e):
                e.transpose(ps[:, 6, 0:80], scr[0][0:80, 0:128], ident[0:80, 0:80])
                e.transpose(ps[:, 6, 80:152], scr[0][0:72, 128:256], ident[0:72, 0:72])
                e.transpose(ps[:, 6, 152:280], R2[0:128, 0:128], ident[:, :])
                ins = e.transpose(ps[:, 6, 280:400], R2[0:120, 128:256], ident[0:120, 0:120])
                for c in range(KC):
                    ins = e.transpose(ps[:, 7, c * 17:(c + 1) * 17], scr[1][0:17, c * 128:(c + 1) * 128], ident[0:17, 0:17])
                return ins
            S.add("pe", t_params, reads=SCR(0) + SCR(1) + [("setupR2",), ("const",)], writes=[("bank", 6), ("bank", 7)])
            S.add("dve", lambda e: e.tensor_copy(out=pT[:, :], in_=ps[:, 6, 0:80]), reads=[("bank", 6)], writes=[("params",)])
            S.add("dve", lambda e: e.tensor_copy(out=baT[:, :], in_=ps[:, 6, 80:152]), reads=[("bank", 6)], writes=[("params",)])
            S.add("dve", lambda e: e.tensor_copy(out=cwT[:, :], in_=ps[:, 6, 152:400]), reads=[("bank", 6)], writes=[("params",)])
            S.add("act", lambda e: e.activation(out=cT[:, :, :], in_=ps[:, 7, 0:KC * 17].rearrange("p (c t) -> p c t", t=17),
                                                func=AF.Silu), reads=[("bank", 7)], writes=[("cT",)])
            for blk in range(4):
                ada_block(blk)
            dma_op("pool", "poolw", [(poolw[:, g, :, :], pool_w[g].rearrange("(i p) j -> p i j", p=128)) for g in range(4)],
                   writes=[("poolw",)])
            dbg_dump("modT", modT[:, :, :].rearrange("p m t -> p (m t)"), [("mod",)])

        def x_load(parts, row0):
            S.fence(["xin", "xs_in"], R1N)
            for pk in parts:
                if pk == "p":
                    for hf in range(2):
                        src = x_p[row0 + hf * 512: row0 + (hf + 1) * 512, :].rearrange("(r p) d -> p r d", p=128)
                        dma_op("sp", "xin%d" % hf, [(xin[:, hf, :, :], src)], writes=[("xin", hf)])
                else:
                    dma_op("sp", "xsin", [(xs_in[0:SWID, :], x_s[:, :])], writes=[("xs_in",)])

        def phase_x(parts, row0):
            S.fence(["bank"], ["PB"])
            bank_ctr = [0]
            for pk in parts:
                if pk == "p":
                    for hf in range(2):
                        for c in range(KC):
                            bank = bank_ctr[0] % 6
                            bank_ctr[0] += 1

                            def tr(e, hf=hf, c=c, bank=bank):
                                ins = None
                                for r in range(4):
                                    ins = e.transpose(ps[:, bank, r * 128:(r + 1) * 128], xin[:, hf, r, c * 128:(c + 1) * 128], ident[:, :])
                                return ins
                            S.add("pe", tr, reads=[("xin", hf), ("const",)], writes=[("bank", bank)])
                            cs = slice(hf * 512, (hf + 1) * 512)
                            S.add("act", lambda e, c=c, bank=bank, cs=cs: e.activation(out=xT[:, c, cs], in_=ps[:, bank, :], func=AF.Copy),
                                  reads=[("bank", bank)], writes=[("xT", c, "p")])
                            S.add("dve", lambda e, c=c, cs=cs: e.tensor_scalar(
                                out=h[:, c, cs], in0=xT[:, c, cs], scalar1=modp(1, c), scalar2=modp(0, c), op0=ALU.mult, op1=ALU.add),
                                reads=[("xT", c, "p"), ("mod",)], writes=[("h", c, "p")])
                else:
                    bank = bank_ctr[0] % 6
                    bank_ctr[0] += 1

                    def tr(e, bank=bank):
                        ins = None
                        for c in range(KC):
                            ins = e.transpose(ps[:, bank, c * SWID:(c + 1) * SWID], xs_in[0:SWID, c * 128:(c + 1) * 128], ident[0:SWID, 0:SWID])
                        return ins
                    S.add("pe", tr, reads=[("xs_in",), ("const",)], writes=[("bank", bank)])
                    pv3 = ps[:, bank, 0:KC * SWID].rearrange("p (c t) -> p c t", t=SWID)
                    S.add("act", lambda e, pv3=pv3: e.activation(out=xT[:, :, PWID:NTC], in_=pv3, func=AF.Copy),
                          reads=[("bank", bank)], writes=[("xT", c, "s") for c in range(KC)])
                    S.add("dve", lambda e: e.tensor_tensor(out=stmp[:, :, :], in0=xT[:, :, PWID:NTC], in1=modT[:, 8:16, 1:17], op=ALU.mult),
                          reads=[("xT", c, "s") for c in range(KC)] + [("mod",)], writes=[("stmp",)])
                    S.add("dve", lambda e: e.tensor_tensor(out=h[:, :, PWID:NTC], in0=stmp[:, :, :], in1=modT[:, 0:8, 1:17], op=ALU.add),
                          reads=[("stmp",), ("mod",)], writes=[("h", c, "s") for c in range(KC)])
            S.fence(["PB"], ["bank"])
            dbg_dump("xT0", xT[:, :, :], [("xT", c, pk) for c in range(KC) for pk in parts])
            dbg_dump("h0", h[:, :, :], [("h", c, pk) for c in range(KC) for pk in parts], is_bf16=True)

        def layer_norm(parts, src_fn, src3_s, src_keys_fn, eps, gname, outs_p, outs_s, next_af=None, pre=False, apply_af=None):
            has_p, has_s = "p" in parts, "s" in parts
            gi0 = PV[gname] * 8
            MB = [("bank", 6), ("bank", 7)]
            if has_s:
                skeys = [k for c in range(KC) for k in src_keys_fn(c, "s")]
                S.add("act", lambda e: e.activation(out=lns[:, 0, :, :], in_=src3_s, func=AF.Copy), reads=skeys, writes=[("lns", 0)])
                S.add("dve", lambda e: e.tensor_tensor(out=lns[:, 1, :, :], in0=src3_s, in1=src3_s, op=ALU.mult), reads=skeys, writes=[("lns", 1)])

                def st_s(e):
                    ins = None
                    for c in range(KC):
                        ins = e.matmul(ps[:, 2, 0:SWID], ones_bf[:, :], lns[:, 0, c, :], start=(c == 0), stop=(c == KC - 1))
                    for c in range(KC):
                        ins = e.matmul(ps[:, 2, SWID:2 * SWID], ones_bf[:, :], lns[:, 1, c, :], start=(c == 0), stop=(c == KC - 1))
                    return ins
                if not has_p:
                    S.add("pe", st_s, reads=[("lns", 0), ("lns", 1), ("ones",)], writes=[("PB", 0)])
            if has_p:
                for c in range(KC):
                    i = c % 2
                    if pre:
                        vbf, vsq = h[:, c, :], vsqf[:, c, :]
                        rk = [("h", c, "p"), ("vsq", c)]
                    else:
                        vbf = scr[i][:, 0:520].bitcast(BF16)
                        vsq = scr[i][:, 520:1040].bitcast(BF16)
                        rk = SCR(i)
                        S.add("act", lambda e, c=c, vbf=vbf: e.activation(out=vbf[:, 0:PWID], in_=src_fn(c)[:, 0:PWID], func=AF.Copy),
                              reads=src_keys_fn(c, "p"), writes=[("scr", i, "a")])
                        S.add("dve", lambda e, c=c, vsq=vsq: e.tensor_tensor(out=vsq[:, 0:PWID], in0=src_fn(c)[:, 0:PWID], in1=src_fn(c)[:, 0:PWID],
                                                                            op=ALU.mult),
                              reads=src_keys_fn(c, "p"), writes=[("scr", i, "b")])

                    def st(e, c=c, vbf=vbf, vsq=vsq):
                        ins = None
                        for (bo, c0) in ((0, 0), (1, 512)):
                            e.matmul(ps[:, 6 + bo, :], ones_bf[:, :], vbf[:, c0:c0 + 512], start=(c == 0), stop=(c == KC - 1))
                            ins = e.matmul(ps[:, bo, :], ones_bf[:, :], vsq[:, c0:c0 + 512], start=(c == 0), stop=(c == KC - 1))
                        return ins
                    S.add("pe", st, reads=rk + [("ones",)], writes=MB + [("PB", 0)])
                if has_s:
                    S.add("pe", st_s, reads=[("lns", 0), ("lns", 1), ("ones",)], writes=[("PB", 0)])
                S.add("act", lambda e: e.activation(out=jnk[:, 0:1], in_=epsT[:, 0:1], func=AF.Ln), reads=[("ones",)], writes=[("jnk",)])
            for pk in parts:
                if pk == "p":
                    m_ap, r_ap, t_ap, rs_ap = ps[:, 6:8, :], ps[:, 0:2, :], P3(scr[0][:, 0:PWID]), P3(rstd_sb[:, 0:PWID])
                    mk = MB
                else:
                    m_ap, r_ap, t_ap, rs_ap = ps[:, 2, 0:SWID], ps[:, 2, SWID:2 * SWID], sscr[:, 0, :], rstd_sb[:, PWID:NTC]
                    mk = [("PB", 0)]
                S.add("act", lambda e, m_ap=m_ap, t_ap=t_ap: e.activation(out=t_ap, in_=m_ap, func=AF.Square),
                      reads=mk, writes=SK(0, pk))
                S.add("dve", lambda e, r_ap=r_ap, t_ap=t_ap, rs_ap=rs_ap: e.tensor_tensor(out=rs_ap, in0=r_ap, in1=t_ap, op=ALU.subtract),
                      reads=SK(0, pk) + [("PB", 0)], writes=[("rstd", pk)])
            for pk in parts:
                rs_ap = P3(rstd_sb[:, 0:PWID]) if pk == "p" else rstd_sb[:, PWID:NTC]
                S.add("act", lambda e, rs_ap=rs_ap: e.activation(out=rs_ap, in_=rs_ap, func=AF.Ln,
                                                                 bias=epsT[:, (0 if eps == EPS else 1):(1 if eps == EPS else 2)], scale=1.0),
                      reads=[("ones",)], writes=[("rstd", pk)])
                S.add("act", lambda e, rs_ap=rs_ap: e.activation(out=rs_ap, in_=rs_ap, func=AF.Exp, scale=-0.5),
                      writes=[("rstd", pk)])
            if apply_af is not None:
                S.add("act", lambda e: e.activation(out=jnk[:, 1:2], in_=epsT[:, 0:1], func=apply_af), reads=[("ones",)], writes=[("jnk",)])
            if has_s:
                t3 = stmp[:, :, :]
                S.add("dve", lambda e: e.tensor_tensor(out=t3, in0=src3_s, in1=ps[:, 2, 0:SWID].unsqueeze(1).to_broadcast([128, KC, SWID]),
                                                       op=ALU.subtract),
                      reads=[k for c in range(KC) for k in src_keys_fn(c, "s")] + [("PB", 0)], writes=[("stmp",)])
                S.add("dve", lambda e: e.tensor_tensor(out=t3, in0=t3, in1=rstd_sb[:, PWID:NTC].unsqueeze(1).to_broadcast([128, KC, SWID]),
                                                       op=ALU.mult),
                      reads=[("rstd", "s")], writes=[("stmp",)])
                S.add("dve", lambda e: e.tensor_tensor(out=t3, in0=t3, in1=pT[:, gi0:gi0 + 8].unsqueeze(2).to_broadcast([128, KC, SWID]),
                                                       op=ALU.mult),
                      reads=[("params",)], writes=[("stmp",)])
                outs_s(t3, [("stmp",)])
            if has_p:
                for c in range(KC):
                    i = c % 2
                    cs = slice(0, PWID)
                    m_ap, r_ap = ps[:, 6:8, :], P3(rstd_sb[:, 0:PWID])
                    v_ap, t_ap = P3(src_fn(c)[:, cs]), P3(scr[i][:, cs])
                    S.add("dve", lambda e, v_ap=v_ap, t_ap=t_ap, m_ap=m_ap: e.tensor_tensor(out=t_ap, in0=v_ap, in1=m_ap, op=ALU.subtract),
                          reads=src_keys_fn(c, "p") + MB, writes=SCR(i))
                    bop = S.add("dve", lambda e, t_ap=t_ap, r_ap=r_ap, c=c: e.scalar_tensor_tensor(
                        out=t_ap, in0=t_ap, scalar=pv(gname, c), in1=r_ap, op0=ALU.mult, op1=ALU.mult),
                        reads=[("rstd", "p"), ("params",)], writes=SCR(i))
                    if c == WARM_C:
                        def warm(e):
                            ins = None
                            for _ in range(WARM_N):
                                ins = e.matmul(ps[:, 5, :], ones_bf[:, :], warm_src[:, :], start=True, stop=True)
                            return ins
                        S.add("pe", warm, reads=[("ones",)], writes=[("PB", 1)], extra=[bop])
                    outs_p(c, scr[i][:, cs], SCR(i))
            if next_af is not None:
                S.add("act", lambda e: e.activation(out=jnk[:, 1:2], in_=epsT[:, 0:1], func=next_af), reads=[("ones",)], writes=[("jnk",)])

        def ln_outs_resid(bname, mod_base, with_h):
            bi0 = PV[bname] * 8

            def outs_p(c, tB, tkeys):
                cs = slice(0, PWID)
                xk = [("xT", c, "p")]
                S.add("act", lambda e, c=c, tB=tB, cs=cs: e.activation(out=xT[:, c, cs], in_=tB, func=AF.Identity, bias=pv(bname, c), scale=1.0),
                      reads=tkeys + [("params",)], writes=xk)
                if with_h:
                    S.add("act", lambda e, c=c, cs=cs: e.activation(out=h[:, c, cs], in_=xT[:, c, cs], func=AF.Identity,
                                                                    bias=modp(mod_base, c), scale=modp(mod_base + 1, c)),
                          reads=xk + [("mod",)], writes=[("h", c, "p")])

            def outs_s(t3, tkeys):
                xk = [("xT", c, "s") for c in range(KC)]
                S.add("dve", lambda e: e.tensor_tensor(out=xT[:, :, PWID:NTC], in0=t3,
                                                       in1=pT[:, bi0:bi0 + 8].unsqueeze(2).to_broadcast([128, KC, SWID]), op=ALU.add),
                      reads=tkeys + [("params",)], writes=xk)
                if with_h:
                    mb = mod_base
                    S.add("dve", lambda e: e.tensor_tensor(out=t3, in0=xT[:, :, PWID:NTC], in1=modT[:, (mb + 1) * 8:(mb + 2) * 8, 1:17], op=ALU.mult),
                          reads=xk + [("mod",)], writes=[("stmp",)])
                    S.add("dve", lambda e: e.tensor_tensor(out=h[:, :, PWID:NTC], in0=t3, in1=modT[:, mb * 8:(mb + 1) * 8, 1:17], op=ALU.add),
                          reads=[("stmp",), ("mod",)], writes=[("h", c, "s") for c in range(KC)])
            return outs_p, outs_s

        def resid_add(parts, b, oc, gate_mod):
            for pk in parts:
                if pk == "p":
                    S.add("dve", lambda e, b=b, oc=oc: e.scalar_tensor_tensor(
                        out=P3(xT[:, oc, 0:PWID]), in0=pb_p(b), scalar=modp(gate_mod, oc), in1=P3(xT[:, oc, 0:PWID]),
                        op0=ALU.mult, op1=ALU.add),
                        reads=[("PB", b), ("mod",)], writes=[("xT", oc, "p")])
                    S.add("act", lambda e, oc=oc: e.activation(out=h[:, oc, 0:PWID], in_=xT[:, oc, 0:PWID], func=AF.Copy),
                          reads=[("xT", oc, "p")], writes=[("h", oc, "p")])
                    S.add("dve", lambda e, oc=oc: e.tensor_tensor(out=vsqf[:, oc, 0:PWID], in0=xT[:, oc, 0:PWID], in1=xT[:, oc, 0:PWID], op=ALU.mult),
                          reads=[("xT", oc, "p")], writes=[("vsq", oc)])
                else:
                    S.add("dve", lambda e, b=b, oc=oc: e.tensor_tensor(out=stmp[:, 1, :], in0=pb_s(b), in1=mods(gate_mod, oc), op=ALU.mult),
                          reads=[("PB", b), ("mod",)], writes=[("stmp",)])
                    S.add("dve", lambda e, oc=oc: e.tensor_tensor(out=xT[:, oc, PWID:NTC], in0=stmp[:, 1, :], in1=xT[:, oc, PWID:NTC], op=ALU.add),
                          reads=[("stmp",)], writes=[("xT", oc, "s")])

        def phase_ffn(parts, f, gate_mod, gname, bname, next_mod, with_h, tag):
            win, wout = ffn_w_in[f], ffn_w_out[f]
            S.fence(["hid"], R1N)
            hkeys = [("h", c, pk) for c in range(KC) for pk in parts]
            for blk in range(FC // 2):
                def dm(s, blk=blk):
                    v = slot_k(s, KC, 512)
                    return [(v[:, :, 0:256], wsrc(win, KC, blk * 256, 256)),
                            (v[:, :, 256:512], wsrc(win, KC, DFF + blk * 256, 256))]
                s = wblock(dm)
                for jj in range(2):
                    j = 2 * blk + jj
                    sv = slot_k(s, KC, 512)
                    mm_group(0, parts, lambda k, sv=sv, jj=jj: sv[:, k, jj * 128:(jj + 1) * 128],
                             lambda k, c0, n: h[:, k, c0:c0 + n], KC, reads=[("wslot", s)] + hkeys)
                    mm_group(1, parts, lambda k, sv=sv, jj=jj: sv[:, k, 256 + jj * 128:256 + (jj + 1) * 128],
                             lambda k, c0, n: h[:, k, c0:c0 + n], KC, reads=[("wslot", s)] + hkeys)
                    i = j % 2
                    for pk in parts:
                        if pk == "p":
                            g_ap, u_ap, s_ap, o_ap = pb_p(0), pb_p(1), P3(scr[i][:, 0:PWID]), P3(hid[:, j, 0:PWID])
                        else:
                            g_ap, u_ap, s_ap, o_ap = pb_s(0), pb_s(1), sscr[:, i, :], hid[:, j, PWID:NTC]
                        S.add("act", lambda e, g_ap=g_ap, s_ap=s_ap: e.activation(out=s_ap, in_=g_ap, func=AF.Silu),
                              reads=[("PB", 0)], writes=SK(i, pk))
                        S.add("dve", lambda e, u_ap=u_ap, s_ap=s_ap, o_ap=o_ap: e.tensor_tensor(out=o_ap, in0=u_ap, in1=s_ap, op=ALU.mult),
                              reads=[("PB", 1)] + SK(i, pk), writes=[("hid", j, pk)])
                if len(ada_pending) > 8:
                    ada_more(1)
            dbg_dump("hid" + tag, hid[:, 0:8, :], [("hid", j, pk) for j in range(FC) for pk in parts], is_bf16=True)
            hidkeys = [("hid", j, pk) for j in range(FC) for pk in parts]
            S.fence(["vsq"], R2N)
            for oc in range(KC):
                s = wblock(lambda s, oc=oc: [(slot_k(s, FC, 128), wsrc(wout, FC, oc * 128, 128))])
                sv = slot_k(s, FC, 128)
                b = oc % 2
                mm_group(b, parts, lambda k, sv=sv: sv[:, k, :], lambda k, c0, n: hid[:, k, c0:c0 + n], FC,
                         reads=[("wslot", s)] + hidkeys)
                resid_add(parts, b, oc, gate_mod)
            dbg_dump("v" + tag, xT[:, :, :], [("xT", c, pk) for c in range(KC) for pk in parts])
            layer_norm(parts, lambda c: xT[:, c, :], xT[:, :, PWID:NTC], lambda c, pk: [("xT", c, pk)], EPS_DN, gname,
                       *ln_outs_resid(bname, next_mod, with_h), next_af=(AF.Sigmoid if tag == "1" else None), pre=("p" in parts))
            dbg_dump("x" + tag, xT[:, :, :], [("xT", c, pk) for c in range(KC) for pk in parts])

        def phase_mixer(parts, first_prompt):
            hkeys = [("h", c, pk) for c in range(KC) for pk in parts]
            has_p = "p" in parts
            has_s = "s" in parts
            S.fence(["glu", "gluhalo"], R1N)
            S.fence(["sstage", "scprod"], R2N)
            if has_s:
                scv = sconv.rearrange("(q s) k d -> q (s k) d", s=4)
                spv = spool.rearrange("(q s) k d -> q (s k) d", s=8)
                pairs = [(sc_st[0:120, q, :], scv[q]) for q in range(4)]
                pairs += [(wrep[s4 * 30:(s4 + 1) * 30, :], conv_w[0:30, :]) for s4 in range(4)]
                pairs += [(sp_st[0:120, q, :], spv[q]) for q in range(2)]
                dma_op("sp", "sstate", pairs, writes=[("sstage",)])
                dma_op("sp", "sshift", [(ncs[:, 0:29, :], sconv[:, 1:30, :]), (nps[:, 0:14, :], spool[:, 1:15, :])], out=True)
                for q in range(4):
                    S.add("dve", lambda e, q=q: e.tensor_tensor(out=sc_st[0:120, q, :], in0=sc_st[0:120, q, :], in1=wrep[0:120, :], op=ALU.mult),
                          reads=[("sstage",)], writes=[("scprod", q)])

                def selmm(e):
                    ins = None
                    for hh in range(2):
                        for q in range(4):
                            ins = e.matmul(ps[0:16, 6 + hh, :], selc[0:120, q, :], sc_st[0:120, q, hh * 512:(hh + 1) * 512],
                                           start=(q == 0), stop=(q == 3))
                    return ins
                S.add("pe", selmm, reads=[("scprod", q) for q in range(4)] + [("const",)], writes=[("bank", 6), ("bank", 7)])
                S.add("dve", lambda e: e.tensor_tensor(out=P3(tmA[0:16, :]), in0=ps[0:16, 6:8, :], in1=P3(cbrow[:, 1, :]), op=ALU.add),
                      reads=[("bank", 6), ("bank", 7), ("const",)], writes=[("tmA",)])

                def selpm(e):
                    ins = None
                    for g in range(4):
                        for q in range(2):
                            ins = e.matmul(ps[0:16, 6 + g // 2, (g % 2) * 256:(g % 2) * 256 + 256], selp[0:120, q, g, :],
                                           sp_st[0:120, q, g * 256:(g + 1) * 256], start=(q == 0), stop=(q == 1))
                    return ins
                S.add("pe", selpm, reads=[("sstage",), ("const",)], writes=[("bank", 6), ("bank", 7)])
                S.add("act", lambda e: e.activation(out=P3(mprev[0:16, :]), in_=ps[0:16, 6:8, :], func=AF.Copy),
                      reads=[("bank", 6), ("bank", 7)], writes=[("mprev",)])
            if has_p:
                if first_prompt:
                    S.add("dve", lambda e: e.memset(glub[:, :, 0:30], 0.0), writes=[("gluhalo",)])
                else:
                    S.add("dve", lambda e: e.tensor_copy(out=glub[:, :, 0:30], in_=gl_halo[:, :, :]), reads=[("gl_halo",)],
                          writes=[("gluhalo",)])
            slots = {}
            for c in range(KC):
                blk, cc = c // 4, c % 4
                if cc == 0:
                    slots["bg"] = wblock(lambda s, blk=blk: [(slot_k(s, KC, 512), wsrc(w_in, KC, D + blk * 512, 512))])
                    slots["a"] = wblock(lambda s, blk=blk: [(slot_k(s, KC, 512), wsrc(w_in, KC, blk * 512, 512))])
                sa, sg = slot_k(slots["a"], KC, 512), slot_k(slots["bg"], KC, 512)
                mm_group(1, parts, lambda k, sg=sg, cc=cc: sg[:, k, cc * 128:(cc + 1) * 128], lambda k, c0, n: h[:, k, c0:c0 + n], KC,
                         reads=[("wslot", slots["bg"])] + hkeys)
                mm_group(0, parts, lambda k, sa=sa, cc=cc: sa[:, k, cc * 128:(cc + 1) * 128], lambda k, c0, n: h[:, k, c0:c0 + n], KC,
                         reads=[("wslot", slots["a"])] + hkeys)
                i = c % 2
                for pk in parts:
                    if pk == "p":
                        a_ap, g_ap, s_ap, o_ap = pb_p(0), pb_p(1), P3(scr[i][:, 0:PWID]), P3(glub[:, c, 30:30 + PWID])
                    else:
                        a_ap, g_ap, s_ap, o_ap = pb_s(0), pb_s(1), sscr[:, i, :], glus[:, c, :]
                    S.add("act", lambda e, g_ap=g_ap, s_ap=s_ap: e.activation(out=s_ap, in_=g_ap, func=AF.Sigmoid),
                          reads=[("PB", 1)], writes=SK(i, pk))
                    S.add("dve", lambda e, a_ap=a_ap, s_ap=s_ap, o_ap=o_ap: e.tensor_tensor(out=o_ap, in0=a_ap, in1=s_ap, op=ALU.mult),
                          reads=[("PB", 0)] + SK(i, pk), writes=[("glu", c, pk)])
                    if pk == "p":
                        S.add("dve", lambda e, c=c, i=i: e.tensor_tensor(out=gl_halo[:, c, :], in0=ps[:, 1, 482:512],
                                                                        in1=scr[i][:, PWID - 30:PWID], op=ALU.mult),
                              reads=[("PB", 0), ("gluhalo",)] + SCR(i), writes=[("gl_halo",)])
            S.fence(["conv"], R2N)
            if has_p:
                PT = 24
                dctr = [0]
                for c in range(KC):
                    b = c % 2
                    rk = [("glu", c, "p"), ("gluhalo",), ("params",)]
                    for k in range(PT, 31):
                        wk = cwT[:, k * 8 + c:k * 8 + c + 1]
                        if k == PT:
                            S.add("dve", lambda e, c=c, k=k, wk=wk: e.tensor_scalar(
                                out=conv[:, c, 0:PWID], in0=glub[:, c, k:k + PWID], scalar1=wk, scalar2=pv("conv_b", c),
                                op0=ALU.mult, op1=ALU.add), reads=rk, writes=[("conv", c, "p")])
                        else:
                            S.add("dve", lambda e, c=c, k=k, wk=wk: e.scalar_tensor_tensor(
                                out=conv[:, c, 0:PWID], in0=glub[:, c, k:k + PWID], scalar=wk, in1=conv[:, c, 0:PWID],
                                op0=ALU.mult, op1=ALU.add), reads=rk, writes=[("conv", c, "p")])
                    for k in range(PT):
                        di = dctr[0] % 8
                        dctr[0] += 1
                        wk = cwT[:, k * 8 + c:k * 8 + c + 1]
                        S.add("act", lambda e, di=di, wk=wk: e.activation(out=diag[:, di, :], in_=ident[:, :], func=AF.Copy, scale=wk),
                              reads=[("params",), ("const",)], writes=[("diag", di)])

                        def cmm(e, c=c, k=k, di=di, b=b):
                            e.matmul(ps[:, 3 * b, :], diag[:, di, :], glub[:, c, k:k + 512], start=(k == 0), stop=(k == PT - 1))
                            return e.matmul(ps[:, 3 * b + 1, :], diag[:, di, :], glub[:, c, k + 512:k + 1024], start=(k == 0), stop=(k == PT - 1))
                        S.add("pe", cmm, reads=[("diag", di), ("glu", c, "p"), ("gluhalo",)], writes=[("PB", b)])
                    S.add("dve", lambda e, c=c, b=b: e.tensor_tensor(out=P3(conv[:, c, 0:PWID]), in0=pb_p(b), in1=P3(conv[:, c, 0:PWID]), op=ALU.add),
                          reads=[("PB", b)], writes=[("conv", c, "p")])
                    ada_more(1)
            if has_s:
                def trg(e):
                    ins = None
                    for c in range(KC):
                        ins = e.matmul(ps[0:16, 6 + c // 4, (c % 4) * 128:(c % 4 + 1) * 128], glus[:, c, :], ident[:, :], start=True, stop=True)
                    return ins
                S.add("pe", trg, reads=[("glu", c, "s") for c in range(KC)] + [("const",)], writes=[("bank", 6), ("bank", 7)])
                S.add("act", lambda e: e.activation(out=P3(scr[1][0:16, 0:D]), in_=ps[0:16, 6:8, :], func=AF.Copy),
                      reads=[("bank", 6), ("bank", 7)], writes=SCR(1))
                dma_op("sp", "o_ncs", [(ncs[:, 29, :], scr[1][0:16, 0:D])], reads=SCR(1), out=True)
                S.add("dve", lambda e: e.tensor_tensor(out=scr[0][0:16, 0:D], in0=scr[1][0:16, 0:D], in1=cbrow[:, 0, :], op=ALU.mult),
                      reads=SCR(1) + [("const",)], writes=SCR(0))
                S.add("dve", lambda e: e.tensor_tensor(out=scr[0][0:16, 0:D], in0=scr[0][0:16, 0:D], in1=tmA[0:16, :], op=ALU.add),
                      reads=[("tmA",)], writes=SCR(0))

                def trb(e):
                    ins = None
                    for c in range(KC):
                        ins = e.transpose(ps[:, 6, c * SWID:(c + 1) * SWID], scr[0][0:16, c * 128:(c + 1) * 128], ident[0:16, 0:16])
                    return ins
                S.add("pe", trb, reads=SCR(0) + [("const",)], writes=[("bank", 6)])
                S.add("act", lambda e: e.activation(out=conv[:, :, PWID:NTC], in_=ps[:, 6, 0:KC * SWID].rearrange("p (c t) -> p c t", t=SWID),
                                                    func=AF.Copy),
                      reads=[("bank", 6)], writes=[("conv", c, "s") for c in range(KC)])
            dbg_dump("conv", conv[:, :, :], [("conv", c, pk) for c in range(KC) for pk in parts])
            S.fence(["siluln"], R1N)

            def cl_outs_p(c, tB, tkeys):
                S.add("act", lambda e, c=c, tB=tB: e.activation(out=siluln[:, c, 0:PWID], in_=tB, func=AF.Silu, bias=pv("cln_b", c), scale=1.0),
                      reads=tkeys + [("params",)], writes=[("siluln", c, "p")])

            def cl_outs_s(t3, tkeys):
                bi0 = PV["cln_b"] * 8
                S.add("dve", lambda e: e.tensor_tensor(out=t3, in0=t3, in1=pT[:, bi0:bi0 + 8].unsqueeze(2).to_broadcast([128, KC, SWID]), op=ALU.add),
                      reads=tkeys + [("params",)], writes=[("stmp",)])
                S.add("act", lambda e: e.activation(out=siluln[:, :, PWID:NTC], in_=t3, func=AF.Silu),
                      reads=[("stmp",)], writes=[("siluln", c, "s") for c in range(KC)])
            layer_norm(parts, lambda c: conv[:, c, :], conv[:, :, PWID:NTC], lambda c, pk: [("conv", c, pk)], EPS, "cln_g", cl_outs_p, cl_outs_s, next_af=AF.Sigmoid, apply_af=AF.Silu)
            dbg_dump("siluln", siluln[:, :, :], [("siluln", c, pk) for c in range(KC) for pk in parts], is_bf16=True)
            S.fence(["upb", "tmb"], R2N)
            S.fence(["mixs", "pooled"], ["hid", "xin", "xs_in", "yst", "glu", "gluhalo", "mixs", "pooled"])
            def emit_m3(g):
                for jh in range(2):
                    oc = 2 * g + jh
                    bb = jh
                    pk_g = [("pooled", 2 * (g % 3) + t, pk) for t in range(2) for pk in parts]
                    mm_group(bb, parts, lambda k, g=g, jh=jh: poolw[:, g, k, jh * 128:(jh + 1) * 128],
                             lambda k, c0, n, g=g: pooled[:, 2 * (g % 3) + k, c0:c0 + n], 2, reads=pk_g + [("poolw",)])
                    for pk in parts:
                        if pk == "p":
                            i_ap, o_ap = pb_p(bb), P3(mixs[:, oc, 0:PWID])
                        else:
                            i_ap, o_ap = pb_s(bb), mixs[:, oc, PWID:NTC]
                        S.add("act", lambda e, i_ap=i_ap, o_ap=o_ap, oc=oc: e.activation(out=o_ap, in_=i_ap, func=AF.Copy,
                                                                                     scale=pv("pscale", oc)),
                              reads=[("PB", bb), ("params",)], writes=[("mixs", oc, pk)])

            def s_stage1(c):
                w = POOL_W[c // 2]
                ut = tmbuf[0:16, 2 * (c % 2), 0:128]
                pt = tmbuf[0:16, 2 * (c % 2) + 1, 0:128]
                S.add("pe", lambda e, c=c: e.matmul(ps[0:16, 6, 0:128], glus[:, c, :], ident[:, :], start=True, stop=True),
                      reads=[("usfm", c), ("const",)], writes=[("bank", 6)])
                S.add("act", lambda e, ut=ut: e.activation(out=ut, in_=ps[0:16, 6, 0:128], func=AF.Copy), reads=[("bank", 6)],
                      writes=[("tmb", 2 * (c % 2))])
                dma_op("sp", "o_nps%d" % (c % 2), [(nps[:, 14, c * 128:(c + 1) * 128], ut)], reads=[("tmb", 2 * (c % 2))], out=True)
                S.add("dve", lambda e, ut=ut, pt=pt, c=c, w=w: e.scalar_tensor_tensor(
                    out=pt, in0=ut, scalar=(1.0 / w - 1.0), in1=mprev[0:16, c * 128:(c + 1) * 128], op0=ALU.mult, op1=ALU.add),
                    reads=[("tmb", 2 * (c % 2)), ("mprev",)], writes=[("tmb", 2 * (c % 2) + 1)])

            def s_stage2(c):
                g = c // 2
                pidx = 2 * (g % 3) + (c % 2)
                pt = tmbuf[0:16, 2 * (c % 2) + 1, 0:128]
                S.add("pe", lambda e, pt=pt: e.transpose(ps[:, 7, 0:SWID], pt, ident[0:16, 0:16]), reads=[("tmb", 2 * (c % 2) + 1), ("const",)],
                      writes=[("bank", 7)])
                S.add("act", lambda e, pidx=pidx: e.activation(out=pooled[:, pidx, PWID:NTC], in_=ps[:, 7, 0:SWID], func=AF.Copy),
                      reads=[("bank", 7)], writes=[("pooled", pidx, "s")])

            uslot = None
            for c in range(KC):
                g = c // 2
                w = POOL_W[g]
                if c % 4 == 0:
                    uslot = wblock(lambda s, c=c: [(slot_k(s, KC, 512), wsrc(w_in, KC, 2 * D + (c // 4) * 512, 512))])
                su = slot_k(uslot, KC, 512)
                b = c % 2
                mm_group(b, parts, lambda k, su=su, c=c: su[:, k, (c % 4) * 128:(c % 4 + 1) * 128], lambda k, c0, n: h[:, k, c0:c0 + n], KC,
                         reads=[("wslot", uslot)] + hkeys)
                if has_s:
                    if c >= 3:
                        s_stage2(c - 3)
                    if c >= 1:
                        s_stage1(c - 1)
                U, Pb, Qb = upb[:, 3 * b + 0, :], upb[:, 3 * b + 1, :], upb[:, 3 * b + 2, :]
                uk = [("upb", 3 * b + t) for t in range(3)]
                pidx = 2 * (g % 3) + (c % 2)
                pl = pooled[:, pidx, :]
                if has_p:
                    E = 15 + PWID
                    if first_prompt:
                        S.add("dve", lambda e, U=U: e.memset(U[:, 0:15], 0.0), writes=[uk[0]])
                    else:
                        S.add("dve", lambda e, U=U, c=c: e.tensor_copy(out=U[:, 0:15], in_=up_halo[:, c, :]), reads=[("up_halo", c)],
                              writes=[uk[0]])
                    S.add("act", lambda e, U=U, b=b: e.activation(out=P3(U[:, 15:E]), in_=pb_p(b), func=AF.Copy),
                          reads=[("PB", b)], writes=[uk[0]])
                    S.add("act", lambda e, U=U, c=c: e.activation(out=up_halo[:, c, :], in_=U[:, PWID:PWID + 15], func=AF.Copy), reads=[uk[0]],
                          writes=[("up_halo", c)])
                    S.add("dve", lambda e, U=U, Pb=Pb: e.tensor_tensor(out=Pb[:, 1:E], in0=U[:, 1:E], in1=U[:, 0:E - 1], op=ALU.add),
                          reads=[uk[0]], writes=[uk[1]])
                    Sb, skey = Pb, uk[1]
                    if w >= 4:
                        S.add("dve", lambda e, Pb=Pb, Qb=Qb: e.tensor_tensor(out=Qb[:, 3:E], in0=Pb[:, 3:E], in1=Pb[:, 1:E - 2], op=ALU.add),
                              reads=[uk[1]], writes=[uk[2]])
                        Sb, skey = Qb, uk[2]
                    if w >= 8:
                        S.add("dve", lambda e, Pb=Pb, Qb=Qb: e.tensor_tensor(out=Pb[:, 7:E], in0=Qb[:, 7:E], in1=Qb[:, 3:E - 4], op=ALU.add),
                              reads=[uk[2]], writes=[uk[1]])
                        Sb, skey = Pb, uk[1]
                    if w >= 16:
                        S.add("dve", lambda e, Pb=Pb, Qb=Qb: e.tensor_tensor(out=Qb[:, 15:E], in0=Pb[:, 15:E], in1=Pb[:, 7:E - 8], op=ALU.add),
                              reads=[uk[1]], writes=[uk[2]])
                        Sb, skey = Qb, uk[2]
                    S.add("dve", lambda e, Sb=Sb, U=U, pl=pl, w=w: e.scalar_tensor_tensor(
                        out=pl[:, 0:PWID], in0=Sb[:, 15:E], scalar=1.0 / w, in1=U[:, 15:E], op0=ALU.mult, op1=ALU.subtract),
                        reads=[skey, uk[0]], writes=[("pooled", pidx, "p")])
                    if first_prompt:
                        S.add("dve", lambda e, Sb=Sb, g=g: e.tensor_tensor(out=stmp[:, 2, :], in0=Sb[:, 15:31], in1=invc[:, g, :], op=ALU.mult),
                              reads=[skey, ("const",)], writes=[("stmp",)])
                        S.add("dve", lambda e, U=U, pl=pl: e.tensor_tensor(out=pl[:, 0:16], in0=stmp[:, 2, :], in1=U[:, 15:31], op=ALU.subtract),
                              reads=[("stmp",), uk[0]], writes=[("pooled", pidx, "p")])
                if has_s:
                    S.add("act", lambda e, b=b, c=c: e.activation(out=glus[:, c, :], in_=pb_s(b), func=AF.Copy), reads=[("PB", b)],
                          writes=[("usfm", c)])
                if c % 2 == 1 and g >= 2:
                    emit_m3(g - 2)
            if has_s:
                s_stage2(KC - 3)
                s_stage1(KC - 1)
                s_stage2(KC - 2)
                s_stage2(KC - 1)
            emit_m3(2)
            emit_m3(3)
            dbg_dump("mixs", mixs[:, :, :], [("mixs", c, pk) for c in range(KC) for pk in parts], is_bf16=True)
            S.fence(["merged", "m1h"], R2N)
            slk = [("siluln", c, pk) for c in range(KC) for pk in parts]
            mxk = [("mixs", c, pk) for c in range(KC) for pk in parts]
            for half in range(2):
                s_ga = wblock(lambda s, half=half: [(slot_k(s, KC, 512), wsrc(w_in, KC, 3 * D + half * 512, 512))])
                s_co = wblock(lambda s, half=half: [(slot_k(s, KC, 512), wsrc(w_conv_out, KC, half * 512, 512))])
                for cc in range(4):
                    oc = half * 4 + cc
                    sv_g, sv_c = slot_k(s_ga, KC, 512), slot_k(s_co, KC, 512)
                    mm_group(0, parts, lambda k, sv_g=sv_g, cc=cc: sv_g[:, k, cc * 128:(cc + 1) * 128], lambda k, c0, n: h[:, k, c0:c0 + n], KC,
                             reads=[("wslot", s_ga)] + hkeys)
                    mm_group(1, parts, lambda k, sv_c=sv_c, cc=cc: sv_c[:, k, cc * 128:(cc + 1) * 128],
                             lambda k, c0, n: siluln[:, k, c0:c0 + n], KC, reads=[("wslot", s_co)] + slk)
                    i = oc % 2
                    for pk in parts:
                        if pk == "p":
                            g_ap, y_ap, s_ap, o_ap = pb_p(0), pb_p(1), P3(scr[i][:, 0:PWID]), P3(m1h[:, cc, 0:PWID])
                        else:
                            g_ap, y_ap, s_ap, o_ap = pb_s(0), pb_s(1), sscr[:, i, :], m1h[:, cc, PWID:NTC]
                        S.add("act", lambda e, g_ap=g_ap, s_ap=s_ap: e.activation(out=s_ap, in_=g_ap, func=AF.Sigmoid),
                              reads=[("PB", 0)], writes=SK(i, pk))
                        S.add("dve", lambda e, y_ap=y_ap, s_ap=s_ap, o_ap=o_ap: e.tensor_tensor(out=o_ap, in0=y_ap, in1=s_ap, op=ALU.mult),
                              reads=[("PB", 1)] + SK(i, pk), writes=[("m1h", cc, pk)])
                s_gb = wblock(lambda s, half=half: [(slot_k(s, KC, 512), wsrc(w_in, KC, 4 * D + half * 512, 512))])
                s_po = wblock(lambda s, half=half: [(slot_k(s, KC, 512), wsrc(w_pool_out, KC, half * 512, 512))])
                for cc in range(4):
                    oc = half * 4 + cc
                    sv_g, sv_c = slot_k(s_gb, KC, 512), slot_k(s_po, KC, 512)
                    mm_group(0, parts, lambda k, sv_g=sv_g, cc=cc: sv_g[:, k, cc * 128:(cc + 1) * 128], lambda k, c0, n: h[:, k, c0:c0 + n], KC,
                             reads=[("wslot", s_gb)] + hkeys)
                    mm_group(1, parts, lambda k, sv_c=sv_c, cc=cc: sv_c[:, k, cc * 128:(cc + 1) * 128],
                             lambda k, c0, n: mixs[:, k, c0:c0 + n], KC, reads=[("wslot", s_po)] + mxk)
                    i = oc % 2
                    for pk in parts:
                        if pk == "p":
                            g_ap, y_ap, s_ap, m_ap, o_ap = pb_p(0), pb_p(1), P3(scr[i][:, 0:PWID]), P3(m1h[:, cc, 0:PWID]), P3(merged[:, oc, 0:PWID])
                        else:
                            g_ap, y_ap, s_ap, m_ap, o_ap = pb_s(0), pb_s(1), sscr[:, i, :], m1h[:, cc, PWID:NTC], merged[:, oc, PWID:NTC]
                        S.add("act", lambda e, g_ap=g_ap, s_ap=s_ap: e.activation(out=s_ap, in_=g_ap, func=AF.Sigmoid),
                              reads=[("PB", 0)], writes=SK(i, pk))
                        S.add("dve", lambda e, y_ap=y_ap, s_ap=s_ap: e.tensor_tensor(out=s_ap, in0=y_ap, in1=s_ap, op=ALU.mult),
                              reads=[("PB", 1)], writes=SK(i, pk))
                        S.add("dve", lambda e, s_ap=s_ap, m_ap=m_ap, o_ap=o_ap: e.tensor_tensor(out=o_ap, in0=s_ap, in1=m_ap, op=ALU.add),
                              reads=SK(i, pk) + [("m1h", cc, pk)], writes=[("merged", oc, pk)])
            dbg_dump("merged", merged[:, :, :], [("merged", c, pk) for c in range(KC) for pk in parts], is_bf16=True)
            mgk = [("merged", c, pk) for c in range(KC) for pk in parts]
            S.fence(["vsq"], ["m1h", "upb", "tmb", "conv", "sstage", "scprod", "yst", "vsq"])
            for half in range(2):
                s_o = wblock(lambda s, half=half: [(slot_k(s, KC, 512), wsrc(w_out, KC, half * 512, 512))])
                for cc in range(4):
                    oc = half * 4 + cc
                    sv = slot_k(s_o, KC, 512)
                    b = oc % 2
                    mm_group(b, parts, lambda k, sv=sv, cc=cc: sv[:, k, cc * 128:(cc + 1) * 128], lambda k, c0, n: merged[:, k, c0:c0 + n], KC,
                             reads=[("wslot", s_o)] + mgk)
                    resid_add(parts, b, oc, 5)
            layer_norm(parts, lambda c: xT[:, c, :], xT[:, :, PWID:NTC], lambda c, pk: [("xT", c, pk)], EPS_DN, "ln2_g",
                       *ln_outs_resid("ln2_b", 6, True), next_af=AF.Silu, pre=("p" in parts))
            dbg_dump("x2", xT[:, :, :], [("xT", c, pk) for c in range(KC) for pk in parts])

        def phase_out(parts, row0):
            S.fence(["yst"], R2N)
            S.fence(["bank"], ["PB"])
            n = 0
            for pk in parts:
                if pk == "p":
                    for r in range(8):
                        bp = (n % 3) * 2
                        sl = n % 4
                        n += 1

                        def tr(e, r=r, bp=bp):
                            ins = None
                            for c in range(KC):
                                ins = e.transpose(ps[:, bp + c // 4, (c % 4) * 128:(c % 4 + 1) * 128], xT[:, c, r * 128:(r + 1) * 128], ident[:, :])
                            return ins
                        S.add("pe", tr, reads=[("xT", c, "p") for c in range(KC)] + [("const",)],
                              writes=[("bank", bp), ("bank", bp + 1)])
                        if r % 2 == 0:
                            S.add("act", lambda e, bp=bp, sl=sl: e.activation(out=P3(yst[:, sl, :]), in_=ps[:, bp:bp + 2, :], func=AF.Copy),
                                  reads=[("bank", bp), ("bank", bp + 1)], writes=[("yst", sl)])
                        else:
                            S.add("dve", lambda e, bp=bp, sl=sl: e.tensor_copy(out=P3(yst[:, sl, :]), in_=ps[:, bp:bp + 2, :]),
                                  reads=[("bank", bp), ("bank", bp + 1)], writes=[("yst", sl)])
                        dma_op("sp", "o_y%d" % sl, [(y_p[row0 + r * 128: row0 + (r + 1) * 128, :], yst[:, sl, :])], reads=[("yst", sl)], out=True)
                else:
                    def tr(e):
                        ins = None
                        for c in range(KC):
                            ins = e.matmul(ps[0:16, 6 + c // 4, (c % 4) * 128:(c % 4 + 1) * 128], xT[:, c, PWID:NTC], ident[:, :], start=True, stop=True)
                        return ins
                    S.add("pe", tr, reads=[("xT", c, "s") for c in range(KC)] + [("const",)], writes=[("bank", 6), ("bank", 7)])
                    S.add("act", lambda e: e.activation(out=P3(scr[1][0:16, 0:D]), in_=ps[0:16, 6:8, :], func=AF.Copy),
                          reads=[("bank", 6), ("bank", 7)], writes=SCR(1))
                    dma_op("sp", "o_ys", [(y_s[:, :], scr[1][0:16, 0:D])], reads=SCR(1), out=True)
            S.fence(["PB"], ["bank"])

        def phase_final_states():
            def tr(e):
                ins = None
                for c in range(KC):
                    ins = e.matmul(ps[0:30, 6 + c // 4, (c % 4) * 128:(c % 4 + 1) * 128], gl_halo[:, c, :], ident[:, :], start=True, stop=True)
                return ins
            S.add("pe", tr, reads=[("gl_halo",), ("const",)], writes=[("bank", 6), ("bank", 7)])
            S.add("act", lambda e: e.activation(out=P3(scr[0][0:30, 0:D]), in_=ps[0:30, 6:8, :], func=AF.Copy),
                  reads=[("bank", 6), ("bank", 7)], writes=SCR(0))
            dma_op("sp", "o_fs", [(ncp[:, :], scr[0][0:30, 0:D])], reads=SCR(0), out=True)

            def tr2(e):
                ins = None
                for c in range(KC):
                    ins = e.matmul(ps[0:15, 6 + c // 4, (c % 4) * 128:(c % 4 + 1) * 128], up_halo[:, c, :], ident[:, :], start=True, stop=True)
                return ins
            S.add("pe", tr2, reads=[("up_halo", c) for c in range(KC)] + [("const",)], writes=[("bank", 6), ("bank", 7)])
            S.add("act", lambda e: e.activation(out=P3(scr[1][0:15, 0:D]), in_=ps[0:15, 6:8, :], func=AF.Copy),
                  reads=[("bank", 6), ("bank", 7)], writes=SCR(1))
            dma_op("sp", "o_fs", [(npp[:, :], scr[1][0:15, 0:D])], reads=SCR(1), out=True)

        stop = None
        stop_g = 0
        if debug and ":" in debug:
            parts_ = debug.split(":")
            debug, stop = parts_[0], parts_[1]
            if len(parts_) > 2:
                stop_g = int(parts_[2])
        phase_setup()
        prow = 0
        pg = 0
        x_load(groups_cfg[0][1], 0)
        for gi, (gname, parts) in enumerate(groups_cfg):
            has_p = "p" in parts
            phase_x(parts, prow)
            if stop == "x" and gi == stop_g:
                break
            phase_ffn(parts, 0, 2, "ln1_g", "ln1_b", 3, True, "1")
            if stop == "ffn1" and gi == stop_g:
                break
            phase_mixer(parts, first_prompt=(pg == 0))
            if stop == "mixer" and gi == stop_g:
                break
            phase_ffn(parts, 1, 8, "ln3_g", "ln3_b", 0, False, "3")
            if stop == "ffn3" and gi == stop_g:
                break
            if gi + 1 < len(groups_cfg):
                x_load(groups_cfg[gi + 1][1], prow + (PWID if has_p else 0))
            phase_out(parts, prow)
            if stop == "out" and gi == stop_g:
                break
            if has_p:
                prow += PWID
                pg += 1
        if stop is None:
            phase_final_states()

        final = S.add("sp", lambda e: None, extra=OUT_OPS)
        S.finalize()
        prog_sems = {k: es.enter_context(nc.semaphore("prog_" + k)) for k in Sched.ENGS}
        for n in sem_names:
            SEMS[n] = es.enter_context(nc.semaphore("d_" + n))
        with nc.Block() as block:
            @block.tensor
            def _(e):
                S.emit("pe", e, prog_sems, SEMS)

            @block.scalar
            def _(e):
                S.emit("act", e, prog_sems, SEMS)

            @block.vector
            def _(e):
                S.emit("dve", e, prog_sems, SEMS)

            @block.gpsimd
            def _(e):
                S.emit("pool", e, prog_sems, SEMS)

            @block.sync
            def _(e):
                S.ops["sp"].remove(final)
                S.emit("sp", e, prog_sems, SEMS)
                for d in final.deps:
                    e.wait_ge(SEMS[d.dma_sem], S.dma_counts[d.dma_sem])
    return nc


_NC_CACHE = {}


def _consts():
    ident = np.eye(128, dtype=np.float32)
    selc = np.zeros((120, 4, 16), np.float32)
    for q in range(4):
        for s in range(4):
            selc[s * 30:(s + 1) * 30, q, 4 * q + s] = 1.0
    selp = np.zeros((120, 2, 4, 16), np.float32)
    for q in range(2):
        for g, w in enumerate(POOL_W):
            for s in range(8):
                for k in range(15):
                    if k >= 16 - w:
                        selp[s * 15 + k, q, g, 8 * q + s] = 1.0 / w
    invc = np.zeros((128, 4, 16), np.float32)
    for g, w in enumerate(POOL_W):
        for t in range(16):
            invc[:, g, t] = 1.0 / min(w, t + 1)
    return ident, selc, selp, invc


def make_in_maps(x_prompt, x_sample, state_conv, state_pool, c_prompt, c_sample,
                 w_ada, b_ada, ffn1_w_in, ffn1_w_out, ln1_g, ln1_b,
                 w_in, conv_w, conv_b, conv_ln_g, conv_ln_b, w_conv_out,
                 pool_w, pool_scale, w_pool_out, w_out, ln2_g, ln2_b,
                 ffn2_w_in, ffn2_w_out, ln3_g, ln3_b):
    f = lambda a: np.ascontiguousarray(np.asarray(a, dtype=np.float32))
    ident, selc, selp, invc = _consts()
    pvec = np.concatenate([f(v)[0].reshape(8, 128) for v in
                           (ln1_g, ln1_b, ln2_g, ln2_b, ln3_g, ln3_b, conv_b, conv_ln_g, conv_ln_b, pool_scale)], axis=0)
    cb_row = np.stack([f(conv_w)[0, 30], f(conv_b)[0]], axis=0)
    shared = {
        "w_ada": f(w_ada)[0], "b_ada": f(b_ada)[0].reshape(72, 128),
        "ffn1_w_in": f(ffn1_w_in)[0], "ffn2_w_in": f(ffn2_w_in)[0],
        "ffn1_w_out": f(ffn1_w_out)[0], "ffn2_w_out": f(ffn2_w_out)[0],
        "w_in": f(w_in)[0], "conv_w": f(conv_w)[0], "w_conv_out": f(w_conv_out)[0],
        "pool_w": f(pool_w)[0], "w_pool_out": f(w_pool_out)[0], "w_out": f(w_out)[0],
        "pvec": np.ascontiguousarray(pvec), "cb_row": np.ascontiguousarray(cb_row),
        "ident": ident, "selc": selc, "selp": selp, "invc": invc,
    }
    xp, xs = f(x_prompt), f(x_sample)
    sc, sp = f(state_conv)[0], f(state_pool)[0]
    cp, cs = f(c_prompt), f(c_sample)
    in_maps = []
    for i in range(8):
        m = dict(shared)
        m["x_p"] = xp[i]
        m["x_s"] = np.ascontiguousarray(xs[16 * i:16 * i + 16, 0, :])
        m["sconv"] = np.ascontiguousarray(sc[16 * i:16 * i + 16])
        m["spool"] = np.ascontiguousarray(sp[16 * i:16 * i + 16])
        m["c_all"] = np.ascontiguousarray(np.concatenate([cp[i:i + 1], cs[16 * i:16 * i + 16]], axis=0))
        in_maps.append(m)
    return in_maps


def kernel(**inputs):
    if "nc" not in _NC_CACHE:
        _NC_CACHE["nc"] = build_program()
    nc = _NC_CACHE["nc"]
    in_maps = make_in_maps(**inputs)
    res = run_bass_kernel_spmd(nc, in_maps, core_ids=list(range(8)))
    R = res.results
    y_prompt = np.stack([R[i]["y_p"] for i in range(8)], axis=0)
    y_sample = np.concatenate([R[i]["y_s"] for i in range(8)], axis=0)[:, None, :]
    ncp = np.stack([R[i]["ncp"] for i in range(8)], axis=0)[None]
    npp = np.stack([R[i]["npp"] for i in range(8)], axis=0)[None]
    ncs = np.concatenate([R[i]["ncs"] for i in range(8)], axis=0)[None]
    nps = np.concatenate([R[i]["nps"] for i in range(8)], axis=0)[None]
    return (y_prompt.astype(np.float32), y_sample.astype(np.float32), ncp.astype(np.float32),
            npp.astype(np.float32), ncs.astype(np.float32), nps.astype(np.float32))
```
